# Optimizing a Trainium2 kernel written in Bass

```python
import jax, jax.numpy as jnp
from jax import lax
import numpy as np

D_MODEL = 1024
BATCH = 4
SEQ = 4096
DEPTH = 4

N_MIXERS = 2
N_MLA = (DEPTH + 1) // 2
N_HGRN = DEPTH // 2

MLA_HEADS = 8
MLA_Q_LORA = 512
MLA_KV_LORA = 256
MLA_NOPE = 128
MLA_ROPE = 64
MLA_V = 128
ROPE_BASE = 10000.0
ATTN_BLOCK = 128

HGRN_HEADS = 8
HGRN_DK = D_MODEL // HGRN_HEADS
HGRN_DV = D_MODEL // HGRN_HEADS
HGRN_CHUNK = 32

D_FF = 4 * D_MODEL

EPS = 1e-6

kernel_name = "hybrid_mla_hgrn2_sqrelu_sandwich"


def rms_norm(x, g):
    xf = x.astype(jnp.float32)
    y = xf * lax.rsqrt(jnp.mean(xf * xf, axis=-1, keepdims=True) + EPS)
    return (y * g.astype(jnp.float32)).astype(x.dtype)


def rope_cos_sin(positions):
    inv_freq = jnp.power(ROPE_BASE, -jnp.arange(0, MLA_ROPE, 2, dtype=jnp.float32) / MLA_ROPE)
    ang = positions.astype(jnp.float32)[..., None] * inv_freq
    return jnp.cos(ang), jnp.sin(ang)


def apply_rope(t, cos, sin):
    tf = t.astype(jnp.float32)
    t1, t2 = jnp.split(tf, 2, axis=-1)
    return jnp.concatenate([t1 * cos - t2 * sin, t1 * sin + t2 * cos], axis=-1).astype(t.dtype)


def mla_mixer(h, cos, sin, w_in, q_norm, kv_norm, w_uq, w_ukv, w_o):
    B, S, _ = h.shape
    H = MLA_HEADS
    proj = h @ w_in
    c_q, c_kv, k_r = jnp.split(proj, [MLA_Q_LORA, MLA_Q_LORA + MLA_KV_LORA], axis=-1)
    q = (rms_norm(c_q, q_norm) @ w_uq).reshape(B, S, H, MLA_NOPE + MLA_ROPE)
    q_nope = q[..., :MLA_NOPE]
    q_rope = apply_rope(q[..., MLA_NOPE:], cos[:, :, None, :], sin[:, :, None, :])
    kv = (rms_norm(c_kv, kv_norm) @ w_ukv).reshape(B, S, H, MLA_NOPE + MLA_V)
    k_nope, v = kv[..., :MLA_NOPE], kv[..., MLA_NOPE:]
    k_rope = apply_rope(k_r, cos, sin)
    scale = (MLA_NOPE + MLA_ROPE) ** -0.5
    nb = S // ATTN_BLOCK
    qn_b = q_nope.reshape(B, nb, ATTN_BLOCK, H, MLA_NOPE).transpose(1, 0, 2, 3, 4)
    qr_b = q_rope.reshape(B, nb, ATTN_BLOCK, H, MLA_ROPE).transpose(1, 0, 2, 3, 4)
    k_pos = jnp.arange(S)

    def block(args):
        qn, qr, blk = args
        s = (jnp.einsum('bqhd,bkhd->bhqk', qn, k_nope)
             + jnp.einsum('bqhr,bkr->bhqk', qr, k_rope)).astype(jnp.float32) * scale
        q_pos = blk * ATTN_BLOCK + jnp.arange(ATTN_BLOCK)
        s = jnp.where(q_pos[:, None] >= k_pos[None, :], s, -jnp.inf)
        p = jax.nn.softmax(s, axis=-1).astype(v.dtype)
        return jnp.einsum('bhqk,bkhd->bqhd', p, v)

    o = lax.map(block, (qn_b, qr_b, jnp.arange(nb)))
    o = o.transpose(1, 0, 2, 3, 4).reshape(B, S, H * MLA_V)
    return o @ w_o


def hgrn2_mixer(h, lb, w_in, o_norm, w_o):
    B, S, _ = h.shape
    H, DK, DV, C = HGRN_HEADS, HGRN_DK, HGRN_DV, HGRN_CHUNK
    nc = S // C
    HK, HV = H * DK, H * DV
    proj = h @ w_in
    q_x, f_x, i_x, g_x = jnp.split(proj, [HK, 2 * HK, 2 * HK + HV], axis=-1)

    def heads(t, d):
        return t.astype(jnp.float32).reshape(B, nc, C, H, d).transpose(0, 3, 1, 2, 4)

    f = lb + (1.0 - lb) * jax.nn.sigmoid(f_x.astype(jnp.float32))
    q = heads(jax.nn.silu(q_x.astype(jnp.float32)), DK)
    k = heads(1.0 - f, DK)
    log_f = heads(jnp.log(f), DK)
    v = heads(i_x, DV)

    b = jnp.cumsum(log_f, axis=3)
    b_ref = b[:, :, :, C // 2:C // 2 + 1, :]
    b_last = b[:, :, :, -1:, :]
    q_rel = q * jnp.exp(b - b_ref)
    k_rel = k * jnp.exp(b_ref - b)
    causal = jnp.tril(jnp.ones((C, C), dtype=bool))
    a = jnp.where(causal, jnp.einsum('bhncd,bhnsd->bhncs', q_rel, k_rel), 0.0)
    o_intra = jnp.einsum('bhncs,bhnse->bhnce', a, v)

    q_dec = q * jnp.exp(b)
    k_dec = k * jnp.exp(b_last - b)
    chunk_decay = jnp.exp(b_last[:, :, :, 0, :])

    def step(state, xs):
        qd, kd, vc, dec = xs
        o_inter = jnp.einsum('bhcd,bhde->bhce', qd, state)
        state = dec[..., None] * state + jnp.einsum('bhcd,bhce->bhde', kd, vc)
        return state, o_inter

    mv = lambda t: jnp.moveaxis(t, 2, 0)
    s0 = jnp.zeros((B, H, DK, DV), jnp.float32)
    _, o_inter = lax.scan(step, s0, (mv(q_dec), mv(k_dec), mv(v), mv(chunk_decay)))
    o = o_intra + jnp.moveaxis(o_inter, 0, 2)
    o = o.transpose(0, 2, 3, 1, 4).reshape(B, S, H, DV)
    gate = jax.nn.silu(g_x.astype(jnp.float32)).reshape(B, S, H, DV)
    o = rms_norm(o, o_norm) * gate
    return o.reshape(B, S, HV).astype(h.dtype) @ w_o


def sq_relu_mlp(h, w1, w2):
    a = jax.nn.relu(h @ w1)
    return (a * a) @ w2


def setup_inputs(seed: int = 0) -> dict:
    key = jax.random.key(seed)
    ks = jax.random.split(key, 16)
    f32 = jnp.float32
    nrm = lambda k, shape, fan_in: jax.random.normal(k, shape, f32) * (fan_in ** -0.5)
    mla_in_w = MLA_Q_LORA + MLA_KV_LORA + MLA_ROPE
    hgrn_in_w = 3 * HGRN_HEADS * HGRN_DK + HGRN_HEADS * HGRN_DV
    return {
        "x": jax.random.normal(ks[0], (BATCH, SEQ, D_MODEL), f32),
        "positions": jnp.broadcast_to(jnp.arange(SEQ, dtype=jnp.int32), (BATCH, SEQ)),
        "norm_gains": 1.0 + 0.1 * jax.random.normal(ks[1], (DEPTH, 4, D_MODEL), f32),
        "mla_w_in": nrm(ks[2], (N_MLA, D_MODEL, mla_in_w), D_MODEL),
        "mla_q_norm": 1.0 + 0.1 * jax.random.normal(ks[3], (N_MLA, MLA_Q_LORA), f32),
        "mla_kv_norm": 1.0 + 0.1 * jax.random.normal(ks[4], (N_MLA, MLA_KV_LORA), f32),
        "mla_w_uq": nrm(ks[5], (N_MLA, MLA_Q_LORA, MLA_HEADS * (MLA_NOPE + MLA_ROPE)), MLA_Q_LORA),
        "mla_w_ukv": nrm(ks[6], (N_MLA, MLA_KV_LORA, MLA_HEADS * (MLA_NOPE + MLA_V)), MLA_KV_LORA),
        "mla_w_o": nrm(ks[7], (N_MLA, MLA_HEADS * MLA_V, D_MODEL), MLA_HEADS * MLA_V),
        "hgrn_w_in": nrm(ks[8], (N_HGRN, D_MODEL, hgrn_in_w), D_MODEL),
        "hgrn_lb_logits": 0.1 * jax.random.normal(ks[9], (DEPTH, HGRN_HEADS * HGRN_DK), f32),
        "hgrn_o_norm": 1.0 + 0.1 * jax.random.normal(ks[10], (N_HGRN, HGRN_DV), f32),
        "hgrn_w_o": nrm(ks[11], (N_HGRN, HGRN_HEADS * HGRN_DV, D_MODEL), HGRN_HEADS * HGRN_DV),
        "mlp_w1": nrm(ks[12], (DEPTH, D_MODEL, D_FF), D_MODEL),
        "mlp_w2": nrm(ks[13], (DEPTH, D_FF, D_MODEL), D_FF),
    }


def reference(x, positions, norm_gains, mla_w_in, mla_q_norm, mla_kv_norm, mla_w_uq, mla_w_ukv,
              mla_w_o, hgrn_w_in, hgrn_lb_logits, hgrn_o_norm, hgrn_w_o, mlp_w1, mlp_w2):
    cos, sin = rope_cos_sin(positions)
    p = jax.nn.softmax(hgrn_lb_logits.astype(jnp.float32), axis=0)
    lower_bounds = jnp.cumsum(p, axis=0) - p[0]
    h = x
    for layer in range(DEPTH):
        slot = layer // N_MIXERS
        a = rms_norm(h, norm_gains[layer, 0])
        if layer % N_MIXERS == 0:
            m = mla_mixer(a, cos, sin, mla_w_in[slot], mla_q_norm[slot], mla_kv_norm[slot],
                          mla_w_uq[slot], mla_w_ukv[slot], mla_w_o[slot])
        else:
            m = hgrn2_mixer(a, lower_bounds[layer], hgrn_w_in[slot], hgrn_o_norm[slot], hgrn_w_o[slot])
        h = h + rms_norm(m, norm_gains[layer, 1])
        a = rms_norm(h, norm_gains[layer, 2])
        h = h + rms_norm(sq_relu_mlp(a, mlp_w1[layer], mlp_w2[layer]), norm_gains[layer, 3])
    return h
```

```python
import os
import numpy as np
import concourse.bass as bass
import concourse.mybir as mybir
from concourse.bass_utils import run_bass_kernel_spmd

F32 = mybir.dt.float32
BF16 = mybir.dt.bfloat16
I32 = mybir.dt.int32
AF = mybir.ActivationFunctionType
ALU = mybir.AluOpType

D = 1024
T = 2048
TT = 512
NT = T // TT
KC = D // 128
DEPTH = 4
EPS = 1e-6
ENGS = ["pe", "act", "dve", "pool", "sp"]
SEM_LIM = 30000


class Prog:
    def __init__(self, nc):
        self.nc = nc
        self.ops = {e: [] for e in ENGS}
        self.last_w = {}
        self.readers = {}
        self.marked = set()
        self.dcount = {}
        self.last_compute = {e: None for e in ENGS}

    def _add(self, eng, fn, r, w, dma_sem=None):
        deps = set()
        for k in r:
            t = self.last_w.get(k)
            if t is not None:
                deps.add(t)
        for k in w:
            t = self.last_w.get(k)
            if t is not None:
                deps.add(t)
            rd = self.readers.get(k)
            if rd:
                deps.update(rd.values())
        idx = len(self.ops[eng])
        if dma_sem is None:
            tok = ("c", eng, idx)
            if eng == "pe":
                deps = {d for d in deps if not (d[0] == "c" and d[1] == eng)}
            self.last_compute[eng] = idx
        else:
            self.dcount[dma_sem] = self.dcount.get(dma_sem, 0) + 16
            tok = ("d", dma_sem, self.dcount[dma_sem])
            deps = {d for d in deps if not (d[0] == "d" and d[1] == dma_sem)}
        for d in deps:
            if d[0] == "c":
                self.marked.add((d[1], d[2]))
        self.ops[eng].append((fn, deps, tok))
        for k in w:
            self.last_w[k] = tok
            self.readers[k] = {}
        for k in r:
            rd = self.readers.setdefault(k, {})
            rd[tok if tok[0] == "d" else eng] = tok
        return tok

    def matmul(self, out, lhsT, rhs, start, stop, r, w):
        self._add("pe", lambda e: e.matmul(out, lhsT=lhsT, rhs=rhs, start=start, stop=stop), r, w)

    def transpose(self, out, in_, ident, r, w):
        self._add("pe", lambda e: e.transpose(out, in_, ident), r, w)

    def act(self, out, in_, func, r, w, bias=None, scale=None, eng="act"):
        kw = {}
        if bias is not None:
            kw["bias"] = bias
        if scale is not None:
            kw["scale"] = scale
        self._add("act", lambda e: e.activation(out=out, in_=in_, func=func, **kw), r, w)

    def ts(self, eng, out, in0, s1, s2, op0, op1, r, w):
        if op1 is None:
            self._add(eng, lambda e: e.tensor_scalar(out=out, in0=in0, scalar1=s1, scalar2=None, op0=op0), r, w)
        else:
            self._add(eng, lambda e: e.tensor_scalar(out=out, in0=in0, scalar1=s1, scalar2=s2, op0=op0, op1=op1), r, w)

    def tt(self, eng, out, in0, in1, op, r, w):
        self._add(eng, lambda e: e.tensor_tensor(out=out, in0=in0, in1=in1, op=op), r, w)

    def stt(self, eng, out, in0, scalar, in1, op0, op1, r, w):
        self._add(eng, lambda e: e.scalar_tensor_tensor(out=out, in0=in0, scalar=scalar, in1=in1, op0=op0, op1=op1), r, w)

    def copy(self, eng, out, in_, r, w):
        if eng == "act":
            self._add(eng, lambda e: e.copy(out=out, in_=in_), r, w)
        else:
            self._add(eng, lambda e: e.tensor_copy(out=out, in_=in_), r, w)

    def memset(self, eng, ap, val, w):
        self._add(eng, lambda e: e.memset(ap, val), [], w)

    def scan(self, eng, out, d0, d1, init, op0, op1, r, w):
        self._add(eng, lambda e: e.tensor_tensor_scan(out=out, data0=d0, data1=d1, initial=init, op0=op0, op1=op1), r, w)

    def recip(self, out, in_, r, w):
        self._add("dve", lambda e: e.reciprocal(out=out, in_=in_), r, w)

    def dma(self, q, out, in_, sem, r=(), w=()):
        return self._add(q, lambda e: e.dma_start(out=out, in_=in_), list(r), list(w), dma_sem=sem)

    def collective(self, kind, groups, in_ap, out_ap, sem, r, w):
        def fn(e):
            return e.collective_compute(kind, ALU.bypass, replica_groups=groups, ins=[in_ap], outs=[out_ap])
        eng = "pool"
        deps = set()
        for k in r:
            t = self.last_w.get(k)
            if t is not None:
                deps.add(t)
        for k in w:
            t = self.last_w.get(k)
            if t is not None:
                deps.add(t)
            rd = self.readers.get(k)
            if rd:
                deps.update(rd.values())
        self.dcount[sem] = self.dcount.get(sem, 0) + 1
        tok = ("d", sem, self.dcount[sem])
        for d in deps:
            if d[0] == "c":
                self.marked.add((d[1], d[2]))
        self.ops[eng].append((fn, deps, ("cc", sem, self.dcount[sem])))
        for k in w:
            self.last_w[k] = tok
            self.readers[k] = {}
        for k in r:
            self.readers.setdefault(k, {})[tok] = tok
        return tok

    def barrier(self):
        toks = []
        for e in ENGS:
            i = self.last_compute[e]
            if i is not None:
                toks.append(("c", e, i))
                self.marked.add((e, i))
        for e in ENGS:
            deps = {t for t in toks if t[1] != e}
            if deps:
                self.ops[e].append((None, deps, ("c", e, len(self.ops[e]))))
        self.last_w = {k: t for k, t in self.last_w.items() if t[0] == "d"}
        newr = {}
        for k, rd in self.readers.items():
            rd2 = {a: t for a, t in rd.items() if t[0] == "d"}
            if rd2:
                newr[k] = rd2
        self.readers = newr

    def wait_all_dma(self, eng="sp"):
        deps = {("d", s, c) for s, c in self.dcount.items()}
        self.ops[eng].append((None, deps, ("c", eng, len(self.ops[eng]))))

    def emit(self):
        nc = self.nc
        ordinal = {}
        nmarked = {}
        for e in ENGS:
            c = 0
            for i in range(len(self.ops[e])):
                if (e, i) in self.marked:
                    c += 1
                    ordinal[(e, i)] = c
            nmarked[e] = c
        esem = {}
        for e in ENGS:
            nep = (nmarked[e] + SEM_LIM - 1) // SEM_LIM
            esem[e] = [nc.alloc_semaphore(f"s_{e}_{j}") for j in range(max(nep, 1))]
        dsem = {name: nc.alloc_semaphore(f"d_{name}") for name in self.dcount}

        def run(eng_name, e):
            waited_c = {}
            waited_d = {}
            for i, (fn, deps, tok) in enumerate(self.ops[eng_name]):
                for d in sorted(deps, key=lambda x: (x[0], str(x[1]), x[2])):
                    if d[0] == "c":
                        o = ordinal[(d[1], d[2])]
                        if waited_c.get(d[1], 0) >= o:
                            continue
                        waited_c[d[1]] = o
                        ep = (o - 1) // SEM_LIM
                        e.wait_ge(esem[d[1]][ep], o - ep * SEM_LIM)
                    else:
                        if waited_d.get(d[1], 0) >= d[2]:
                            continue
                        waited_d[d[1]] = d[2]
                        e.wait_ge(dsem[d[1]], d[2])
                if fn is None:
                    continue
                ins = fn(e)
                if tok[0] == "d":
                    ins.then_inc(dsem[tok[1]], 16)
                elif tok[0] == "cc":
                    ins.then_inc(dsem[tok[1]])
                elif (eng_name, i) in self.marked:
                    o = ordinal[(eng_name, i)]
                    ep = (o - 1) // SEM_LIM
                    ins.then_inc(esem[eng_name][ep], 1)

        with nc.Block() as block:
            @block.tensor
            def _(e):
                run("pe", e)

            @block.scalar
            def _(e):
                run("act", e)

            @block.vector
            def _(e):
                run("dve", e)

            @block.gpsimd
            def _(e):
                run("pool", e)

            @block.sync
            def _(e):
                run("sp", e)


def build_program(plan, in_name="xT"):
    nc = bass.Bass("TRN2", target_bir_lowering=False)
    P = Prog(nc)

    def din(name, shape, dt=F32):
        return nc.dram_tensor(name, shape, dt, kind="ExternalInput").ap()

    xT_d = din("xT", [D, T])
    outT_d = nc.dram_tensor("outT", [D, T], F32, kind="ExternalOutput").ap()
    gains_d = din("gains", [128, 128])
    ctab_d = din("ctab", [128, 16])
    w1r_d = din("w1r", [DEPTH, 16, 128, 8 * 256])
    w2r_d = din("w2r", [DEPTH, 8, 128, 32 * 128])
    posb_d = din("posb", [64, T], I32)
    masks_d = din("masks", [128, 4 * 512])
    mla_win_d = din("mla_win", [2, 128, 8 * 896])
    mla_g_d = din("mla_g", [128, 16])
    mla_wh_d = din("mla_wh", [2, 8, 128, 1536])
    mla_wo_d = din("mla_wo", [2, 128, 8 * 1024])
    hg_win_d = din("hg_win", [2, 32, 128, 8 * 128])
    hg_wo_d = din("hg_wo", [2, 128, 8 * 1024])
    hg_c_d = din("hg_c", [128, 64])
    cmask_d = din("cmask", [128, 128 + 128])
    sin_d = [nc.dram_tensor(f"sin{i}", [128, 1024], F32).ap() for i in range(2)]
    sout_d = [nc.dram_tensor(f"sout{i}", [256, 1024], F32).ap() for i in range(2)]
    xin_d = [nc.dram_tensor(f"xin{i}", [320, T], BF16).ap() for i in range(2)]
    xout_d = [nc.dram_tensor(f"xout{i}", [640, T], BF16).ap() for i in range(2)]
    PAIRS = [[0, 1], [2, 3], [4, 5], [6, 7]]

    hT = nc.alloc_sbuf_tensor("hT", [128, KC, T], F32)
    gains = nc.alloc_sbuf_tensor("gains_sb", [128, 128], F32)
    ctab = nc.alloc_sbuf_tensor("ctab_sb", [128, 16], F32)
    ones_f = nc.alloc_sbuf_tensor("ones_f", [128, 128], F32)
    mla_g = nc.alloc_sbuf_tensor("mla_g_sb", [128, 16], F32)
    hg_c = nc.alloc_sbuf_tensor("hg_c_sb", [128, 64], F32)
    lbt = nc.alloc_sbuf_tensor("lbt", [128, 64], F32)
    cmask_f = nc.alloc_sbuf_tensor("cmask_f", [128, 256], F32)
    cmask = nc.alloc_sbuf_tensor("cmask_b", [128, 256], BF16)
    ones = {}
    for nm in (1024, 512, 256, 128, 1):
        ones[nm] = nc.alloc_sbuf_tensor(f"ones{nm}", [128, 128], BF16)
    ARENA_W = 35500
    arena = nc.alloc_sbuf_tensor("arena", [128, ARENA_W], F32)
    PS = nc.alloc_psum_tensor("PS", [128, 8, 512], F32)

    def carve(off_w, n_elem, dt):
        nw = n_elem if dt in (F32, I32) else (n_elem + 1) // 2
        ap = arena[:, off_w:off_w + nw]
        if dt != F32:
            ap = ap.bitcast(dt)
        return ap

    P.dma("sp", out=gains[:], in_=gains_d, sem="c0", w=["gains"])
    P.dma("sp", out=ctab[:], in_=ctab_d, sem="c1", w=["ctab"])
    P.dma("sp", out=mla_g[:], in_=mla_g_d, sem="c2", w=["mla_g"])
    P.dma("sp", out=hg_c[:], in_=hg_c_d, sem="c3", w=["hg_c"])
    P.dma("sp", out=cmask_f[:], in_=cmask_d, sem="c4", w=["cmask_f"])
    P.copy("dve", cmask[:], cmask_f[:], r=["cmask_f"], w=["cmask"])
    lg = hg_c[:, 0:32].rearrange("p (l h) -> p l h", l=4)
    mx = lbt[:, 48:56]
    P.tt("dve", mx, lg[:, 0, :], lg[:, 1, :], ALU.max, r=["hg_c"], w=["lb_mx"])
    P.tt("dve", mx, mx, lg[:, 2, :], ALU.max, r=["hg_c", "lb_mx"], w=["lb_mx"])
    P.tt("dve", mx, mx, lg[:, 3, :], ALU.max, r=["hg_c", "lb_mx"], w=["lb_mx"])
    ex = cmask_f[:, 0:32].rearrange("p (l h) -> p l h", l=4)
    for li_ in range(4):
        P.tt("dve", ex[:, li_, :], lg[:, li_, :], mx, ALU.subtract, r=["hg_c", "lb_mx", "cmask"], w=[("lb_ex", li_)])
    P.act(cmask_f[:, 0:32], cmask_f[:, 0:32], AF.Exp, r=[("lb_ex", i) for i in range(4)], w=["lb_e"])
    sm = lbt[:, 56:64]
    P.tt("dve", sm, ex[:, 0, :], ex[:, 1, :], ALU.add, r=["lb_e"], w=["lb_sm"])
    P.tt("dve", sm, sm, ex[:, 2, :], ALU.add, r=["lb_e", "lb_sm"], w=["lb_sm"])
    P.tt("dve", sm, sm, ex[:, 3, :], ALU.add, r=["lb_e", "lb_sm"], w=["lb_sm"])
    P.recip(sm, sm, r=["lb_sm"], w=["lb_sm"])
    P.tt("dve", lbt[:, 0:8], ex[:, 1, :], sm, ALU.mult, r=["lb_e", "lb_sm"], w=["lbt0"])
    P.tt("dve", lbt[:, 8:16], ex[:, 2, :], ex[:, 3, :], ALU.add, r=["lb_e"], w=["lbt1"])
    P.tt("dve", lbt[:, 8:16], lbt[:, 8:16], sm, ALU.mult, r=["lbt1", "lb_sm"], w=["lbt1"])
    P.tt("dve", lbt[:, 8:16], lbt[:, 8:16], lbt[:, 0:8], ALU.add, r=["lbt1", "lbt0"], w=["lbt1"])
    P.ts("dve", lbt[:, 16:32], lbt[:, 0:16], -1.0, 1.0, ALU.mult, ALU.add, r=["lbt0", "lbt1"], w=["oml"])
    P.ts("dve", lbt[:, 32:48], lbt[:, 16:32], -1.0, None, ALU.mult, None, r=["oml"], w=["noml"])
    for k in range(KC):
        P.dma("sp", out=hT[:, k, :], in_=xT_d[k * 128:(k + 1) * 128, :], sem="xin", w=[("h", kk, t) for kk in range(KC) for t in range(NT)])
    for nm in ones:
        P.memset("pool", ones_f[:], 1.0 / nm, w=["ones_f"])
        P.copy("pool", ones[nm][:], ones_f[:], r=["ones_f"], w=[("ones", nm)])

    state = {"ps": 0}

    def next_ps(lo=0, hi=6):
        b = lo + state["ps"] % (hi - lo)
        state["ps"] += 1
        return b

    def rstd_from(ss_bank, rstd_ap, rstd_key):
        P.ts("dve", rstd_ap, PS[:, ss_bank, :], EPS, None, ALU.add, None, r=[("ps", ss_bank)], w=[rstd_key])
        P.act(rstd_ap, rstd_ap, AF.Sqrt, r=[rstd_key], w=[rstd_key])
        P.recip(rstd_ap, rstd_ap, r=[rstd_key], w=[rstd_key])

    def norm_pre(tt, gi, dst, dst_key, sq, rstd, ss_bank):
        tok = slice(tt * TT, (tt + 1) * TT)
        for k in range(KC):
            s = sq[k % len(sq)]
            P.act(s[0], hT[:, k, tok], AF.Square, r=[("h", k, tt)], w=[s[1]])
            P.matmul(PS[:, ss_bank, :], ones[1024][:], s[0], k == 0, k == KC - 1, r=[s[1], ("ones", 1024)], w=[("ps", ss_bank)])
        rstd_from(ss_bank, rstd[0], rstd[1])
        for k in range(KC):
            P.stt("dve", dst[:, k, :], hT[:, k, tok], gains[:, gi * 8 + k:gi * 8 + k + 1], rstd[0], ALU.mult, ALU.mult,
                  r=[("h", k, tt), "gains", rstd[1]], w=[(dst_key, k)])

    def mlp_layer(l):
        P.barrier()
        o = 0
        actT = carve(o, 32 * 1024, BF16).rearrange("p (f t) -> p f t", f=32); o += 16384
        mT = carve(o, 2 * 8 * 512, F32).rearrange("p (a k t) -> p a k t", a=2, k=8); o += 8192
        aTh = carve(o - 8192, 8 * 1024, BF16).rearrange("p (k t) -> p k t", k=8)
        W1s = []
        for i in range(3):
            W1s.append(carve(o, 8 * 256, BF16).rearrange("p (k c) -> p k c", k=8)); o += 1024
        W2s = []
        for i in range(2):
            W2s.append(carve(o, 32 * 128, BF16).rearrange("p (f c) -> p f c", f=32)); o += 2048
        tmp = []
        for i in range(3):
            tmp.append((carve(o, 512, F32), ("tmp", i))); o += 512
        sq = []
        for i in range(4):
            sq.append((carve(o, 512, BF16), ("sq", i))); o += 256
        rstd = []
        for i in range(2):
            rstd.append((carve(o, 512, F32), ("rstd", i))); o += 512
        assert o <= ARENA_W, o
        w1c = 0
        w2c = 0
        tc = 0
        for hf in range(2):
            if hf == 1:
                P.barrier()
            for t2 in range(2):
                norm_pre(hf * 2 + t2, l * 4 + 2, aTh[:, :, t2 * 512:(t2 + 1) * 512], ("aTh", t2), sq, rstd[t2], 6 + t2)
            for g in range(16):
                slot = w1c % 3
                w1c += 1
                P.dma("pool", out=W1s[slot], in_=w1r_d[l, g].rearrange("p (k c) -> p k c", k=8), sem=f"w1_{slot}", w=[("w1", slot)])
                for fi in range(2):
                    f = g * 2 + fi
                    for t2 in range(2):
                        b = next_ps()
                        for k in range(KC):
                            P.matmul(PS[:, b, :], W1s[slot][:, k, fi * 128:(fi + 1) * 128], aTh[:, k, t2 * 512:(t2 + 1) * 512],
                                     k == 0, k == KC - 1, r=[("w1", slot), (("aTh", t2), k)], w=[("ps", b)])
                        tm = tmp[tc % 3]
                        eng = "dve" if tc % 2 == 0 else "pool"
                        tc += 1
                        P.act(tm[0], PS[:, b, :], AF.Relu, r=[("ps", b)], w=[tm[1]])
                        P.tt(eng, actT[:, f, t2 * 512:(t2 + 1) * 512], tm[0], tm[0], ALU.mult, r=[tm[1]], w=[("act", f, t2)])
            P.barrier()
            pend = []
            for oc in range(8):
                slot = w2c % 2
                w2c += 1
                for q in range(4):
                    P.dma("pool", out=W2s[slot][:, q * 8:(q + 1) * 8, :],
                          in_=w2r_d[l, oc][:, q * 1024:(q + 1) * 1024].rearrange("p (f c) -> p f c", f=8),
                          sem=f"w2_{slot}", w=[("w2", slot, qq) for qq in range(4)])
                for t2 in range(2):
                    b = next_ps()
                    for f in range(32):
                        P.matmul(PS[:, b, :], W2s[slot][:, f, :], actT[:, f, t2 * 512:(t2 + 1) * 512], f == 0, f == 31,
                                 r=[("w2", slot, f // 8), ("act", f, t2)], w=[("ps", b)])
                    for (pb, pt2, poc, ps_) in pend:
                        P.matmul(PS[:, 6 + pt2, :], ones[1024][:], ps_[0], poc == 0, poc == 7, r=[ps_[1], ("ones", 1024)], w=[("ps", 6 + pt2)])
                    pend = []
                    s = sq[(oc * 2 + t2) % 4]
                    P.act(mT[:, t2, oc, :], PS[:, b, :], AF.Copy, r=[("ps", b)], w=[("mT", t2, oc)])
                    P.act(s[0], PS[:, b, :], AF.Square, r=[("ps", b)], w=[s[1]])
                    pend.append((b, t2, oc, s))
            for (pb, pt2, poc, ps_) in pend:
                P.matmul(PS[:, 6 + pt2, :], ones[1024][:], ps_[0], poc == 0, poc == 7, r=[ps_[1], ("ones", 1024)], w=[("ps", 6 + pt2)])
            gi = l * 4 + 3
            for t2 in range(2):
                tt_ = hf * 2 + t2
                tok = slice(tt_ * TT, (tt_ + 1) * TT)
                rstd_from(6 + t2, rstd[t2][0], rstd[t2][1])
                for oc in range(8):
                    tm = tmp[tc % 3]
                    tc += 1
                    P.stt("dve", tm[0], mT[:, t2, oc, :], gains[:, gi * 8 + oc:gi * 8 + oc + 1], rstd[t2][0], ALU.mult, ALU.mult,
                          r=[("mT", t2, oc), "gains", rstd[t2][1]], w=[tm[1]])
                    P.tt("pool", hT[:, oc, tok], hT[:, oc, tok], tm[0], ALU.add, r=[("h", oc, tt_), tm[1]], w=[("h", oc, tt_)])


    def out_proj_post(l, wo, wkey, oT, okey, mT, sqC, rstdC, tC):
        gi = l * 4 + 1
        tcn = 0
        for tt in range(NT):
            tok = slice(tt * TT, (tt + 1) * TT)
            pend = []
            for oc in range(8):
                b = next_ps()
                for c in range(8):
                    P.matmul(PS[:, b, :], wo[:, c, oc * 128:(oc + 1) * 128], oT[:, c, tok], c == 0, c == 7,
                             r=[wkey, (okey, c, tt)], w=[("ps", b)])
                for (poc, ps_) in pend:
                    P.matmul(PS[:, 7, :], ones[1024][:], ps_[0], poc == 0, poc == 7, r=[ps_[1], ("ones", 1024)], w=[("ps", 7)])
                pend = []
                sq_ = sqC[oc % 4]
                P.act(mT[:, oc, :], PS[:, b, :], AF.Copy, r=[("ps", b)], w=[("mT", oc)])
                P.act(sq_[0], PS[:, b, :], AF.Square, r=[("ps", b)], w=[sq_[1]])
                pend.append((oc, sq_))
            for (poc, ps_) in pend:
                P.matmul(PS[:, 7, :], ones[1024][:], ps_[0], poc == 0, poc == 7, r=[ps_[1], ("ones", 1024)], w=[("ps", 7)])
            rstd_from(7, rstdC[0], rstdC[1])
            for oc in range(8):
                tm = tC[tcn % 3]; tcn += 1
                P.stt("dve", tm[0], mT[:, oc, :], gains[:, gi * 8 + oc:gi * 8 + oc + 1], rstdC[0], ALU.mult, ALU.mult,
                      r=[("mT", oc), "gains", rstdC[1]], w=[tm[1]])
                P.tt("pool", hT[:, oc, tok], hT[:, oc, tok], tm[0], ALU.add, r=[("h", oc, tt), tm[1]], w=[("h", oc, tt)])

    SCALE = float(192 ** -0.5)
    TWO_PI = float(2.0 * np.pi)

    def mla_layer(l):
        s_ = l // 2
        P.barrier()
        o = 0
        cqn = carve(o, 4 * T, BF16).rearrange("p (c t) -> p c t", c=4); o += 4096
        ckv_own = carve(o, 2 * T, BF16).rearrange("p (c t) -> p c t", c=2); o += 2048
        ckv_pre = carve(o, 2 * T, BF16).rearrange("p (c t) -> p c t", c=2); o += 2048
        kr_own = carve(o, T, BF16); o += 1024
        kr_pre = carve(o, T, BF16); o += 1024
        cs = carve(o, T, BF16); o += 1024
        sn = carve(o, T, BF16); o += 1024
        masks = carve(o, 4 * 512, BF16).rearrange("p (j q) -> p j q", j=4); o += 1024
        whs = []
        for i in range(2):
            whs.append(carve(o, 1536, BF16)); o += 768
        visones = carve(o, 128, BF16); o += 64
        oT_off = o
        oT = carve(o, 8 * T, BF16).rearrange("p (c t) -> p c t", c=8); o += 8192
        base = o
        Kh = carve(o, 2 * T, BF16); o += 2048
        Vh = carve(o, 32 * 128, BF16).rearrange("p (b d) -> p b d", b=32); o += 2048
        qn = carve(o, T, BF16); o += 1024
        qr = carve(o, T, BF16); o += 1024
        PT = []
        for i in range(4):
            PT.append((carve(o, 512, BF16), ("PT", i))); o += 256
        rden = []
        for i in range(2):
            rden.append((carve(o, 512, F32), ("rden", i))); o += 512
        tq_ = []
        for i in range(4):
            tq_.append((carve(o, 512, F32), ("tq", i))); o += 512
        assert o <= ARENA_W, o
        a = oT_off
        aT = carve(a, 8 * 512, BF16).rearrange("p (k t) -> p k t", k=8); a += 2048
        cq_raw = carve(a, 4 * 512, F32).rearrange("p (c t) -> p c t", c=4); a += 2048
        ckv_raw = carve(a, 2 * 512, F32).rearrange("p (c t) -> p c t", c=2); a += 1024
        sqA = []
        for i in range(4):
            sqA.append((carve(a, 512, BF16), ("sqA", i))); a += 256
        rstdA = []
        for i in range(3):
            rstdA.append((carve(a, 512, F32), ("rstdA", i))); a += 512
        tA = []
        for i in range(2):
            tA.append((carve(a, 512, F32), ("tA", i))); a += 512
        posi = carve(a, T, I32); a += 2048
        yv = carve(a, T, F32); a += 2048
        kf = carve(a, T, F32); a += 2048
        ki = carve(a, T, I32); a += 2048
        win = carve(a, 8 * 896, BF16).rearrange("p (k c) -> p k c", k=8); a += 3584
        assert a <= ARENA_W, a

        P.dma("pool", out=win, in_=mla_win_d[s_].rearrange("p (k c) -> p k c", k=8), sem="win", w=["win"])
        P.dma("pool", out=masks, in_=masks_d.rearrange("p (j q) -> p j q", j=4), sem="masks", w=["masks"])
        P.dma("sp", out=posi[0:64, :], in_=posb_d, sem="posi", w=["posi"])
        P.ts("dve", visones, ones[1][:], ctab[:, 3:4], None, ALU.mult, None, r=[("ones", 1), "ctab"], w=["visones"])
        posf = kf
        P.copy("dve", posf[0:64, :], posi[0:64, :], r=["posi"], w=["posf"])
        for (tbl, col, key) in ((cs, 1, "cs"), (sn, 2, "sn")):
            P.ts("dve", yv[0:64, :], posf[0:64, :], ctab[0:64, 0:1], ctab[0:64, col:col + 1], ALU.mult, ALU.add,
                 r=["posf", "ctab"], w=["yv"])
            P.copy("dve", ki[0:64, :], yv[0:64, :], r=["yv"], w=["ki"])
            kf2 = carve(oT_off, T, F32)
            P.copy("dve", kf2[0:64, :], ki[0:64, :], r=["ki"], w=["kf2"])
            P.tt("dve", yv[0:64, :], yv[0:64, :], kf2[0:64, :], ALU.subtract, r=["yv", "kf2"], w=["yv"])
            P.ts("dve", kf2[0:64, :], yv[0:64, :], 0.5, None, ALU.is_gt, None, r=["yv"], w=["kf2"])
            P.tt("dve", yv[0:64, :], yv[0:64, :], kf2[0:64, :], ALU.subtract, r=["yv", "kf2"], w=["yv"])
            P.ts("dve", kf2[0:64, :], yv[0:64, :], -0.5, None, ALU.is_lt, None, r=["yv"], w=["kf2"])
            P.tt("dve", yv[0:64, :], yv[0:64, :], kf2[0:64, :], ALU.add, r=["yv", "kf2"], w=["yv"])
            P.act(tbl[0:64, :], yv[0:64, :], AF.Sin, scale=TWO_PI, r=["yv"], w=[key])
        P.barrier()

        gq = 8 * s_
        for tt in range(NT):
            tok = slice(tt * TT, (tt + 1) * TT)
            norm_pre(tt, l * 4 + 0, aT, "aT", sqA, rstdA[0], 6)
            for c in range(4):
                b = next_ps()
                for k in range(KC):
                    P.matmul(PS[:, b, :], win[:, k, c * 128:(c + 1) * 128], aT[:, k, :], k == 0, k == KC - 1,
                             r=["win", ("aT", k)], w=[("ps", b)])
                sq_ = sqA[c % 4]
                P.act(cq_raw[:, c, :], PS[:, b, :], AF.Copy, r=[("ps", b)], w=[("cq_raw", c)])
                P.act(sq_[0], PS[:, b, :], AF.Square, r=[("ps", b)], w=[sq_[1]])
                P.matmul(PS[:, 7, :], ones[512][:], sq_[0], c == 0, c == 3, r=[sq_[1], ("ones", 512)], w=[("ps", 7)])
            rstd_from(7, rstdA[1][0], rstdA[1][1])
            for c in range(4):
                P.stt("dve", cqn[:, c, tok], cq_raw[:, c, :], mla_g[:, gq + c:gq + c + 1], rstdA[1][0], ALU.mult, ALU.mult,
                      r=[("cq_raw", c), "mla_g", rstdA[1][1]], w=[("cqn", c, tt)])
            for c in range(2):
                b = next_ps()
                for k in range(KC):
                    P.matmul(PS[:, b, :], win[:, k, 512 + c * 128:512 + (c + 1) * 128], aT[:, k, :], k == 0, k == KC - 1,
                             r=["win", ("aT", k)], w=[("ps", b)])
                sq_ = sqA[c % 4]
                P.act(ckv_raw[:, c, :], PS[:, b, :], AF.Copy, r=[("ps", b)], w=[("ckv_raw", c)])
                P.act(sq_[0], PS[:, b, :], AF.Square, r=[("ps", b)], w=[sq_[1]])
                P.matmul(PS[:, 6, :], ones[256][:], sq_[0], c == 0, c == 1, r=[sq_[1], ("ones", 256)], w=[("ps", 6)])
            rstd_from(6, rstdA[2][0], rstdA[2][1])
            for c in range(2):
                P.stt("dve", ckv_own[:, c, tok], ckv_raw[:, c, :], mla_g[:, gq + 4 + c:gq + 5 + c], rstdA[2][0], ALU.mult, ALU.mult,
                      r=[("ckv_raw", c), "mla_g", rstdA[2][1]], w=[("ckv_own", c, tt)])
            b1 = next_ps()
            b2 = next_ps()
            for (bb, c0) in ((b1, 768), (b2, 832)):
                for k in range(KC):
                    P.matmul(PS[0:64, bb, :], win[:, k, c0:c0 + 64], aT[:, k, :], k == 0, k == KC - 1,
                             r=["win", ("aT", k)], w=[("ps", bb)])
            P.tt("dve", tA[0][0][0:64, :], PS[0:64, b1, :], cs[0:64, tok], ALU.mult, r=[("ps", b1), "cs"], w=[tA[0][1]])
            P.tt("dve", tA[1][0][0:64, :], PS[0:64, b2, :], sn[0:64, tok], ALU.mult, r=[("ps", b2), "sn"], w=[tA[1][1]])
            P.tt("pool", kr_own[0:64, tok], tA[0][0][0:64, :], tA[1][0][0:64, :], ALU.add, r=[tA[0][1], tA[1][1]], w=[("kr_own", tt)])

        if os.environ.get("KDBG") == "A":
            return
        li = s_
        for c in range(2):
            P.dma("sp", out=xin_d[li][c * 128:(c + 1) * 128, :], in_=ckv_own[:, c, :], sem=f"xi{li}_{c}",
                  r=[("ckv_own", c, t) for t in range(NT)], w=[("xin", li, c)])
        P.dma("sp", out=xin_d[li][256:320, :], in_=kr_own[0:64, :], sem=f"xi{li}_2",
              r=[("kr_own", t) for t in range(NT)], w=[("xin", li, 2)])
        P.collective("AllGather", PAIRS, xin_d[li].opt(), xout_d[li].opt(), f"cc{li}",
                     r=[("xin", li, c) for c in range(3)], w=[("xout", li)])
        for c in range(2):
            P.dma("sp", out=ckv_pre[:, c, :], in_=xout_d[li][c * 128:(c + 1) * 128, :], sem=f"xo{li}_{c}",
                  r=[("xout", li)], w=[("ckv_pre", c)])
        P.dma("sp", out=kr_pre[0:64, :], in_=xout_d[li][256:320, :], sem=f"xo{li}_2", r=[("xout", li)], w=["kr_pre"])
        P.barrier()
        if os.environ.get("KDBG") == "X":
            return

        ev = 0
        for h in range(1 if os.environ.get("KDBG") in ("B1", "B2") else 8):
            wh = whs[h % 2]
            wkey = ("wh", h % 2)
            P.dma("pool", out=wh, in_=mla_wh_d[s_, h], sem=f"wh{h % 2}", w=[wkey])
            uk = wh[:, 0:256].rearrange("p (c d) -> p c d", c=2)
            uv = wh[:, 256:512].rearrange("p (c d) -> p c d", c=2)
            uqn = wh[:, 512:1024].rearrange("p (c d) -> p c d", c=4)
            uqr = wh[:, 1024:1280].rearrange("p (c d) -> p c d", c=4)
            uqrs = wh[:, 1280:1536].rearrange("p (c d) -> p c d", c=4)
            for t8 in range(8):
                src, skey = (ckv_pre, "ckv_pre") if t8 < 4 else (ckv_own, "ckv_own")
                tl = t8 % 4
                b = next_ps(0, 3)
                for c in range(2):
                    rk = (skey, c) if t8 < 4 else (skey, c, tl)
                    P.matmul(PS[:, b, :], uk[:, c, :], src[:, c, tl * 512:(tl + 1) * 512], c == 0, c == 1,
                             r=[wkey, rk], w=[("ps", b)])
                P.copy("dve", Kh[:, t8 * 512:(t8 + 1) * 512], PS[:, b, :], r=[("ps", b)], w=[("Kh", t8)])
            for g in range(8):
                b = next_ps(0, 3)
                for bi in range(4):
                    blk = g * 4 + bi
                    src, skey = (ckv_pre, "ckv_pre") if blk < 16 else (ckv_own, "ckv_own")
                    lb_ = blk % 16
                    for c in range(2):
                        rk = (skey, c) if blk < 16 else (skey, c, lb_ // 4)
                        P.matmul(PS[:, b, bi * 128:(bi + 1) * 128], src[:, c, lb_ * 128:(lb_ + 1) * 128], uv[:, c, :], c == 0, c == 1,
                                 r=[wkey, rk], w=[("ps", b)])
                dst = Vh[:, g * 4:(g + 1) * 4, :]
                srcp = PS[:, b, :].rearrange("p (b d) -> p b d", b=4)
                if g < 4:
                    P.ts("dve", dst, srcp, ctab[:, 3:4], None, ALU.mult, None, r=[("ps", b), "ctab"], w=[("Vh", g)])
                else:
                    P.copy("dve", dst, srcp, r=[("ps", b)], w=[("Vh", g)])
            for tt in range(NT):
                tok = slice(tt * TT, (tt + 1) * TT)
                b = next_ps(0, 3)
                for c in range(4):
                    P.matmul(PS[:, b, :], uqn[:, c, :], cqn[:, c, tok], c == 0, c == 3, r=[wkey, ("cqn", c, tt)], w=[("ps", b)])
                P.copy("dve", qn[:, tok], PS[:, b, :], r=[("ps", b)], w=[("qn", tt)])
                b1 = next_ps(0, 3)
                for c in range(4):
                    P.matmul(PS[0:64, b1, :], uqr[:, c, :], cqn[:, c, tok], c == 0, c == 3, r=[wkey, ("cqn", c, tt)], w=[("ps", b1)])
                t1 = tq_[ev % 4]; ev += 1
                P.tt("dve", t1[0][0:64, :], PS[0:64, b1, :], cs[0:64, tok], ALU.mult, r=[("ps", b1), "cs"], w=[t1[1]])
                b2 = next_ps(0, 3)
                for c in range(4):
                    P.matmul(PS[0:64, b2, :], uqrs[:, c, :], cqn[:, c, tok], c == 0, c == 3, r=[wkey, ("cqn", c, tt)], w=[("ps", b2)])
                t2 = tq_[ev % 4]; ev += 1
                P.tt("dve", t2[0][0:64, :], PS[0:64, b2, :], sn[0:64, tok], ALU.mult, r=[("ps", b2), "sn"], w=[t2[1]])
                P.tt("pool", qr[0:64, tok], t1[0][0:64, :], t2[0][0:64, :], ALU.add, r=[t1[1], t2[1]], w=[("qr", tt)])
            for tq in range(0 if os.environ.get("KDBG") == "B1" else NT):
                qtok = slice(tq * TT, (tq + 1) * TT)
                ab = (h * NT + tq) % 2
                bo, bd = 3 + 2 * ab, 4 + 2 * ab
                units = [(True, kb) for kb in range(16)] + [(False, kb) for kb in range(4 * tq + 4)]
                n = len(units)
                pend = None
                for u in range(n + 1):
                    if u < n:
                        pre, kb = units[u]
                        col = kb * 128 if pre else T + kb * 128
                        bs = next_ps(0, 3)
                        krs, krk = (kr_pre, "kr_pre") if pre else (kr_own, ("kr_own", kb // 4))
                        P.matmul(PS[:, bs, :], Kh[:, col:col + 128], qn[:, qtok], True, False,
                                 r=[("Kh", col // 512), ("qn", tq)], w=[("ps", bs)])
                        P.matmul(PS[:, bs, :], krs[0:64, kb * 128:(kb + 1) * 128], qr[0:64, qtok], False, True,
                                 r=[krk, ("qr", tq)], w=[("ps", bs)])
                        pt = PT[u % 4]
                        P.act(pt[0], PS[:, bs, :], AF.Exp, scale=SCALE, r=[("ps", bs)], w=[pt[1]])
                        if (not pre) and kb >= 4 * tq:
                            P.tt("pool", pt[0], pt[0], masks[:, kb - 4 * tq, :], ALU.mult, r=[pt[1], "masks"], w=[pt[1]])
                    if pend is not None:
                        (ppre, pkb, ppt, pu) = pend
                        vb = pkb if ppre else 16 + pkb
                        P.matmul(PS[:, bo, :], Vh[:, vb, :], ppt[0], pu == 0, pu == n - 1, r=[("Vh", vb // 4), ppt[1]], w=[("ps", bo)])
                        lh = visones if ppre else ones[1][:]
                        P.matmul(PS[:, bd, :], lh, ppt[0], pu == 0, pu == n - 1,
                                 r=["visones" if ppre else ("ones", 1), ppt[1]], w=[("ps", bd)])
                    pend = (pre, kb, pt, u) if u < n else None
                rd = rden[ab]
                P.recip(rd[0], PS[:, bd, :], r=[("ps", bd)], w=[rd[1]])
                P.tt("dve", oT[:, h, qtok], PS[:, bo, :], rd[0], ALU.mult, r=[("ps", bo), rd[1]], w=[("oT", h, tq)])
        P.barrier()
        if os.environ.get("KDBG") in ("B", "B1", "B2"):
            return

        wo = carve(0, 8 * 1024, BF16).rearrange("p (c n) -> p c n", c=8)
        c_ = base
        mT = carve(c_, 8 * 512, F32).rearrange("p (k t) -> p k t", k=8); c_ += 4096
        sqC = []
        for i in range(4):
            sqC.append((carve(c_, 512, BF16), ("sqC", i))); c_ += 256
        rstdC = (carve(c_, 512, F32), "rstdC"); c_ += 512
        tC = []
        for i in range(3):
            tC.append((carve(c_, 512, F32), ("tC", i))); c_ += 512
        assert c_ <= ARENA_W
        P.dma("pool", out=wo, in_=mla_wo_d[s_].rearrange("p (c n) -> p c n", c=8), sem="wo", w=["wo"])
        gi = l * 4 + 1
        tcn = 0
        for tt in range(NT):
            tok = slice(tt * TT, (tt + 1) * TT)
            pend = []
            for oc in range(8):
                b = next_ps()
                for c in range(8):
                    P.matmul(PS[:, b, :], wo[:, c, oc * 128:(oc + 1) * 128], oT[:, c, tok], c == 0, c == 7,
                             r=["wo", ("oT", c, tt)], w=[("ps", b)])
                for (poc, ps_) in pend:
                    P.matmul(PS[:, 7, :], ones[1024][:], ps_[0], poc == 0, poc == 7, r=[ps_[1], ("ones", 1024)], w=[("ps", 7)])
                pend = []
                sq_ = sqC[oc % 4]
                P.act(mT[:, oc, :], PS[:, b, :], AF.Copy, r=[("ps", b)], w=[("mT", oc)])
                P.act(sq_[0], PS[:, b, :], AF.Square, r=[("ps", b)], w=[sq_[1]])
                pend.append((oc, sq_))
            for (poc, ps_) in pend:
                P.matmul(PS[:, 7, :], ones[1024][:], ps_[0], poc == 0, poc == 7, r=[ps_[1], ("ones", 1024)], w=[("ps", 7)])
            rstd_from(7, rstdC[0], rstdC[1])
            for oc in range(8):
                tm = tC[tcn % 3]; tcn += 1
                P.stt("dve", tm[0], mT[:, oc, :], gains[:, gi * 8 + oc:gi * 8 + oc + 1], rstdC[0], ALU.mult, ALU.mult,
                      r=[("mT", oc), "gains", rstdC[1]], w=[tm[1]])
                P.tt("pool", hT[:, oc, tok], hT[:, oc, tok], tm[0], ALU.add, r=[("h", oc, tt), tm[1]], w=[("h", oc, tt)])


    def hgrn_layer(l):
        s_ = l // 2
        P.barrier()
        HW = 1024
        o = 0
        aT = carve(o, 8 * T, BF16).rearrange("p (k t) -> p k t", k=8); o += 8192
        ogT = carve(o, 8 * T, BF16).rearrange("p (k t) -> p k t", k=8); o += 8192
        qf = carve(o, HW, F32); o += 1024
        kf = carve(o, HW, F32); o += 1024
        bb = carve(o, HW, F32); o += 1024
        X = carve(o, HW, F32); o += 1024
        X2 = carve(o, HW, F32); o += 1024
        mscan = carve(o, HW, F32); o += 1024
        mone = carve(o, HW, F32); o += 1024
        qrel_off = o
        qrel = carve(o, HW, BF16); o += 512
        krel = carve(o, HW, BF16); o += 512
        qdec = carve(o, HW, BF16); o += 512
        kdec = carve(o, HW, BF16); o += 512
        vT = carve(o, HW, BF16); o += 512
        gate = carve(o, HW, BF16); o += 512
        Ws = []
        for i in range(4):
            Ws.append(carve(o, 8 * 128, BF16).rearrange("p (k c) -> p k c", k=8)); o += 512
        kdT = [carve(o + 256 * i, 512, BF16) for i in range(2)]; o += 512
        vch = [carve(o + 256 * i, 512, BF16) for i in range(2)]; o += 512
        ATs = [carve(o + 64 * i, 128, BF16) for i in range(2)]; o += 128
        Sf = [carve(o + 128 * i, 128, F32) for i in range(2)]; o += 256
        Sb = [carve(o + 64 * i, 128, BF16) for i in range(2)]; o += 128
        dec = carve(o, 32, F32); o += 32
        dhalf = carve(o, 2, F32); o += 2
        o += 2
        S1f = carve(o, 8 * 128, F32).rearrange("p (h e) -> p h e", h=8); o += 1024
        S0 = carve(o, 8 * 128, F32).rearrange("p (h e) -> p h e", h=8); o += 1024
        osb = [(carve(o + 512 * i, 512, F32), ("osb", i)) for i in range(2)]; o += 1024
        sqH = [(carve(o + 256 * i, 512, BF16), ("sqH", i)) for i in range(4)]; o += 1024
        rstdH = [(carve(o + 512 * i, 512, F32), ("rstdH", i)) for i in range(2)]; o += 1024
        kfT = carve(qrel_off, 8 * 128, BF16).rearrange("p (b d) -> p b d", b=8)
        vtok = carve(qrel_off + 512, 8 * 128, BF16).rearrange("p (b e) -> p b e", b=8)
        assert o <= ARENA_W, o
        ident = cmask[:, 128:256]
        m32 = cmask[0:32, 0:128]
        gcol = 2 * s_
        lb_ap = lambda h: lbt[:, 8 * s_ + h:8 * s_ + h + 1]
        oml_ap = lambda h: lbt[:, 16 + 8 * s_ + h:16 + 8 * s_ + h + 1]
        noml_ap = lambda h: lbt[:, 32 + 8 * s_ + h:32 + 8 * s_ + h + 1]
        onorm_ap = hg_c[:, 32 + s_:33 + s_]

        P.memset("pool", mone, 1.0, w=["mone"])
        P.memset("pool", mscan, 1.0, w=["mscan"])
        P.memset("pool", mscan.rearrange("p (c t) -> p c t", t=32)[:, :, 0:1], 0.0, w=["mscan"])
        for tt in range(NT):
            norm_pre(tt, l * 4 + 0, aT[:, :, tt * TT:(tt + 1) * TT], ("aTg", tt), sqH, rstdH[tt % 2], 6 + tt % 2)

        wcnt = {"n": 0}

        def load_w(which, h):
            P.dma("pool", out=Ws[which], in_=hg_win_d[s_, which * 8 + h].rearrange("p (k c) -> p k c", k=8),
                  sem=f"hw{which}", w=[("hw", which)])

        def proj(which, hf, evac):
            for t2 in range(2):
                tt = hf * 2 + t2
                b = next_ps(0, 4)
                for k in range(KC):
                    P.matmul(PS[:, b, :], Ws[which][:, k, :], aT[:, k, tt * TT:(tt + 1) * TT], k == 0, k == KC - 1,
                             r=[("hw", which), (("aTg", tt), k)], w=[("ps", b)])
                evac(b, t2)

        def f_branch(h, hf):
            def ev(b, t2):
                P.act(X[:, t2 * 512:(t2 + 1) * 512], PS[:, b, :], AF.Sigmoid, r=[("ps", b)], w=[("X", t2)])
            proj(1, hf, ev)
            P.ts("dve", kf, X, noml_ap(h), oml_ap(h), ALU.mult, ALU.add, r=[("X", 0), ("X", 1), "noml", "oml"], w=["kf"])
            P.ts("dve", X, kf, -1.0, 1.0, ALU.mult, ALU.add, r=["kf"], w=[("X", 0), ("X", 1)])
            P.act(bb, X, AF.Ln, r=[("X", 0), ("X", 1)], w=["bb"])

        for h in range(8):
            load_w(1, h)
            load_w(2, h)
            for hf in range(2):
                f_branch(h, hf)
                P.scan("dve", X2, mone, bb, 0.0, ALU.mult, ALU.add, r=["mone", "bb"], w=["X2"])
                P.tt("dve", X, X2[:, HW - 1:HW].broadcast_to([128, HW]), X2, ALU.subtract, r=["X2"], w=[("X", 0), ("X", 1)])
                P.act(bb, X, AF.Exp, r=[("X", 0), ("X", 1)], w=["bb"])
                P.tt("pool", kdec, kf, bb, ALU.mult, r=["kf", "bb"], w=["kdec"])
                P.act(dhalf[:, 0:1], X2[:, HW - 1:HW], AF.Exp, r=["X2"], w=["dhalf"])
                for g in range(2):
                    b = next_ps(0, 4)
                    for bi in range(4):
                        blk = g * 4 + bi
                        t0 = hf * HW + blk * 128
                        for k in range(KC):
                            P.matmul(PS[:, b, bi * 128:(bi + 1) * 128], aT[:, k, t0:t0 + 128], Ws[2][:, k, :], k == 0, k == KC - 1,
                                     r=[("hw", 2), (("aTg", t0 // TT), k)], w=[("ps", b)])
                    P.copy("dve", vtok[:, g * 4:(g + 1) * 4, :], PS[:, b, :].rearrange("p (b e) -> p b e", b=4), r=[("ps", b)], w=[("vtok", g)])
                    b2 = next_ps(0, 4)
                    pb = PS[:, b2, :].bitcast(BF16)
                    for bi in range(4):
                        blk = g * 4 + bi
                        P.transpose(pb[:, bi * 128:(bi + 1) * 128], kdec[:, blk * 128:(blk + 1) * 128], ident, r=["kdec", "cmask"], w=[("ps", b2)])
                    P.copy("dve", kfT[:, g * 4:(g + 1) * 4, :], pb[:, 0:512].rearrange("p (b d) -> p b d", b=4), r=[("ps", b2)], w=[("kfT", g)])
                bS = next_ps(4, 6)
                for blk in range(8):
                    P.matmul(PS[:, bS, 0:128], kfT[:, blk, :], vtok[:, blk, :], blk == 0, blk == 7,
                             r=[("kfT", blk // 4), ("vtok", blk // 4)], w=[("ps", bS)])
                if hf == 0:
                    P.copy("dve", S1f[:, h, :], PS[:, bS, 0:128], r=[("ps", bS)], w=[("S1f", h)])
                else:
                    P.stt("dve", S1f[:, h, :], S1f[:, h, :], dhalf[:, 0:1], PS[:, bS, 0:128], ALU.mult, ALU.add,
                          r=[("S1f", h), "dhalf", ("ps", bS)], w=[("S1f", h)])
        li = s_
        P.dma("sp", out=sin_d[li], in_=S1f.rearrange("p h e -> p (h e)"), sem=f"si{li}", r=[("S1f", h) for h in range(8)], w=[("sin", li)])
        P.collective("AllGather", PAIRS, sin_d[li].opt(), sout_d[li].opt(), f"hcc{li}", r=[("sin", li)], w=[("sout", li)])
        P.dma("sp", out=S0.rearrange("p h e -> p (h e)"), in_=sout_d[li][0:128, :], sem=f"so{li}", r=[("sout", li)], w=["S0"])

        P.barrier()
        oc_ = 0
        for h in range(8):
            for w_ in range(4):
                load_w(w_, h)
            cur = 0
            P.ts("dve", Sf[0], S0[:, h, :], ctab[:, 3:4], None, ALU.mult, None, r=["S0", "ctab"], w=[("Sf", 0)])
            P.copy("pool", Sb[0], Sf[0], r=[("Sf", 0)], w=[("Sb", 0)])
            for hf in range(2):
                def evq(b, t2):
                    P.act(qf[:, t2 * 512:(t2 + 1) * 512], PS[:, b, :], AF.Silu, r=[("ps", b)], w=[("qf", t2)])
                proj(0, hf, evq)
                def evg(b, t2):
                    P.act(gate[:, t2 * 512:(t2 + 1) * 512], PS[:, b, :], AF.Silu, r=[("ps", b)], w=[("gate", t2)])
                proj(3, hf, evg)
                def evv(b, t2):
                    P.copy("dve", vT[:, t2 * 512:(t2 + 1) * 512], PS[:, b, :], r=[("ps", b)], w=[("vT", t2)])
                proj(2, hf, evv)
                f_branch(h, hf)
                QF = [("qf", 0), ("qf", 1)]
                XK = [("X", 0), ("X", 1)]
                P.scan("dve", X2, mscan, bb, 0.0, ALU.mult, ALU.add, r=["mscan", "bb"], w=["X2"])
                b3 = X2.rearrange("p (c t) -> p c t", t=32)
                X3 = X.rearrange("p (c t) -> p c t", t=32)
                P.tt("dve", X3, b3, b3[:, :, 16:17].broadcast_to([128, 32, 32]), ALU.subtract, r=["X2"], w=XK)
                P.act(bb, X, AF.Exp, r=XK, w=["bb"])
                P.tt("pool", qrel, qf, bb, ALU.mult, r=QF + ["bb"], w=["qrel"])
                P.act(bb, X, AF.Exp, scale=-1.0, r=XK, w=["bb"])
                P.tt("pool", krel, kf, bb, ALU.mult, r=["kf", "bb"], w=["krel"])
                P.act(bb, X2, AF.Exp, r=["X2"], w=["bb"])
                P.tt("pool", qdec, qf, bb, ALU.mult, r=QF + ["bb"], w=["qdec"])
                P.tt("dve", X3, b3[:, :, 31:32].broadcast_to([128, 32, 32]), b3, ALU.subtract, r=["X2"], w=XK)
                P.act(bb, X, AF.Exp, r=XK, w=["bb"])
                P.tt("pool", kdec, kf, bb, ALU.mult, r=["kf", "bb"], w=["kdec"])
                P.act(dec.rearrange("p (c o) -> p c o", o=1), b3[:, :, 31:32], AF.Exp, r=["X2"], w=["dec"])
                for g in range(8):
                    sl = g % 2
                    bT = next_ps(0, 4)
                    pbk = PS[:, bT, :].bitcast(BF16)
                    for n in range(4):
                        c0 = (g * 4 + n) * 32
                        P.transpose(pbk[0:32, n * 128:(n + 1) * 128], kdec[:, c0:c0 + 32], ident, r=["kdec", "cmask"], w=[("ps", bT)])
                    P.copy("dve", kdT[sl][0:32, :], pbk[0:32, 0:512], r=[("ps", bT)], w=[("kdT", sl)])
                    bV = next_ps(0, 4)
                    pbv = PS[:, bV, :].bitcast(BF16)
                    for n in range(4):
                        c0 = (g * 4 + n) * 32
                        P.transpose(pbv[0:32, n * 128:(n + 1) * 128], vT[:, c0:c0 + 32], ident, r=[("vT", c0 // 512), "cmask"], w=[("ps", bV)])
                    P.copy("dve", vch[sl][0:32, :], pbv[0:32, 0:512], r=[("ps", bV)], w=[("vch", sl)])
                    bA = next_ps(0, 4)
                    for n in range(4):
                        c0 = (g * 4 + n) * 32
                        P.matmul(PS[0:32, bA, n * 32:(n + 1) * 32], krel[:, c0:c0 + 32], qrel[:, c0:c0 + 32], True, True,
                                 r=["krel", "qrel"], w=[("ps", bA)])
                    P.tt("dve", ATs[sl][0:32, :], PS[0:32, bA, 0:128], m32, ALU.mult, r=[("ps", bA), "cmask"], w=[("AT", sl)])
                    bO = 4 + ((hf * 8 + g) // 4) % 2
                    for n in range(4):
                        c0 = (g * 4 + n) * 32
                        oc0 = ((g % 4) * 4 + n) * 32
                        P.matmul(PS[:, bO, oc0:oc0 + 32], vch[sl][0:32, n * 128:(n + 1) * 128], ATs[sl][0:32, n * 32:(n + 1) * 32], True, False,
                                 r=[("vch", sl), ("AT", sl)], w=[("ps", bO)])
                        P.matmul(PS[:, bO, oc0:oc0 + 32], Sb[cur], qdec[:, c0:c0 + 32], False, True,
                                 r=[("Sb", cur), "qdec"], w=[("ps", bO)])
                        bU = 6 + (g * 4 + n) % 2
                        P.matmul(PS[:, bU, 0:128], kdT[sl][0:32, n * 128:(n + 1) * 128], vch[sl][0:32, n * 128:(n + 1) * 128], True, True,
                                 r=[("kdT", sl), ("vch", sl)], w=[("ps", bU)])
                        nxt = 1 - cur
                        ci = g * 4 + n
                        P.stt("dve", Sf[nxt], Sf[cur], dec[:, ci:ci + 1], PS[:, bU, 0:128], ALU.mult, ALU.add,
                              r=[("Sf", cur), "dec", ("ps", bU)], w=[("Sf", nxt)])
                        P.copy("pool", Sb[nxt], Sf[nxt], r=[("Sf", nxt)], w=[("Sb", nxt)])
                        cur = nxt
                    if g % 4 == 3:
                        t2 = g // 4
                        tt = hf * 2 + t2
                        ob = osb[oc_ % 2]; sq_ = sqH[oc_ % 4]; rs = rstdH[oc_ % 2]; oc_ += 1
                        P.act(ob[0], PS[:, bO, :], AF.Copy, r=[("ps", bO)], w=[ob[1]])
                        P.act(sq_[0], PS[:, bO, :], AF.Square, r=[("ps", bO)], w=[sq_[1]])
                        bN = next_ps(0, 4)
                        P.matmul(PS[:, bN, :], ones[128][:], sq_[0], True, True, r=[sq_[1], ("ones", 128)], w=[("ps", bN)])
                        rstd_from(bN, rs[0], rs[1])
                        P.stt("dve", ob[0], ob[0], onorm_ap, rs[0], ALU.mult, ALU.mult, r=[ob[1], "hg_c", rs[1]], w=[ob[1]])
                        P.tt("pool", ogT[:, h, tt * TT:(tt + 1) * TT], ob[0], gate[:, t2 * 512:(t2 + 1) * 512], ALU.mult,
                             r=[ob[1], ("gate", t2)], w=[("ogT", h, tt)])
        P.barrier()
        wo = carve(0, 8 * 1024, BF16).rearrange("p (c n) -> p c n", c=8)
        c_ = 16384
        mT = carve(c_, 8 * 512, F32).rearrange("p (k t) -> p k t", k=8); c_ += 4096
        sqC = []
        for i in range(4):
            sqC.append((carve(c_, 512, BF16), ("sqC", i))); c_ += 256
        rstdC = (carve(c_, 512, F32), "rstdC"); c_ += 512
        tC = []
        for i in range(3):
            tC.append((carve(c_, 512, F32), ("tC", i))); c_ += 512
        P.dma("pool", out=wo, in_=hg_wo_d[s_].rearrange("p (c n) -> p c n", c=8), sem="hwo", w=["hwo"])
        out_proj_post(l, wo, "hwo", ogT, "ogT", mT, sqC, rstdC, tC)

    for (kind, l) in plan:
        if kind == "mlp":
            mlp_layer(l)
        elif kind == "mla":
            mla_layer(l)
        elif kind == "hgrn":
            hgrn_layer(l)
        else:
            raise NotImplementedError(kind)

    P.barrier()
    for k in range(KC):
        P.dma("sp", out=outT_d[k * 128:(k + 1) * 128, :], in_=hT[:, k, :], sem="xout", r=[("h", k, t) for t in range(NT)])
    P.wait_all_dma("sp")
    P.emit()
    return nc


FULL_PLAN = []
for _l in range(DEPTH):
    FULL_PLAN.append(("mla" if _l % 2 == 0 else "hgrn", _l))
    FULL_PLAN.append(("mlp", _l))


def prep_weights(inp):
    f = np.float32
    out = {}
    ng = np.asarray(inp["norm_gains"], f)
    out["gains"] = np.ascontiguousarray(ng.reshape(16, 8, 128).transpose(2, 0, 1).reshape(128, 128))
    w1 = np.asarray(inp["mlp_w1"], f)
    out["w1r"] = np.ascontiguousarray(w1.reshape(4, 8, 128, 16, 256).transpose(0, 3, 2, 1, 4).reshape(4, 16, 128, 2048))
    w2 = np.asarray(inp["mlp_w2"], f)
    out["w2r"] = np.ascontiguousarray(w2.reshape(4, 32, 128, 8, 128).transpose(0, 3, 2, 1, 4).reshape(4, 8, 128, 4096))
    w_in = np.asarray(inp["mla_w_in"], f)
    kr = w_in[:, :, 768:832]
    kr_sw = np.concatenate([kr[:, :, 32:64], kr[:, :, 0:32]], axis=2)
    w_in_ext = np.concatenate([w_in, kr_sw], axis=2)
    out["mla_win"] = np.ascontiguousarray(w_in_ext.reshape(2, 8, 128, 896).transpose(0, 2, 1, 3).reshape(2, 128, 8 * 896))
    g = np.zeros((128, 16), f)
    qn_ = np.asarray(inp["mla_q_norm"], f)
    kvn_ = np.asarray(inp["mla_kv_norm"], f)
    for s_ in range(2):
        g[:, 8 * s_:8 * s_ + 4] = qn_[s_].reshape(4, 128).T
        g[:, 8 * s_ + 4:8 * s_ + 6] = kvn_[s_].reshape(2, 128).T
    out["mla_g"] = g
    w_uq = np.asarray(inp["mla_w_uq"], f).reshape(2, 4, 128, 8, 192)
    w_ukv = np.asarray(inp["mla_w_ukv"], f).reshape(2, 2, 128, 8, 256)
    uk = w_ukv[..., 0:128].transpose(0, 3, 2, 1, 4).reshape(2, 8, 128, 256)
    uv = w_ukv[..., 128:256].transpose(0, 3, 2, 1, 4).reshape(2, 8, 128, 256)
    uqn = w_uq[..., 0:128].transpose(0, 3, 2, 1, 4).reshape(2, 8, 128, 512)
    qr_ = w_uq[..., 128:192]
    uqr = qr_.transpose(0, 3, 2, 1, 4).reshape(2, 8, 128, 256)
    qrs_ = np.concatenate([qr_[..., 32:64], qr_[..., 0:32]], axis=-1)
    uqrs = qrs_.transpose(0, 3, 2, 1, 4).reshape(2, 8, 128, 256)
    out["mla_wh"] = np.ascontiguousarray(np.concatenate([uk, uv, uqn, uqr, uqrs], axis=3))
    w_o = np.asarray(inp["mla_w_o"], f)
    out["mla_wo"] = np.ascontiguousarray(w_o.reshape(2, 8, 128, 1024).transpose(0, 2, 1, 3).reshape(2, 128, 8192))
    hw = np.asarray(inp["hgrn_w_in"], f)
    out["hg_win"] = np.ascontiguousarray(hw.reshape(2, 8, 128, 32, 128).transpose(0, 3, 2, 1, 4).reshape(2, 32, 128, 1024))
    hwo = np.asarray(inp["hgrn_w_o"], f)
    out["hg_wo"] = np.ascontiguousarray(hwo.reshape(2, 8, 128, 1024).transpose(0, 2, 1, 3).reshape(2, 128, 8192))
    hc = np.zeros((128, 64), f)
    lbl = np.asarray(inp["hgrn_lb_logits"], f)
    hc[:, 0:32] = lbl.reshape(4, 8, 128).transpose(2, 0, 1).reshape(128, 32)
    hc[:, 32:34] = np.asarray(inp["hgrn_o_norm"], f).T
    out["hg_c"] = hc
    cm = np.zeros((128, 256), f)
    ss_ = np.arange(32)[:, None]
    cc_ = np.arange(32)[None, :]
    cm[0:32, 0:128] = np.tile((cc_ >= ss_).astype(f), (1, 4))
    cm[:, 128:256] = np.eye(128, dtype=f)
    out["cmask"] = cm
    kk = np.arange(128)[:, None, None]
    jj = np.arange(4)[None, :, None]
    qq = np.arange(512)[None, None, :]
    out["masks"] = np.ascontiguousarray((qq >= kk + 128 * jj).astype(f).reshape(128, 2048))
    return out


def make_ctab(j):
    ct = np.zeros((128, 16), np.float32)
    i = np.arange(128) % 32
    ct[:, 0] = ((10000.0 ** (-(2.0 * i) / 64.0)) / (2.0 * np.pi)).astype(np.float32)
    ct[:, 1] = 0.25
    ct[:, 2] = np.where((np.arange(128) % 64) < 32, 0.5, 0.0)
    ct[:, 3] = float(j)
    return ct


def run_plan(plan, x, inputs, wts=None):
    if wts is None:
        wts = prep_weights(inputs)
    nc = build_program(plan)
    in_maps = []
    for c in range(8):
        b, j = c // 2, c % 2
        m = dict(wts)
        m["xT"] = np.ascontiguousarray(np.asarray(x[b, j * T:(j + 1) * T, :], np.float32).T)
        m["ctab"] = make_ctab(j)
        m["posb"] = np.ascontiguousarray(np.broadcast_to(np.asarray(inputs["positions"])[b, j * T:(j + 1) * T].astype(np.int32)[None, :], (64, T)))
        in_maps.append(m)
    res = run_bass_kernel_spmd(nc, in_maps, core_ids=list(range(8)))
    out = np.empty((4, 4096, D), np.float32)
    for c in range(8):
        b, j = c // 2, c % 2
        out[b, j * T:(j + 1) * T, :] = res.results[c]["outT"].T
    return out


def kernel(**inputs):
    x = np.asarray(inputs["x"], np.float32)
    return run_plan(FULL_PLAN, x, inputs)
```

```python
import os
import numpy as np
import concourse.bass as bass
import concourse.mybir as mybir
from concourse.bass_utils import run_bass_kernel_spmd

F32 = mybir.dt.float32
BF16 = mybir.dt.bfloat16
I32 = mybir.dt.int32
AF = mybir.ActivationFunctionType
ALU = mybir.AluOpType

D = 1024
T = 2048
TT = 512
NT = T // TT
KC = D // 128
DEPTH = 4
EPS = 1e-6
ENGS = ["pe", "act", "dve", "pool", "sp"]
SEM_LIM = 30000


class Prog:
    def __init__(self, nc):
        self.nc = nc
        self.ops = {e: [] for e in ENGS}
        self.last_w = {}
        self.readers = {}
        self.marked = set()
        self.dcount = {}
        self.last_compute = {e: None for e in ENGS}

    def _add(self, eng, fn, r, w, dma_sem=None):
        deps = set()
        for k in r:
            t = self.last_w.get(k)
            if t is not None:
                deps.add(t)
        for k in w:
            t = self.last_w.get(k)
            if t is not None:
                deps.add(t)
            rd = self.readers.get(k)
            if rd:
                deps.update(rd.values())
        idx = len(self.ops[eng])
        if dma_sem is None:
            tok = ("c", eng, idx)
            if eng == "pe":
                deps = {d for d in deps if not (d[0] == "c" and d[1] == eng)}
            self.last_compute[eng] = idx
        else:
            self.dcount[dma_sem] = self.dcount.get(dma_sem, 0) + 16
            tok = ("d", dma_sem, self.dcount[dma_sem])
            deps = {d for d in deps if not (d[0] == "d" and d[1] == dma_sem)}
        for d in deps:
            if d[0] == "c":
                self.marked.add((d[1], d[2]))
        self.ops[eng].append((fn, deps, tok))
        for k in w:
            self.last_w[k] = tok
            self.readers[k] = {}
        for k in r:
            rd = self.readers.setdefault(k, {})
            rd[tok if tok[0] == "d" else eng] = tok
        return tok

    def matmul(self, out, lhsT, rhs, start, stop, r, w):
        self._add("pe", lambda e: e.matmul(out, lhsT=lhsT, rhs=rhs, start=start, stop=stop), r, w)

    def transpose(self, out, in_, ident, r, w):
        self._add("pe", lambda e: e.transpose(out, in_, ident), r, w)

    def act(self, out, in_, func, r, w, bias=None, scale=None, eng="act"):
        kw = {}
        if bias is not None:
            kw["bias"] = bias
        if scale is not None:
            kw["scale"] = scale
        self._add("act", lambda e: e.activation(out=out, in_=in_, func=func, **kw), r, w)

    def ts(self, eng, out, in0, s1, s2, op0, op1, r, w):
        if op1 is None:
            self._add(eng, lambda e: e.tensor_scalar(out=out, in0=in0, scalar1=s1, scalar2=None, op0=op0), r, w)
        else:
            self._add(eng, lambda e: e.tensor_scalar(out=out, in0=in0, scalar1=s1, scalar2=s2, op0=op0, op1=op1), r, w)

    def tt(self, eng, out, in0, in1, op, r, w):
        self._add(eng, lambda e: e.tensor_tensor(out=out, in0=in0, in1=in1, op=op), r, w)

    def stt(self, eng, out, in0, scalar, in1, op0, op1, r, w):
        self._add(eng, lambda e: e.scalar_tensor_tensor(out=out, in0=in0, scalar=scalar, in1=in1, op0=op0, op1=op1), r, w)

    def copy(self, eng, out, in_, r, w):
        if eng == "act":
            self._add(eng, lambda e: e.copy(out=out, in_=in_), r, w)
        else:
            self._add(eng, lambda e: e.tensor_copy(out=out, in_=in_), r, w)

    def memset(self, eng, ap, val, w):
        self._add(eng, lambda e: e.memset(ap, val), [], w)

    def scan(self, eng, out, d0, d1, init, op0, op1, r, w):
        self._add(eng, lambda e: e.tensor_tensor_scan(out=out, data0=d0, data1=d1, initial=init, op0=op0, op1=op1), r, w)

    def recip(self, out, in_, r, w):
        self._add("dve", lambda e: e.reciprocal(out=out, in_=in_), r, w)

    def dma(self, q, out, in_, sem, r=(), w=()):
        return self._add(q, lambda e: e.dma_start(out=out, in_=in_), list(r), list(w), dma_sem=sem)

    def collective(self, kind, groups, in_ap, out_ap, sem, r, w):
        def fn(e):
            return e.collective_compute(kind, ALU.bypass, replica_groups=groups, ins=[in_ap], outs=[out_ap])
        eng = "pool"
        deps = set()
        for k in r:
            t = self.last_w.get(k)
            if t is not None:
                deps.add(t)
        for k in w:
            t = self.last_w.get(k)
            if t is not None:
                deps.add(t)
            rd = self.readers.get(k)
            if rd:
                deps.update(rd.values())
        self.dcount[sem] = self.dcount.get(sem, 0) + 1
        tok = ("d", sem, self.dcount[sem])
        for d in deps:
            if d[0] == "c":
                self.marked.add((d[1], d[2]))
        self.ops[eng].append((fn, deps, ("cc", sem, self.dcount[sem])))
        for k in w:
            self.last_w[k] = tok
            self.readers[k] = {}
        for k in r:
            self.readers.setdefault(k, {})[tok] = tok
        return tok

    def barrier(self):
        toks = []
        for e in ENGS:
            i = self.last_compute[e]
            if i is not None:
                toks.append(("c", e, i))
                self.marked.add((e, i))
        for e in ENGS:
            deps = {t for t in toks if t[1] != e}
            if deps:
                self.ops[e].append((None, deps, ("c", e, len(self.ops[e]))))
        self.last_w = {k: t for k, t in self.last_w.items() if t[0] == "d"}
        newr = {}
        for k, rd in self.readers.items():
            rd2 = {a: t for a, t in rd.items() if t[0] == "d"}
            if rd2:
                newr[k] = rd2
        self.readers = newr

    def wait_all_dma(self, eng="sp"):
        deps = {("d", s, c) for s, c in self.dcount.items()}
        self.ops[eng].append((None, deps, ("c", eng, len(self.ops[eng]))))

    def emit(self):
        nc = self.nc
        ordinal = {}
        nmarked = {}
        for e in ENGS:
            c = 0
            for i in range(len(self.ops[e])):
                if (e, i) in self.marked:
                    c += 1
                    ordinal[(e, i)] = c
            nmarked[e] = c
        esem = {}
        for e in ENGS:
            nep = (nmarked[e] + SEM_LIM - 1) // SEM_LIM
            esem[e] = [nc.alloc_semaphore(f"s_{e}_{j}") for j in range(max(nep, 1))]
        dsem = {name: nc.alloc_semaphore(f"d_{name}") for name in self.dcount}

        def run(eng_name, e):
            waited_c = {}
            waited_d = {}
            for i, (fn, deps, tok) in enumerate(self.ops[eng_name]):
                for d in sorted(deps, key=lambda x: (x[0], str(x[1]), x[2])):
                    if d[0] == "c":
                        o = ordinal[(d[1], d[2])]
                        if waited_c.get(d[1], 0) >= o:
                            continue
                        waited_c[d[1]] = o
                        ep = (o - 1) // SEM_LIM
                        e.wait_ge(esem[d[1]][ep], o - ep * SEM_LIM)
                    else:
                        if waited_d.get(d[1], 0) >= d[2]:
                            continue
                        waited_d[d[1]] = d[2]
                        e.wait_ge(dsem[d[1]], d[2])
                if fn is None:
                    continue
                ins = fn(e)
                if tok[0] == "d":
                    ins.then_inc(dsem[tok[1]], 16)
                elif tok[0] == "cc":
                    ins.then_inc(dsem[tok[1]])
                elif (eng_name, i) in self.marked:
                    o = ordinal[(eng_name, i)]
                    ep = (o - 1) // SEM_LIM
                    ins.then_inc(esem[eng_name][ep], 1)

        with nc.Block() as block:
            @block.tensor
            def _(e):
                run("pe", e)

            @block.scalar
            def _(e):
                run("act", e)

            @block.vector
            def _(e):
                run("dve", e)

            @block.gpsimd
            def _(e):
                run("pool", e)

            @block.sync
            def _(e):
                run("sp", e)


def build_program(plan, in_name="xT"):
    nc = bass.Bass("TRN2", target_bir_lowering=False)
    P = Prog(nc)

    def din(name, shape, dt=F32):
        return nc.dram_tensor(name, shape, dt, kind="ExternalInput").ap()

    xT_d = din("xT", [D, T])
    outT_d = nc.dram_tensor("outT", [D, T], F32, kind="ExternalOutput").ap()
    gains_d = din("gains", [128, 128])
    ctab_d = din("ctab", [128, 16])
    w1r_d = din("w1r", [DEPTH, 16, 128, 8 * 256])
    w2r_d = din("w2r", [DEPTH, 8, 128, 32 * 128])
    posb_d = din("posb", [64, T], I32)
    masks_d = din("masks", [128, 4 * 512])
    mla_win_d = din("mla_win", [2, 128, 8 * 896])
    mla_g_d = din("mla_g", [128, 16])
    mla_wh_d = din("mla_wh", [2, 8, 128, 1536])
    mla_wo_d = din("mla_wo", [2, 128, 8 * 1024])
    hg_win_d = din("hg_win", [2, 32, 128, 8 * 128])
    hg_wo_d = din("hg_wo", [2, 128, 8 * 1024])
    hg_c_d = din("hg_c", [128, 64])
    cmask_d = din("cmask", [128, 128 + 128])
    sin_d = [nc.dram_tensor(f"sin{i}", [128, 1024], F32).ap() for i in range(2)]
    sout_d = [nc.dram_tensor(f"sout{i}", [256, 1024], F32).ap() for i in range(2)]
    xin_d = [nc.dram_tensor(f"xin{i}", [320, T], BF16).ap() for i in range(2)]
    xout_d = [nc.dram_tensor(f"xout{i}", [640, T], BF16).ap() for i in range(2)]
    PAIRS = [[0, 1], [2, 3], [4, 5], [6, 7]]

    hT = nc.alloc_sbuf_tensor("hT", [128, KC, T], F32)
    gains = nc.alloc_sbuf_tensor("gains_sb", [128, 128], F32)
    ctab = nc.alloc_sbuf_tensor("ctab_sb", [128, 16], F32)
    ones_f = nc.alloc_sbuf_tensor("ones_f", [128, 128], F32)
    mla_g = nc.alloc_sbuf_tensor("mla_g_sb", [128, 16], F32)
    hg_c = nc.alloc_sbuf_tensor("hg_c_sb", [128, 64], F32)
    lbt = nc.alloc_sbuf_tensor("lbt", [128, 64], F32)
    cmask_f = nc.alloc_sbuf_tensor("cmask_f", [128, 256], F32)
    cmask = nc.alloc_sbuf_tensor("cmask_b", [128, 256], BF16)
    ones = {}
    for nm in (1024, 512, 256, 128, 1):
        ones[nm] = nc.alloc_sbuf_tensor(f"ones{nm}", [128, 128], BF16)
    ARENA_W = 35500
    arena = nc.alloc_sbuf_tensor("arena", [128, ARENA_W], F32)
    PS = nc.alloc_psum_tensor("PS", [128, 8, 512], F32)

    def carve(off_w, n_elem, dt):
        nw = n_elem if dt in (F32, I32) else (n_elem + 1) // 2
        ap = arena[:, off_w:off_w + nw]
        if dt != F32:
            ap = ap.bitcast(dt)
        return ap

    P.dma("sp", out=gains[:], in_=gains_d, sem="c0", w=["gains"])
    P.dma("sp", out=ctab[:], in_=ctab_d, sem="c1", w=["ctab"])
    P.dma("sp", out=mla_g[:], in_=mla_g_d, sem="c2", w=["mla_g"])
    P.dma("sp", out=hg_c[:], in_=hg_c_d, sem="c3", w=["hg_c"])
    P.dma("sp", out=cmask_f[:], in_=cmask_d, sem="c4", w=["cmask_f"])
    P.copy("dve", cmask[:], cmask_f[:], r=["cmask_f"], w=["cmask"])
    lg = hg_c[:, 0:32].rearrange("p (l h) -> p l h", l=4)
    mx = lbt[:, 48:56]
    P.tt("dve", mx, lg[:, 0, :], lg[:, 1, :], ALU.max, r=["hg_c"], w=["lb_mx"])
    P.tt("dve", mx, mx, lg[:, 2, :], ALU.max, r=["hg_c", "lb_mx"], w=["lb_mx"])
    P.tt("dve", mx, mx, lg[:, 3, :], ALU.max, r=["hg_c", "lb_mx"], w=["lb_mx"])
    ex = cmask_f[:, 0:32].rearrange("p (l h) -> p l h", l=4)
    for li_ in range(4):
        P.tt("dve", ex[:, li_, :], lg[:, li_, :], mx, ALU.subtract, r=["hg_c", "lb_mx", "cmask"], w=[("lb_ex", li_)])
    P.act(cmask_f[:, 0:32], cmask_f[:, 0:32], AF.Exp, r=[("lb_ex", i) for i in range(4)], w=["lb_e"])
    sm = lbt[:, 56:64]
    P.tt("dve", sm, ex[:, 0, :], ex[:, 1, :], ALU.add, r=["lb_e"], w=["lb_sm"])
    P.tt("dve", sm, sm, ex[:, 2, :], ALU.add, r=["lb_e", "lb_sm"], w=["lb_sm"])
    P.tt("dve", sm, sm, ex[:, 3, :], ALU.add, r=["lb_e", "lb_sm"], w=["lb_sm"])
    P.recip(sm, sm, r=["lb_sm"], w=["lb_sm"])
    P.tt("dve", lbt[:, 0:8], ex[:, 1, :], sm, ALU.mult, r=["lb_e", "lb_sm"], w=["lbt0"])
    P.tt("dve", lbt[:, 8:16], ex[:, 2, :], ex[:, 3, :], ALU.add, r=["lb_e"], w=["lbt1"])
    P.tt("dve", lbt[:, 8:16], lbt[:, 8:16], sm, ALU.mult, r=["lbt1", "lb_sm"], w=["lbt1"])
    P.tt("dve", lbt[:, 8:16], lbt[:, 8:16], lbt[:, 0:8], ALU.add, r=["lbt1", "lbt0"], w=["lbt1"])
    P.ts("dve", lbt[:, 16:32], lbt[:, 0:16], -1.0, 1.0, ALU.mult, ALU.add, r=["lbt0", "lbt1"], w=["oml"])
    P.ts("dve", lbt[:, 32:48], lbt[:, 16:32], -1.0, None, ALU.mult, None, r=["oml"], w=["noml"])
    for k in range(KC):
        P.dma("sp", out=hT[:, k, :], in_=xT_d[k * 128:(k + 1) * 128, :], sem="xin", w=[("h", kk, t) for kk in range(KC) for t in range(NT)])
    for nm in ones:
        P.memset("pool", ones_f[:], 1.0 / nm, w=["ones_f"])
        P.copy("pool", ones[nm][:], ones_f[:], r=["ones_f"], w=[("ones", nm)])

    state = {"ps": 0}

    def next_ps(lo=0, hi=6):
        b = lo + state["ps"] % (hi - lo)
        state["ps"] += 1
        return b

    def rstd_from(ss_bank, rstd_ap, rstd_key):
        P.ts("dve", rstd_ap, PS[:, ss_bank, :], EPS, None, ALU.add, None, r=[("ps", ss_bank)], w=[rstd_key])
        P.act(rstd_ap, rstd_ap, AF.Sqrt, r=[rstd_key], w=[rstd_key])
        P.recip(rstd_ap, rstd_ap, r=[rstd_key], w=[rstd_key])

    def norm_pre(tt, gi, dst, dst_key, sq, rstd, ss_bank):
        tok = slice(tt * TT, (tt + 1) * TT)
        for k in range(KC):
            s = sq[k % len(sq)]
            P.act(s[0], hT[:, k, tok], AF.Square, r=[("h", k, tt)], w=[s[1]])
            P.matmul(PS[:, ss_bank, :], ones[1024][:], s[0], k == 0, k == KC - 1, r=[s[1], ("ones", 1024)], w=[("ps", ss_bank)])
        rstd_from(ss_bank, rstd[0], rstd[1])
        for k in range(KC):
            P.stt("dve", dst[:, k, :], hT[:, k, tok], gains[:, gi * 8 + k:gi * 8 + k + 1], rstd[0], ALU.mult, ALU.mult,
                  r=[("h", k, tt), "gains", rstd[1]], w=[(dst_key, k)])

    def mlp_layer(l):
        P.barrier()
        o = 0
        actT = carve(o, 32 * 1024, BF16).rearrange("p (f t) -> p f t", f=32); o += 16384
        mT = carve(o, 2 * 8 * 512, F32).rearrange("p (a k t) -> p a k t", a=2, k=8); o += 8192
        aTh = carve(o - 8192, 8 * 1024, BF16).rearrange("p (k t) -> p k t", k=8)
        W1s = []
        for i in range(3):
            W1s.append(carve(o, 8 * 256, BF16).rearrange("p (k c) -> p k c", k=8)); o += 1024
        W2s = []
        for i in range(2):
            W2s.append(carve(o, 32 * 128, BF16).rearrange("p (f c) -> p f c", f=32)); o += 2048
        tmp = []
        for i in range(3):
            tmp.append((carve(o, 512, F32), ("tmp", i))); o += 512
        sq = []
        for i in range(4):
            sq.append((carve(o, 512, BF16), ("sq", i))); o += 256
        rstd = []
        for i in range(2):
            rstd.append((carve(o, 512, F32), ("rstd", i))); o += 512
        assert o <= ARENA_W, o
        tc = 0

        def load_w1(hf_, g):
            slot = g % 3
            P.dma("pool", out=W1s[slot], in_=w1r_d[l, g].rearrange("p (k c) -> p k c", k=8), sem=f"w1_{slot}", w=[("w1", slot)])

        def load_w2(oc):
            slot = oc % 2
            for q in range(4):
                P.dma("pool", out=W2s[slot][:, q * 8:(q + 1) * 8, :],
                      in_=w2r_d[l, oc][:, q * 1024:(q + 1) * 1024].rearrange("p (f c) -> p f c", f=8),
                      sem=f"w2_{slot}", w=[("w2", slot, qq) for qq in range(4)])

        for g0 in range(3):
            load_w1(0, g0)
        for hf in range(2):
            if hf == 1:
                P.barrier()
            for t2 in range(2):
                norm_pre(hf * 2 + t2, l * 4 + 2, aTh[:, :, t2 * 512:(t2 + 1) * 512], ("aTh", t2), sq, rstd[t2], 6 + t2)
            load_w2(0)
            load_w2(1)
            for g in range(16):
                slot = g % 3
                for fi in range(2):
                    f = g * 2 + fi
                    for t2 in range(2):
                        b = next_ps()
                        for k in range(KC):
                            P.matmul(PS[:, b, :], W1s[slot][:, k, fi * 128:(fi + 1) * 128], aTh[:, k, t2 * 512:(t2 + 1) * 512],
                                     k == 0, k == KC - 1, r=[("w1", slot), (("aTh", t2), k)], w=[("ps", b)])
                        tm = tmp[tc % 3]
                        tc += 1
                        P.act(tm[0], PS[:, b, :], AF.Relu, r=[("ps", b)], w=[tm[1]])
                        P.tt("dve", actT[:, f, t2 * 512:(t2 + 1) * 512], tm[0], tm[0], ALU.mult, r=[tm[1]], w=[("act", f, t2)])
                if g + 3 < 16:
                    load_w1(hf, g + 3)
            P.barrier()
            if hf == 0:
                for g0 in range(3):
                    load_w1(1, g0)
            pend = []
            for oc in range(8):
                slot = oc % 2
                for t2 in range(2):
                    b = next_ps()
                    for f in range(32):
                        P.matmul(PS[:, b, :], W2s[slot][:, f, :], actT[:, f, t2 * 512:(t2 + 1) * 512], f == 0, f == 31,
                                 r=[("w2", slot, f // 8), ("act", f, t2)], w=[("ps", b)])
                    for (pb, pt2, poc, ps_) in pend:
                        P.matmul(PS[:, 6 + pt2, :], ones[1024][:], ps_[0], poc == 0, poc == 7, r=[ps_[1], ("ones", 1024)], w=[("ps", 6 + pt2)])
                    pend = []
                    s = sq[(oc * 2 + t2) % 4]
                    P.act(mT[:, t2, oc, :], PS[:, b, :], AF.Copy, r=[("ps", b)], w=[("mT", t2, oc)])
                    P.act(s[0], PS[:, b, :], AF.Square, r=[("ps", b)], w=[s[1]])
                    pend.append((b, t2, oc, s))
                if oc + 2 < 8:
                    load_w2(oc + 2)
            for (pb, pt2, poc, ps_) in pend:
                P.matmul(PS[:, 6 + pt2, :], ones[1024][:], ps_[0], poc == 0, poc == 7, r=[ps_[1], ("ones", 1024)], w=[("ps", 6 + pt2)])
            gi = l * 4 + 3
            for t2 in range(2):
                tt_ = hf * 2 + t2
                tok = slice(tt_ * TT, (tt_ + 1) * TT)
                rstd_from(6 + t2, rstd[t2][0], rstd[t2][1])
                for oc in range(8):
                    tm = tmp[tc % 3]
                    tc += 1
                    P.stt("dve", tm[0], mT[:, t2, oc, :], gains[:, gi * 8 + oc:gi * 8 + oc + 1], rstd[t2][0], ALU.mult, ALU.mult,
                          r=[("mT", t2, oc), "gains", rstd[t2][1]], w=[tm[1]])
                    P.tt("dve", hT[:, oc, tok], hT[:, oc, tok], tm[0], ALU.add, r=[("h", oc, tt_), tm[1]], w=[("h", oc, tt_)])

    def out_proj_post(l, wo, wkey, oT, okey, mT, sqC, rstdC, tC):
        gi = l * 4 + 1
        tcn = 0
        for tt in range(NT):
            tok = slice(tt * TT, (tt + 1) * TT)
            pend = []
            for oc in range(8):
                b = next_ps()
                for c in range(8):
                    P.matmul(PS[:, b, :], wo[:, c, oc * 128:(oc + 1) * 128], oT[:, c, tok], c == 0, c == 7,
                             r=[wkey, (okey, c, tt)], w=[("ps", b)])
                for (poc, ps_) in pend:
                    P.matmul(PS[:, 7, :], ones[1024][:], ps_[0], poc == 0, poc == 7, r=[ps_[1], ("ones", 1024)], w=[("ps", 7)])
                pend = []
                sq_ = sqC[oc % 4]
                P.act(mT[:, oc, :], PS[:, b, :], AF.Copy, r=[("ps", b)], w=[("mT", oc)])
                P.act(sq_[0], PS[:, b, :], AF.Square, r=[("ps", b)], w=[sq_[1]])
                pend.append((oc, sq_))
            for (poc, ps_) in pend:
                P.matmul(PS[:, 7, :], ones[1024][:], ps_[0], poc == 0, poc == 7, r=[ps_[1], ("ones", 1024)], w=[("ps", 7)])
            rstd_from(7, rstdC[0], rstdC[1])
            for oc in range(8):
                tm = tC[tcn % 3]; tcn += 1
                P.stt("dve", tm[0], mT[:, oc, :], gains[:, gi * 8 + oc:gi * 8 + oc + 1], rstdC[0], ALU.mult, ALU.mult,
                      r=[("mT", oc), "gains", rstdC[1]], w=[tm[1]])
                P.tt("pool", hT[:, oc, tok], hT[:, oc, tok], tm[0], ALU.add, r=[("h", oc, tt), tm[1]], w=[("h", oc, tt)])

    SCALE = float(192 ** -0.5)
    TWO_PI = float(2.0 * np.pi)

    def mla_layer(l):
        s_ = l // 2
        P.barrier()
        o = 0
        cqn = carve(o, 4 * T, BF16).rearrange("p (c t) -> p c t", c=4); o += 4096
        ckv_own = carve(o, 2 * T, BF16).rearrange("p (c t) -> p c t", c=2); o += 2048
        ckv_pre = carve(o, 2 * T, BF16).rearrange("p (c t) -> p c t", c=2); o += 2048
        kr_own = carve(o, T, BF16); o += 1024
        kr_pre = carve(o, T, BF16); o += 1024
        cs = carve(o, T, BF16); o += 1024
        sn = carve(o, T, BF16); o += 1024
        masks = carve(o, 4 * 512, BF16).rearrange("p (j q) -> p j q", j=4); o += 1024
        whs = []
        for i in range(2):
            whs.append(carve(o, 1536, BF16)); o += 768
        visones = carve(o, 128, BF16); o += 64
        oT_off = o
        oT = carve(o, 8 * T, BF16).rearrange("p (c t) -> p c t", c=8); o += 8192
        base = o
        Kh = carve(o, 2 * T, BF16); o += 2048
        Vh = carve(o, 32 * 128, BF16).rearrange("p (b d) -> p b d", b=32); o += 2048
        qn = carve(o, T, BF16); o += 1024
        qr = carve(o, T, BF16); o += 1024
        PT = []
        for i in range(4):
            PT.append((carve(o, 512, BF16), ("PT", i))); o += 256
        rden = []
        for i in range(2):
            rden.append((carve(o, 512, F32), ("rden", i))); o += 512
        tq_ = []
        for i in range(4):
            tq_.append((carve(o, 512, F32), ("tq", i))); o += 512
        assert o <= ARENA_W, o
        a = oT_off
        aT = carve(a, 8 * 512, BF16).rearrange("p (k t) -> p k t", k=8); a += 2048
        cq_raw = carve(a, 4 * 512, F32).rearrange("p (c t) -> p c t", c=4); a += 2048
        ckv_raw = carve(a, 2 * 512, F32).rearrange("p (c t) -> p c t", c=2); a += 1024
        sqA = []
        for i in range(4):
            sqA.append((carve(a, 512, BF16), ("sqA", i))); a += 256
        rstdA = []
        for i in range(3):
            rstdA.append((carve(a, 512, F32), ("rstdA", i))); a += 512
        tA = []
        for i in range(2):
            tA.append((carve(a, 512, F32), ("tA", i))); a += 512
        posi = carve(a, T, I32); a += 2048
        yv = carve(a, T, F32); a += 2048
        kf = carve(a, T, F32); a += 2048
        ki = carve(a, T, I32); a += 2048
        win = carve(a, 8 * 896, BF16).rearrange("p (k c) -> p k c", k=8); a += 3584
        assert a <= ARENA_W, a

        P.dma("pool", out=win, in_=mla_win_d[s_].rearrange("p (k c) -> p k c", k=8), sem="win", w=["win"])
        P.dma("pool", out=masks, in_=masks_d.rearrange("p (j q) -> p j q", j=4), sem="masks", w=["masks"])
        P.dma("sp", out=posi[0:64, :], in_=posb_d, sem="posi", w=["posi"])
        P.ts("dve", visones, ones[1][:], ctab[:, 3:4], None, ALU.mult, None, r=[("ones", 1), "ctab"], w=["visones"])
        posf = kf
        P.copy("dve", posf[0:64, :], posi[0:64, :], r=["posi"], w=["posf"])
        for (tbl, col, key) in ((cs, 1, "cs"), (sn, 2, "sn")):
            P.ts("dve", yv[0:64, :], posf[0:64, :], ctab[0:64, 0:1], ctab[0:64, col:col + 1], ALU.mult, ALU.add,
                 r=["posf", "ctab"], w=["yv"])
            P.copy("dve", ki[0:64, :], yv[0:64, :], r=["yv"], w=["ki"])
            kf2 = carve(oT_off, T, F32)
            P.copy("dve", kf2[0:64, :], ki[0:64, :], r=["ki"], w=["kf2"])
            P.tt("dve", yv[0:64, :], yv[0:64, :], kf2[0:64, :], ALU.subtract, r=["yv", "kf2"], w=["yv"])
            P.ts("dve", kf2[0:64, :], yv[0:64, :], 0.5, None, ALU.is_gt, None, r=["yv"], w=["kf2"])
            P.tt("dve", yv[0:64, :], yv[0:64, :], kf2[0:64, :], ALU.subtract, r=["yv", "kf2"], w=["yv"])
            P.ts("dve", kf2[0:64, :], yv[0:64, :], -0.5, None, ALU.is_lt, None, r=["yv"], w=["kf2"])
            P.tt("dve", yv[0:64, :], yv[0:64, :], kf2[0:64, :], ALU.add, r=["yv", "kf2"], w=["yv"])
            P.act(tbl[0:64, :], yv[0:64, :], AF.Sin, scale=TWO_PI, r=["yv"], w=[key])
        P.barrier()

        gq = 8 * s_
        for tt in range(NT):
            tok = slice(tt * TT, (tt + 1) * TT)
            norm_pre(tt, l * 4 + 0, aT, "aT", sqA, rstdA[0], 6)
            for c in range(4):
                b = next_ps()
                for k in range(KC):
                    P.matmul(PS[:, b, :], win[:, k, c * 128:(c + 1) * 128], aT[:, k, :], k == 0, k == KC - 1,
                             r=["win", ("aT", k)], w=[("ps", b)])
                sq_ = sqA[c % 4]
                P.act(cq_raw[:, c, :], PS[:, b, :], AF.Copy, r=[("ps", b)], w=[("cq_raw", c)])
                P.act(sq_[0], PS[:, b, :], AF.Square, r=[("ps", b)], w=[sq_[1]])
                P.matmul(PS[:, 7, :], ones[512][:], sq_[0], c == 0, c == 3, r=[sq_[1], ("ones", 512)], w=[("ps", 7)])
            rstd_from(7, rstdA[1][0], rstdA[1][1])
            for c in range(4):
                P.stt("dve", cqn[:, c, tok], cq_raw[:, c, :], mla_g[:, gq + c:gq + c + 1], rstdA[1][0], ALU.mult, ALU.mult,
                      r=[("cq_raw", c), "mla_g", rstdA[1][1]], w=[("cqn", c, tt)])
            for c in range(2):
                b = next_ps()
                for k in range(KC):
                    P.matmul(PS[:, b, :], win[:, k, 512 + c * 128:512 + (c + 1) * 128], aT[:, k, :], k == 0, k == KC - 1,
                             r=["win", ("aT", k)], w=[("ps", b)])
                sq_ = sqA[c % 4]
                P.act(ckv_raw[:, c, :], PS[:, b, :], AF.Copy, r=[("ps", b)], w=[("ckv_raw", c)])
                P.act(sq_[0], PS[:, b, :], AF.Square, r=[("ps", b)], w=[sq_[1]])
                P.matmul(PS[:, 6, :], ones[256][:], sq_[0], c == 0, c == 1, r=[sq_[1], ("ones", 256)], w=[("ps", 6)])
            rstd_from(6, rstdA[2][0], rstdA[2][1])
            for c in range(2):
                P.stt("dve", ckv_own[:, c, tok], ckv_raw[:, c, :], mla_g[:, gq + 4 + c:gq + 5 + c], rstdA[2][0], ALU.mult, ALU.mult,
                      r=[("ckv_raw", c), "mla_g", rstdA[2][1]], w=[("ckv_own", c, tt)])
            b1 = next_ps()
            b2 = next_ps()
            for (bb, c0) in ((b1, 768), (b2, 832)):
                for k in range(KC):
                    P.matmul(PS[0:64, bb, :], win[:, k, c0:c0 + 64], aT[:, k, :], k == 0, k == KC - 1,
                             r=["win", ("aT", k)], w=[("ps", bb)])
            P.tt("dve", tA[0][0][0:64, :], PS[0:64, b1, :], cs[0:64, tok], ALU.mult, r=[("ps", b1), "cs"], w=[tA[0][1]])
            P.tt("dve", tA[1][0][0:64, :], PS[0:64, b2, :], sn[0:64, tok], ALU.mult, r=[("ps", b2), "sn"], w=[tA[1][1]])
            P.tt("pool", kr_own[0:64, tok], tA[0][0][0:64, :], tA[1][0][0:64, :], ALU.add, r=[tA[0][1], tA[1][1]], w=[("kr_own", tt)])

        if os.environ.get("KDBG") == "A":
            return
        li = s_
        for c in range(2):
            P.dma("sp", out=xin_d[li][c * 128:(c + 1) * 128, :], in_=ckv_own[:, c, :], sem=f"xi{li}_{c}",
                  r=[("ckv_own", c, t) for t in range(NT)], w=[("xin", li, c)])
        P.dma("sp", out=xin_d[li][256:320, :], in_=kr_own[0:64, :], sem=f"xi{li}_2",
              r=[("kr_own", t) for t in range(NT)], w=[("xin", li, 2)])
        P.collective("AllGather", PAIRS, xin_d[li].opt(), xout_d[li].opt(), f"cc{li}",
                     r=[("xin", li, c) for c in range(3)], w=[("xout", li)])
        for c in range(2):
            P.dma("sp", out=ckv_pre[:, c, :], in_=xout_d[li][c * 128:(c + 1) * 128, :], sem=f"xo{li}_{c}",
                  r=[("xout", li)], w=[("ckv_pre", c)])
        P.dma("sp", out=kr_pre[0:64, :], in_=xout_d[li][256:320, :], sem=f"xo{li}_2", r=[("xout", li)], w=["kr_pre"])
        P.barrier()
        if os.environ.get("KDBG") == "X":
            return

        ev = 0
        for h in range(1 if os.environ.get("KDBG") in ("B1", "B2") else 8):
            wh = whs[h % 2]
            wkey = ("wh", h % 2)
            P.dma("pool", out=wh, in_=mla_wh_d[s_, h], sem=f"wh{h % 2}", w=[wkey])
            uk = wh[:, 0:256].rearrange("p (c d) -> p c d", c=2)
            uv = wh[:, 256:512].rearrange("p (c d) -> p c d", c=2)
            uqn = wh[:, 512:1024].rearrange("p (c d) -> p c d", c=4)
            uqr = wh[:, 1024:1280].rearrange("p (c d) -> p c d", c=4)
            uqrs = wh[:, 1280:1536].rearrange("p (c d) -> p c d", c=4)
            for t8 in range(8):
                src, skey = (ckv_pre, "ckv_pre") if t8 < 4 else (ckv_own, "ckv_own")
                tl = t8 % 4
                b = next_ps(0, 3)
                for c in range(2):
                    rk = (skey, c) if t8 < 4 else (skey, c, tl)
                    P.matmul(PS[:, b, :], uk[:, c, :], src[:, c, tl * 512:(tl + 1) * 512], c == 0, c == 1,
                             r=[wkey, rk], w=[("ps", b)])
                P.copy("dve", Kh[:, t8 * 512:(t8 + 1) * 512], PS[:, b, :], r=[("ps", b)], w=[("Kh", t8)])
            for g in range(8):
                b = next_ps(0, 3)
                for bi in range(4):
                    blk = g * 4 + bi
                    src, skey = (ckv_pre, "ckv_pre") if blk < 16 else (ckv_own, "ckv_own")
                    lb_ = blk % 16
                    for c in range(2):
                        rk = (skey, c) if blk < 16 else (skey, c, lb_ // 4)
                        P.matmul(PS[:, b, bi * 128:(bi + 1) * 128], src[:, c, lb_ * 128:(lb_ + 1) * 128], uv[:, c, :], c == 0, c == 1,
                                 r=[wkey, rk], w=[("ps", b)])
                dst = Vh[:, g * 4:(g + 1) * 4, :]
                srcp = PS[:, b, :].rearrange("p (b d) -> p b d", b=4)
                if g < 4:
                    P.ts("dve", dst, srcp, ctab[:, 3:4], None, ALU.mult, None, r=[("ps", b), "ctab"], w=[("Vh", g)])
                else:
                    P.copy("dve", dst, srcp, r=[("ps", b)], w=[("Vh", g)])
            for tt in range(NT):
                tok = slice(tt * TT, (tt + 1) * TT)
                b = next_ps(0, 3)
                for c in range(4):
                    P.matmul(PS[:, b, :], uqn[:, c, :], cqn[:, c, tok], c == 0, c == 3, r=[wkey, ("cqn", c, tt)], w=[("ps", b)])
                P.copy("dve", qn[:, tok], PS[:, b, :], r=[("ps", b)], w=[("qn", tt)])
                b1 = next_ps(0, 3)
                for c in range(4):
                    P.matmul(PS[0:64, b1, :], uqr[:, c, :], cqn[:, c, tok], c == 0, c == 3, r=[wkey, ("cqn", c, tt)], w=[("ps", b1)])
                t1 = tq_[ev % 4]; ev += 1
                P.tt("dve", t1[0][0:64, :], PS[0:64, b1, :], cs[0:64, tok], ALU.mult, r=[("ps", b1), "cs"], w=[t1[1]])
                b2 = next_ps(0, 3)
                for c in range(4):
                    P.matmul(PS[0:64, b2, :], uqrs[:, c, :], cqn[:, c, tok], c == 0, c == 3, r=[wkey, ("cqn", c, tt)], w=[("ps", b2)])
                t2 = tq_[ev % 4]; ev += 1
                P.tt("dve", t2[0][0:64, :], PS[0:64, b2, :], sn[0:64, tok], ALU.mult, r=[("ps", b2), "sn"], w=[t2[1]])
                P.tt("pool", qr[0:64, tok], t1[0][0:64, :], t2[0][0:64, :], ALU.add, r=[t1[1], t2[1]], w=[("qr", tt)])
            for tq in range(0 if os.environ.get("KDBG") == "B1" else NT):
                qtok = slice(tq * TT, (tq + 1) * TT)
                ab = (h * NT + tq) % 2
                bo, bd = 3 + 2 * ab, 4 + 2 * ab
                units = [(True, kb) for kb in range(16)] + [(False, kb) for kb in range(4 * tq + 4)]
                n = len(units)
                pend = None
                for u in range(n + 1):
                    if u < n:
                        pre, kb = units[u]
                        col = kb * 128 if pre else T + kb * 128
                        bs = next_ps(0, 3)
                        krs, krk = (kr_pre, "kr_pre") if pre else (kr_own, ("kr_own", kb // 4))
                        P.matmul(PS[:, bs, :], Kh[:, col:col + 128], qn[:, qtok], True, False,
                                 r=[("Kh", col // 512), ("qn", tq)], w=[("ps", bs)])
                        P.matmul(PS[:, bs, :], krs[0:64, kb * 128:(kb + 1) * 128], qr[0:64, qtok], False, True,
                                 r=[krk, ("qr", tq)], w=[("ps", bs)])
                        pt = PT[u % 4]
                        P.act(pt[0], PS[:, bs, :], AF.Exp, scale=SCALE, r=[("ps", bs)], w=[pt[1]])
                        if (not pre) and kb >= 4 * tq:
                            P.tt("pool", pt[0], pt[0], masks[:, kb - 4 * tq, :], ALU.mult, r=[pt[1], "masks"], w=[pt[1]])
                    if pend is not None:
                        (ppre, pkb, ppt, pu) = pend
                        vb = pkb if ppre else 16 + pkb
                        P.matmul(PS[:, bo, :], Vh[:, vb, :], ppt[0], pu == 0, pu == n - 1, r=[("Vh", vb // 4), ppt[1]], w=[("ps", bo)])
                        lh = visones if ppre else ones[1][:]
                        P.matmul(PS[:, bd, :], lh, ppt[0], pu == 0, pu == n - 1,
                                 r=["visones" if ppre else ("ones", 1), ppt[1]], w=[("ps", bd)])
                    pend = (pre, kb, pt, u) if u < n else None
                rd = rden[ab]
                P.recip(rd[0], PS[:, bd, :], r=[("ps", bd)], w=[rd[1]])
                P.tt("dve", oT[:, h, qtok], PS[:, bo, :], rd[0], ALU.mult, r=[("ps", bo), rd[1]], w=[("oT", h, tq)])
        P.barrier()
        if os.environ.get("KDBG") in ("B", "B1", "B2"):
            return

        wo = carve(0, 8 * 1024, BF16).rearrange("p (c n) -> p c n", c=8)
        c_ = base
        mT = carve(c_, 8 * 512, F32).rearrange("p (k t) -> p k t", k=8); c_ += 4096
        sqC = []
        for i in range(4):
            sqC.append((carve(c_, 512, BF16), ("sqC", i))); c_ += 256
        rstdC = (carve(c_, 512, F32), "rstdC"); c_ += 512
        tC = []
        for i in range(3):
            tC.append((carve(c_, 512, F32), ("tC", i))); c_ += 512
        assert c_ <= ARENA_W
        P.dma("pool", out=wo, in_=mla_wo_d[s_].rearrange("p (c n) -> p c n", c=8), sem="wo", w=["wo"])
        gi = l * 4 + 1
        tcn = 0
        for tt in range(NT):
            tok = slice(tt * TT, (tt + 1) * TT)
            pend = []
            for oc in range(8):
                b = next_ps()
                for c in range(8):
                    P.matmul(PS[:, b, :], wo[:, c, oc * 128:(oc + 1) * 128], oT[:, c, tok], c == 0, c == 7,
                             r=["wo", ("oT", c, tt)], w=[("ps", b)])
                for (poc, ps_) in pend:
                    P.matmul(PS[:, 7, :], ones[1024][:], ps_[0], poc == 0, poc == 7, r=[ps_[1], ("ones", 1024)], w=[("ps", 7)])
                pend = []
                sq_ = sqC[oc % 4]
                P.act(mT[:, oc, :], PS[:, b, :], AF.Copy, r=[("ps", b)], w=[("mT", oc)])
                P.act(sq_[0], PS[:, b, :], AF.Square, r=[("ps", b)], w=[sq_[1]])
                pend.append((oc, sq_))
            for (poc, ps_) in pend:
                P.matmul(PS[:, 7, :], ones[1024][:], ps_[0], poc == 0, poc == 7, r=[ps_[1], ("ones", 1024)], w=[("ps", 7)])
            rstd_from(7, rstdC[0], rstdC[1])
            for oc in range(8):
                tm = tC[tcn % 3]; tcn += 1
                P.stt("dve", tm[0], mT[:, oc, :], gains[:, gi * 8 + oc:gi * 8 + oc + 1], rstdC[0], ALU.mult, ALU.mult,
                      r=[("mT", oc), "gains", rstdC[1]], w=[tm[1]])
                P.tt("pool", hT[:, oc, tok], hT[:, oc, tok], tm[0], ALU.add, r=[("h", oc, tt), tm[1]], w=[("h", oc, tt)])


    def hgrn_layer(l):
        s_ = l // 2
        P.barrier()
        HW = 1024
        o = 0
        aT = carve(o, 8 * T, BF16).rearrange("p (k t) -> p k t", k=8); o += 8192
        ogT = carve(o, 8 * T, BF16).rearrange("p (k t) -> p k t", k=8); o += 8192
        qf = carve(o, HW, F32); o += 1024
        kf = carve(o, HW, F32); o += 1024
        bb = carve(o, HW, F32); o += 1024
        X = carve(o, HW, F32); o += 1024
        X2 = carve(o, HW, F32); o += 1024
        mscan = carve(o, HW, F32); o += 1024
        qrel_off = o
        qrel = carve(o, HW, BF16); o += 512
        krel = carve(o, HW, BF16); o += 512
        qdec = carve(o, HW, BF16); o += 512
        kdec = carve(o, HW, BF16); o += 512
        mone = carve(o, HW, F32)
        vT = carve(o, HW, BF16); o += 512
        gate = carve(o, HW, BF16); o += 512
        Ws = []
        for i in range(4):
            Ws.append(carve(o, 8 * 128, BF16).rearrange("p (k c) -> p k c", k=8)); o += 512
        kdT = [carve(o + 256 * i, 512, BF16) for i in range(2)]; o += 512
        vch = [carve(o + 256 * i, 512, BF16) for i in range(2)]; o += 512
        ATs = [carve(o + 64 * i, 128, BF16) for i in range(2)]; o += 128
        NSF, NSB = 4, 12
        Sf = [carve(o + 128 * i, 128, F32) for i in range(NSF)]; o += 128 * NSF
        Sb = [carve(o + 64 * i, 128, BF16) for i in range(NSB)]; o += 64 * NSB
        dec = carve(o, 32, F32); o += 32
        dhalf = carve(o, 2, F32); o += 2
        o += 2
        S1f = carve(o, 8 * 128, F32).rearrange("p (h e) -> p h e", h=8); o += 1024
        S0 = carve(o, 8 * 128, F32).rearrange("p (h e) -> p h e", h=8); o += 1024
        osb = [(carve(o + 512 * i, 512, F32), ("osb", i)) for i in range(2)]; o += 1024
        sqH = [(carve(o + 256 * i, 512, BF16), ("sqH", i)) for i in range(4)]; o += 1024
        rstdH = [(carve(o + 512 * i, 512, F32), ("rstdH", i)) for i in range(2)]; o += 1024
        kfT = carve(qrel_off, 8 * 128, BF16).rearrange("p (b d) -> p b d", b=8)
        vtok = carve(qrel_off + 512, 8 * 128, BF16).rearrange("p (b e) -> p b e", b=8)
        assert o <= ARENA_W, o
        ident = cmask[:, 128:256]
        m32 = cmask[0:32, 0:128]
        gcol = 2 * s_
        lb_ap = lambda h: lbt[:, 8 * s_ + h:8 * s_ + h + 1]
        oml_ap = lambda h: lbt[:, 16 + 8 * s_ + h:16 + 8 * s_ + h + 1]
        noml_ap = lambda h: lbt[:, 32 + 8 * s_ + h:32 + 8 * s_ + h + 1]
        onorm_ap = hg_c[:, 32 + s_:33 + s_]

        P.memset("pool", mone, 1.0, w=["mone"])
        P.memset("pool", mscan, 1.0, w=["mscan"])
        P.memset("pool", mscan.rearrange("p (c t) -> p c t", t=32)[:, :, 0:1], 0.0, w=["mscan"])
        for tt in range(NT):
            norm_pre(tt, l * 4 + 0, aT[:, :, tt * TT:(tt + 1) * TT], ("aTg", tt), sqH, rstdH[tt % 2], 6 + tt % 2)

        wcnt = {"n": 0}

        def load_w(which, h):
            P.dma("pool", out=Ws[which], in_=hg_win_d[s_, which * 8 + h].rearrange("p (k c) -> p k c", k=8),
                  sem=f"hw{which}", w=[("hw", which)])

        def proj(which, hf, evac):
            for t2 in range(2):
                tt = hf * 2 + t2
                b = next_ps(0, 4)
                for k in range(KC):
                    P.matmul(PS[:, b, :], Ws[which][:, k, :], aT[:, k, tt * TT:(tt + 1) * TT], k == 0, k == KC - 1,
                             r=[("hw", which), (("aTg", tt), k)], w=[("ps", b)])
                evac(b, t2)

        def f_branch(h, hf):
            def ev(b, t2):
                P.act(X[:, t2 * 512:(t2 + 1) * 512], PS[:, b, :], AF.Sigmoid, r=[("ps", b)], w=[("X", t2)])
            proj(1, hf, ev)
            P.ts("dve", kf, X, noml_ap(h), oml_ap(h), ALU.mult, ALU.add, r=[("X", 0), ("X", 1), "noml", "oml"], w=["kf"])
            P.ts("dve", X, kf, -1.0, 1.0, ALU.mult, ALU.add, r=["kf"], w=[("X", 0), ("X", 1)])
            P.act(bb, X, AF.Ln, r=[("X", 0), ("X", 1)], w=["bb"])

        for h in range(8):
            load_w(1, h)
            load_w(2, h)
            for hf in range(2):
                f_branch(h, hf)
                P.scan("dve", X2, mone, bb, 0.0, ALU.mult, ALU.add, r=["mone", "bb"], w=["X2"])
                P.tt("dve", X, X2[:, HW - 1:HW].broadcast_to([128, HW]), X2, ALU.subtract, r=["X2"], w=[("X", 0), ("X", 1)])
                P.act(bb, X, AF.Exp, r=[("X", 0), ("X", 1)], w=["bb"])
                P.tt("pool", kdec, kf, bb, ALU.mult, r=["kf", "bb"], w=["kdec"])
                P.act(dhalf[:, 0:1], X2[:, HW - 1:HW], AF.Exp, r=["X2"], w=["dhalf"])
                for g in range(2):
                    b = next_ps(0, 4)
                    for bi in range(4):
                        blk = g * 4 + bi
                        t0 = hf * HW + blk * 128
                        for k in range(KC):
                            P.matmul(PS[:, b, bi * 128:(bi + 1) * 128], aT[:, k, t0:t0 + 128], Ws[2][:, k, :], k == 0, k == KC - 1,
                                     r=[("hw", 2), (("aTg", t0 // TT), k)], w=[("ps", b)])
                    P.copy("dve", vtok[:, g * 4:(g + 1) * 4, :], PS[:, b, :].rearrange("p (b e) -> p b e", b=4), r=[("ps", b)], w=[("vtok", g)])
                    b2 = next_ps(0, 4)
                    pb = PS[:, b2, :].bitcast(BF16)
                    for bi in range(4):
                        blk = g * 4 + bi
                        P.transpose(pb[:, bi * 128:(bi + 1) * 128], kdec[:, blk * 128:(blk + 1) * 128], ident, r=["kdec", "cmask"], w=[("ps", b2)])
                    P.copy("dve", kfT[:, g * 4:(g + 1) * 4, :], pb[:, 0:512].rearrange("p (b d) -> p b d", b=4), r=[("ps", b2)], w=[("kfT", g)])
                bS = next_ps(4, 6)
                for blk in range(8):
                    P.matmul(PS[:, bS, 0:128], kfT[:, blk, :], vtok[:, blk, :], blk == 0, blk == 7,
                             r=[("kfT", blk // 4), ("vtok", blk // 4)], w=[("ps", bS)])
                if hf == 0:
                    P.copy("dve", S1f[:, h, :], PS[:, bS, 0:128], r=[("ps", bS)], w=[("S1f", h)])
                else:
                    P.stt("dve", S1f[:, h, :], S1f[:, h, :], dhalf[:, 0:1], PS[:, bS, 0:128], ALU.mult, ALU.add,
                          r=[("S1f", h), "dhalf", ("ps", bS)], w=[("S1f", h)])
        li = s_
        P.dma("sp", out=sin_d[li], in_=S1f.rearrange("p h e -> p (h e)"), sem=f"si{li}", r=[("S1f", h) for h in range(8)], w=[("sin", li)])
        P.collective("AllGather", PAIRS, sin_d[li].opt(), sout_d[li].opt(), f"hcc{li}", r=[("sin", li)], w=[("sout", li)])
        P.dma("sp", out=S0.rearrange("p h e -> p (h e)"), in_=sout_d[li][0:128, :], sem=f"so{li}", r=[("sout", li)], w=["S0"])

        P.barrier()
        oc_ = 0
        sidx = 0
        for h in range(8):
            for w_ in range(4):
                load_w(w_, h)
            sidx += 1
            P.ts("dve", Sf[sidx % NSF], S0[:, h, :], ctab[:, 3:4], None, ALU.mult, None, r=["S0", "ctab"], w=[("Sf", sidx % NSF)])
            P.copy("act", Sb[sidx % NSB], Sf[sidx % NSF], r=[("Sf", sidx % NSF)], w=[("Sb", sidx % NSB)])
            for hf in range(2):
                def evq(b, t2):
                    P.act(qf[:, t2 * 512:(t2 + 1) * 512], PS[:, b, :], AF.Silu, r=[("ps", b)], w=[("qf", t2)])
                proj(0, hf, evq)
                def evg(b, t2):
                    P.act(gate[:, t2 * 512:(t2 + 1) * 512], PS[:, b, :], AF.Silu, r=[("ps", b)], w=[("gate", t2)])
                proj(3, hf, evg)
                def evv(b, t2):
                    P.copy("dve", vT[:, t2 * 512:(t2 + 1) * 512], PS[:, b, :], r=[("ps", b)], w=[("vT", t2)])
                proj(2, hf, evv)
                f_branch(h, hf)
                QF = [("qf", 0), ("qf", 1)]
                XK = [("X", 0), ("X", 1)]
                P.scan("dve", X2, mscan, bb, 0.0, ALU.mult, ALU.add, r=["mscan", "bb"], w=["X2"])
                b3 = X2.rearrange("p (c t) -> p c t", t=32)
                X3 = X.rearrange("p (c t) -> p c t", t=32)
                P.tt("dve", X3, b3, b3[:, :, 16:17].broadcast_to([128, 32, 32]), ALU.subtract, r=["X2"], w=XK)
                P.act(bb, X, AF.Exp, r=XK, w=["bb"])
                P.tt("pool", qrel, qf, bb, ALU.mult, r=QF + ["bb"], w=["qrel"])
                P.act(bb, X, AF.Exp, scale=-1.0, r=XK, w=["bb"])
                P.tt("dve", krel, kf, bb, ALU.mult, r=["kf", "bb"], w=["krel"])
                P.act(bb, X2, AF.Exp, r=["X2"], w=["bb"])
                P.tt("pool", qdec, qf, bb, ALU.mult, r=QF + ["bb"], w=["qdec"])
                P.tt("dve", X3, b3[:, :, 31:32].broadcast_to([128, 32, 32]), b3, ALU.subtract, r=["X2"], w=XK)
                P.act(bb, X, AF.Exp, r=XK, w=["bb"])
                P.tt("dve", kdec, kf, bb, ALU.mult, r=["kf", "bb"], w=["kdec"])
                P.act(dec.rearrange("p (c o) -> p c o", o=1), b3[:, :, 31:32], AF.Exp, r=["X2"], w=["dec"])
                s_base = sidx

                def emit_tau(g):
                    nonlocal sidx
                    sl = g % 2
                    bT = next_ps(0, 4)
                    pbk = PS[:, bT, :].bitcast(BF16)
                    for n in range(4):
                        c0 = (g * 4 + n) * 32
                        P.transpose(pbk[0:32, n * 128:(n + 1) * 128], kdec[:, c0:c0 + 32], ident, r=["kdec", "cmask"], w=[("ps", bT)])
                    P.copy("dve", kdT[sl][0:32, :], pbk[0:32, 0:512], r=[("ps", bT)], w=[("kdT", sl)])
                    bV = next_ps(0, 4)
                    pbv = PS[:, bV, :].bitcast(BF16)
                    for n in range(4):
                        c0 = (g * 4 + n) * 32
                        P.transpose(pbv[0:32, n * 128:(n + 1) * 128], vT[:, c0:c0 + 32], ident, r=[("vT", c0 // 512), "cmask"], w=[("ps", bV)])
                    P.copy("act", vch[sl][0:32, :], pbv[0:32, 0:512], r=[("ps", bV)], w=[("vch", sl)])
                    bA = next_ps(0, 4)
                    for n in range(4):
                        c0 = (g * 4 + n) * 32
                        P.matmul(PS[0:32, bA, n * 32:(n + 1) * 32], krel[:, c0:c0 + 32], qrel[:, c0:c0 + 32], True, True,
                                 r=["krel", "qrel"], w=[("ps", bA)])
                    P.tt("dve", ATs[sl][0:32, :], PS[0:32, bA, 0:128], m32, ALU.mult, r=[("ps", bA), "cmask"], w=[("AT", sl)])
                    bU = 6 + g % 2
                    for n in range(4):
                        P.matmul(PS[:, bU, n * 128:(n + 1) * 128], kdT[sl][0:32, n * 128:(n + 1) * 128], vch[sl][0:32, n * 128:(n + 1) * 128], True, True,
                                 r=[("kdT", sl), ("vch", sl)], w=[("ps", bU)])
                    for n in range(4):
                        ci = g * 4 + n
                        cur, nxt = sidx, sidx + 1
                        P.stt("dve", Sf[nxt % NSF], Sf[cur % NSF], dec[:, ci:ci + 1], PS[:, bU, n * 128:(n + 1) * 128], ALU.mult, ALU.add,
                              r=[("Sf", cur % NSF), "dec", ("ps", bU)], w=[("Sf", nxt % NSF)])
                        P.copy("act", Sb[nxt % NSB], Sf[nxt % NSF], r=[("Sf", nxt % NSF)], w=[("Sb", nxt % NSB)])
                        sidx = nxt

                def emit_o(g):
                    nonlocal oc_
                    sl = g % 2
                    bO = 4 + ((hf * 8 + g) // 4) % 2
                    for n in range(4):
                        c0 = (g * 4 + n) * 32
                        oc0 = ((g % 4) * 4 + n) * 32
                        sb = (s_base + g * 4 + n) % NSB
                        P.matmul(PS[:, bO, oc0:oc0 + 32], vch[sl][0:32, n * 128:(n + 1) * 128], ATs[sl][0:32, n * 32:(n + 1) * 32], True, False,
                                 r=[("vch", sl), ("AT", sl)], w=[("ps", bO)])
                        P.matmul(PS[:, bO, oc0:oc0 + 32], Sb[sb], qdec[:, c0:c0 + 32], False, True,
                                 r=[("Sb", sb), "qdec"], w=[("ps", bO)])
                    if g % 4 == 3:
                        t2 = g // 4
                        tt = hf * 2 + t2
                        ob = osb[oc_ % 2]; sq_ = sqH[oc_ % 4]; rs = rstdH[oc_ % 2]; oc_ += 1
                        P.act(ob[0], PS[:, bO, :], AF.Copy, r=[("ps", bO)], w=[ob[1]])
                        P.act(sq_[0], PS[:, bO, :], AF.Square, r=[("ps", bO)], w=[sq_[1]])
                        bN = next_ps(0, 4)
                        P.matmul(PS[:, bN, :], ones[128][:], sq_[0], True, True, r=[sq_[1], ("ones", 128)], w=[("ps", bN)])
                        rstd_from(bN, rs[0], rs[1])
                        P.stt("dve", ob[0], ob[0], onorm_ap, rs[0], ALU.mult, ALU.mult, r=[ob[1], "hg_c", rs[1]], w=[ob[1]])
                        P.tt("pool", ogT[:, h, tt * TT:(tt + 1) * TT], ob[0], gate[:, t2 * 512:(t2 + 1) * 512], ALU.mult,
                             r=[ob[1], ("gate", t2)], w=[("ogT", h, tt)])

                emit_tau(0)
                for g in range(8):
                    if g + 1 < 8:
                        emit_tau(g + 1)
                    emit_o(g)
        P.barrier()
        wo = carve(0, 8 * 1024, BF16).rearrange("p (c n) -> p c n", c=8)
        c_ = 16384
        mT = carve(c_, 8 * 512, F32).rearrange("p (k t) -> p k t", k=8); c_ += 4096
        sqC = []
        for i in range(4):
            sqC.append((carve(c_, 512, BF16), ("sqC", i))); c_ += 256
        rstdC = (carve(c_, 512, F32), "rstdC"); c_ += 512
        tC = []
        for i in range(3):
            tC.append((carve(c_, 512, F32), ("tC", i))); c_ += 512
        P.dma("pool", out=wo, in_=hg_wo_d[s_].rearrange("p (c n) -> p c n", c=8), sem="hwo", w=["hwo"])
        out_proj_post(l, wo, "hwo", ogT, "ogT", mT, sqC, rstdC, tC)

    for (kind, l) in plan:
        if kind == "mlp":
            mlp_layer(l)
        elif kind == "mla":
            mla_layer(l)
        elif kind == "hgrn":
            hgrn_layer(l)
        else:
            raise NotImplementedError(kind)

    P.barrier()
    for k in range(KC):
        P.dma("sp", out=outT_d[k * 128:(k + 1) * 128, :], in_=hT[:, k, :], sem="xout", r=[("h", k, t) for t in range(NT)])
    P.wait_all_dma("sp")
    P.emit()
    return nc


FULL_PLAN = []
for _l in range(DEPTH):
    FULL_PLAN.append(("mla" if _l % 2 == 0 else "hgrn", _l))
    FULL_PLAN.append(("mlp", _l))


def prep_weights(inp):
    f = np.float32
    out = {}
    ng = np.asarray(inp["norm_gains"], f)
    out["gains"] = np.ascontiguousarray(ng.reshape(16, 8, 128).transpose(2, 0, 1).reshape(128, 128))
    w1 = np.asarray(inp["mlp_w1"], f)
    out["w1r"] = np.ascontiguousarray(w1.reshape(4, 8, 128, 16, 256).transpose(0, 3, 2, 1, 4).reshape(4, 16, 128, 2048))
    w2 = np.asarray(inp["mlp_w2"], f)
    out["w2r"] = np.ascontiguousarray(w2.reshape(4, 32, 128, 8, 128).transpose(0, 3, 2, 1, 4).reshape(4, 8, 128, 4096))
    w_in = np.asarray(inp["mla_w_in"], f)
    kr = w_in[:, :, 768:832]
    kr_sw = np.concatenate([kr[:, :, 32:64], kr[:, :, 0:32]], axis=2)
    w_in_ext = np.concatenate([w_in, kr_sw], axis=2)
    out["mla_win"] = np.ascontiguousarray(w_in_ext.reshape(2, 8, 128, 896).transpose(0, 2, 1, 3).reshape(2, 128, 8 * 896))
    g = np.zeros((128, 16), f)
    qn_ = np.asarray(inp["mla_q_norm"], f)
    kvn_ = np.asarray(inp["mla_kv_norm"], f)
    for s_ in range(2):
        g[:, 8 * s_:8 * s_ + 4] = qn_[s_].reshape(4, 128).T
        g[:, 8 * s_ + 4:8 * s_ + 6] = kvn_[s_].reshape(2, 128).T
    out["mla_g"] = g
    w_uq = np.asarray(inp["mla_w_uq"], f).reshape(2, 4, 128, 8, 192)
    w_ukv = np.asarray(inp["mla_w_ukv"], f).reshape(2, 2, 128, 8, 256)
    uk = w_ukv[..., 0:128].transpose(0, 3, 2, 1, 4).reshape(2, 8, 128, 256)
    uv = w_ukv[..., 128:256].transpose(0, 3, 2, 1, 4).reshape(2, 8, 128, 256)
    uqn = w_uq[..., 0:128].transpose(0, 3, 2, 1, 4).reshape(2, 8, 128, 512)
    qr_ = w_uq[..., 128:192]
    uqr = qr_.transpose(0, 3, 2, 1, 4).reshape(2, 8, 128, 256)
    qrs_ = np.concatenate([qr_[..., 32:64], qr_[..., 0:32]], axis=-1)
    uqrs = qrs_.transpose(0, 3, 2, 1, 4).reshape(2, 8, 128, 256)
    out["mla_wh"] = np.ascontiguousarray(np.concatenate([uk, uv, uqn, uqr, uqrs], axis=3))
    w_o = np.asarray(inp["mla_w_o"], f)
    out["mla_wo"] = np.ascontiguousarray(w_o.reshape(2, 8, 128, 1024).transpose(0, 2, 1, 3).reshape(2, 128, 8192))
    hw = np.asarray(inp["hgrn_w_in"], f)
    out["hg_win"] = np.ascontiguousarray(hw.reshape(2, 8, 128, 32, 128).transpose(0, 3, 2, 1, 4).reshape(2, 32, 128, 1024))
    hwo = np.asarray(inp["hgrn_w_o"], f)
    out["hg_wo"] = np.ascontiguousarray(hwo.reshape(2, 8, 128, 1024).transpose(0, 2, 1, 3).reshape(2, 128, 8192))
    hc = np.zeros((128, 64), f)
    lbl = np.asarray(inp["hgrn_lb_logits"], f)
    hc[:, 0:32] = lbl.reshape(4, 8, 128).transpose(2, 0, 1).reshape(128, 32)
    hc[:, 32:34] = np.asarray(inp["hgrn_o_norm"], f).T
    out["hg_c"] = hc
    cm = np.zeros((128, 256), f)
    ss_ = np.arange(32)[:, None]
    cc_ = np.arange(32)[None, :]
    cm[0:32, 0:128] = np.tile((cc_ >= ss_).astype(f), (1, 4))
    cm[:, 128:256] = np.eye(128, dtype=f)
    out["cmask"] = cm
    kk = np.arange(128)[:, None, None]
    jj = np.arange(4)[None, :, None]
    qq = np.arange(512)[None, None, :]
    out["masks"] = np.ascontiguousarray((qq >= kk + 128 * jj).astype(f).reshape(128, 2048))
    return out


def make_ctab(j):
    ct = np.zeros((128, 16), np.float32)
    i = np.arange(128) % 32
    ct[:, 0] = ((10000.0 ** (-(2.0 * i) / 64.0)) / (2.0 * np.pi)).astype(np.float32)
    ct[:, 1] = 0.25
    ct[:, 2] = np.where((np.arange(128) % 64) < 32, 0.5, 0.0)
    ct[:, 3] = float(j)
    return ct


def run_plan(plan, x, inputs, wts=None):
    if wts is None:
        wts = prep_weights(inputs)
    nc = build_program(plan)
    in_maps = []
    for c in range(8):
        b, j = c // 2, c % 2
        m = dict(wts)
        m["xT"] = np.ascontiguousarray(np.asarray(x[b, j * T:(j + 1) * T, :], np.float32).T)
        m["ctab"] = make_ctab(j)
        m["posb"] = np.ascontiguousarray(np.broadcast_to(np.asarray(inputs["positions"])[b, j * T:(j + 1) * T].astype(np.int32)[None, :], (64, T)))
        in_maps.append(m)
    res = run_bass_kernel_spmd(nc, in_maps, core_ids=list(range(8)))
    out = np.empty((4, 4096, D), np.float32)
    for c in range(8):
        b, j = c // 2, c % 2
        out[b, j * T:(j + 1) * T, :] = res.results[c]["outT"].T
    return out


def kernel(**inputs):
    x = np.asarray(inputs["x"], np.float32)
    return run_plan(FULL_PLAN, x, inputs)
```

```python
import os
import numpy as np
import concourse.bass as bass
import concourse.mybir as mybir
from concourse.bass_utils import run_bass_kernel_spmd

F32 = mybir.dt.float32
BF16 = mybir.dt.bfloat16
I32 = mybir.dt.int32
AF = mybir.ActivationFunctionType
ALU = mybir.AluOpType

D = 1024
T = 2048
TT = 512
NT = T // TT
KC = D // 128
DEPTH = 4
EPS = 1e-6
ENGS = ["pe", "act", "dve", "pool", "sp"]
SEM_LIM = 30000


class Prog:
    def __init__(self, nc):
        self.nc = nc
        self.ops = {e: [] for e in ENGS}
        self.last_w = {}
        self.readers = {}
        self.marked = set()
        self.dcount = {}
        self.last_compute = {e: None for e in ENGS}

    def _add(self, eng, fn, r, w, dma_sem=None):
        deps = set()
        for k in r:
            t = self.last_w.get(k)
            if t is not None:
                deps.add(t)
        for k in w:
            t = self.last_w.get(k)
            if t is not None:
                deps.add(t)
            rd = self.readers.get(k)
            if rd:
                deps.update(rd.values())
        idx = len(self.ops[eng])
        if dma_sem is None:
            tok = ("c", eng, idx)
            if eng == "pe":
                deps = {d for d in deps if not (d[0] == "c" and d[1] == eng)}
            self.last_compute[eng] = idx
        else:
            self.dcount[dma_sem] = self.dcount.get(dma_sem, 0) + 16
            tok = ("d", dma_sem, self.dcount[dma_sem])
            deps = {d for d in deps if not (d[0] == "d" and d[1] == dma_sem)}
        for d in deps:
            if d[0] == "c":
                self.marked.add((d[1], d[2]))
        self.ops[eng].append((fn, deps, tok))
        for k in w:
            self.last_w[k] = tok
            self.readers[k] = {}
        for k in r:
            rd = self.readers.setdefault(k, {})
            rd[tok if tok[0] == "d" else eng] = tok
        return tok

    def matmul(self, out, lhsT, rhs, start, stop, r, w):
        self._add("pe", lambda e: e.matmul(out, lhsT=lhsT, rhs=rhs, start=start, stop=stop), r, w)

    def transpose(self, out, in_, ident, r, w):
        self._add("pe", lambda e: e.transpose(out, in_, ident), r, w)

    def act(self, out, in_, func, r, w, bias=None, scale=None, eng="act"):
        kw = {}
        if bias is not None:
            kw["bias"] = bias
        if scale is not None:
            kw["scale"] = scale
        self._add("act", lambda e: e.activation(out=out, in_=in_, func=func, **kw), r, w)

    def ts(self, eng, out, in0, s1, s2, op0, op1, r, w):
        if op1 is None:
            self._add(eng, lambda e: e.tensor_scalar(out=out, in0=in0, scalar1=s1, scalar2=None, op0=op0), r, w)
        else:
            self._add(eng, lambda e: e.tensor_scalar(out=out, in0=in0, scalar1=s1, scalar2=s2, op0=op0, op1=op1), r, w)

    def tt(self, eng, out, in0, in1, op, r, w):
        self._add(eng, lambda e: e.tensor_tensor(out=out, in0=in0, in1=in1, op=op), r, w)

    def stt(self, eng, out, in0, scalar, in1, op0, op1, r, w):
        self._add(eng, lambda e: e.scalar_tensor_tensor(out=out, in0=in0, scalar=scalar, in1=in1, op0=op0, op1=op1), r, w)

    def copy(self, eng, out, in_, r, w):
        if eng == "act":
            self._add(eng, lambda e: e.copy(out=out, in_=in_), r, w)
        else:
            self._add(eng, lambda e: e.tensor_copy(out=out, in_=in_), r, w)

    def memset(self, eng, ap, val, w):
        self._add(eng, lambda e: e.memset(ap, val), [], w)

    def scan(self, eng, out, d0, d1, init, op0, op1, r, w):
        self._add(eng, lambda e: e.tensor_tensor_scan(out=out, data0=d0, data1=d1, initial=init, op0=op0, op1=op1), r, w)

    def recip(self, out, in_, r, w):
        self._add("dve", lambda e: e.reciprocal(out=out, in_=in_), r, w)

    def dma(self, q, out, in_, sem, r=(), w=()):
        return self._add(q, lambda e: e.dma_start(out=out, in_=in_), list(r), list(w), dma_sem=sem)

    def collective(self, kind, groups, in_ap, out_ap, sem, r, w):
        def fn(e):
            return e.collective_compute(kind, ALU.bypass, replica_groups=groups, ins=[in_ap], outs=[out_ap])
        eng = "pool"
        deps = set()
        for k in r:
            t = self.last_w.get(k)
            if t is not None:
                deps.add(t)
        for k in w:
            t = self.last_w.get(k)
            if t is not None:
                deps.add(t)
            rd = self.readers.get(k)
            if rd:
                deps.update(rd.values())
        self.dcount[sem] = self.dcount.get(sem, 0) + 1
        tok = ("d", sem, self.dcount[sem])
        for d in deps:
            if d[0] == "c":
                self.marked.add((d[1], d[2]))
        self.ops[eng].append((fn, deps, ("cc", sem, self.dcount[sem])))
        for k in w:
            self.last_w[k] = tok
            self.readers[k] = {}
        for k in r:
            self.readers.setdefault(k, {})[tok] = tok
        return tok

    def barrier(self):
        toks = []
        for e in ENGS:
            i = self.last_compute[e]
            if i is not None:
                toks.append(("c", e, i))
                self.marked.add((e, i))
        for e in ENGS:
            deps = {t for t in toks if (t[1] != e or e in ("act", "dve", "pool"))}
            if deps:
                self.ops[e].append((None, deps, ("c", e, len(self.ops[e]))))
        self.last_w = {k: t for k, t in self.last_w.items() if t[0] == "d"}
        newr = {}
        for k, rd in self.readers.items():
            rd2 = {a: t for a, t in rd.items() if t[0] == "d"}
            if rd2:
                newr[k] = rd2
        self.readers = newr

    def wait_all_dma(self, eng="sp"):
        deps = {("d", s, c) for s, c in self.dcount.items()}
        self.ops[eng].append((None, deps, ("c", eng, len(self.ops[eng]))))

    def emit(self):
        nc = self.nc
        ordinal = {}
        nmarked = {}
        for e in ENGS:
            c = 0
            for i in range(len(self.ops[e])):
                if (e, i) in self.marked:
                    c += 1
                    ordinal[(e, i)] = c
            nmarked[e] = c
        esem = {}
        for e in ENGS:
            nep = (nmarked[e] + SEM_LIM - 1) // SEM_LIM
            esem[e] = [nc.alloc_semaphore(f"s_{e}_{j}") for j in range(max(nep, 1))]
        dsem = {name: nc.alloc_semaphore(f"d_{name}") for name in self.dcount}

        def run(eng_name, e):
            waited_c = {}
            waited_d = {}
            for i, (fn, deps, tok) in enumerate(self.ops[eng_name]):
                for d in sorted(deps, key=lambda x: (x[0], str(x[1]), x[2])):
                    if d[0] == "c":
                        o = ordinal[(d[1], d[2])]
                        if waited_c.get(d[1], 0) >= o:
                            continue
                        waited_c[d[1]] = o
                        ep = (o - 1) // SEM_LIM
                        e.wait_ge(esem[d[1]][ep], o - ep * SEM_LIM)
                    else:
                        if waited_d.get(d[1], 0) >= d[2]:
                            continue
                        waited_d[d[1]] = d[2]
                        e.wait_ge(dsem[d[1]], d[2])
                if fn is None:
                    continue
                ins = fn(e)
                if tok[0] == "d":
                    ins.then_inc(dsem[tok[1]], 16)
                elif tok[0] == "cc":
                    ins.then_inc(dsem[tok[1]])
                elif (eng_name, i) in self.marked:
                    o = ordinal[(eng_name, i)]
                    ep = (o - 1) // SEM_LIM
                    ins.then_inc(esem[eng_name][ep], 1)

        with nc.Block() as block:
            @block.tensor
            def _(e):
                run("pe", e)

            @block.scalar
            def _(e):
                run("act", e)

            @block.vector
            def _(e):
                run("dve", e)

            @block.gpsimd
            def _(e):
                run("pool", e)

            @block.sync
            def _(e):
                run("sp", e)


def build_program(plan, in_name="xT"):
    nc = bass.Bass("TRN2", target_bir_lowering=False)
    P = Prog(nc)

    def din(name, shape, dt=F32):
        return nc.dram_tensor(name, shape, dt, kind="ExternalInput").ap()

    xT_d = din("xT", [D, T])
    outT_d = nc.dram_tensor("outT", [D, T], F32, kind="ExternalOutput").ap()
    gains_d = din("gains", [128, 128])
    ctab_d = din("ctab", [128, 16])
    w1r_d = din("w1r", [DEPTH, 16, 128, 8 * 256])
    w2r_d = din("w2r", [DEPTH, 8, 128, 32 * 128])
    posb_d = din("posb", [64, T], I32)
    masks_d = din("masks", [128, 4 * 512])
    mla_win_d = din("mla_win", [2, 128, 8 * 896])
    mla_g_d = din("mla_g", [128, 16])
    mla_wh_d = din("mla_wh", [2, 8, 128, 1536])
    mla_wo_d = din("mla_wo", [2, 128, 8 * 1024])
    hg_win_d = din("hg_win", [2, 32, 128, 8 * 128])
    hg_wo_d = din("hg_wo", [2, 128, 8 * 1024])
    hg_c_d = din("hg_c", [128, 64])
    cmask_d = din("cmask", [128, 128 + 128])
    sin_d = [nc.dram_tensor(f"sin{i}", [128, 1024], F32).ap() for i in range(2)]
    sout_d = [nc.dram_tensor(f"sout{i}", [256, 1024], F32).ap() for i in range(2)]
    xin_d = [nc.dram_tensor(f"xin{i}", [320, T], BF16).ap() for i in range(2)]
    xout_d = [nc.dram_tensor(f"xout{i}", [640, T], BF16).ap() for i in range(2)]
    PAIRS = [[0, 1], [2, 3], [4, 5], [6, 7]]

    hT = nc.alloc_sbuf_tensor("hT", [128, KC, T], F32)
    gains = nc.alloc_sbuf_tensor("gains_sb", [128, 128], F32)
    ctab = nc.alloc_sbuf_tensor("ctab_sb", [128, 16], F32)
    ones_f = nc.alloc_sbuf_tensor("ones_f", [128, 128], F32)
    mla_g = nc.alloc_sbuf_tensor("mla_g_sb", [128, 16], F32)
    hg_c = nc.alloc_sbuf_tensor("hg_c_sb", [128, 64], F32)
    lbt = nc.alloc_sbuf_tensor("lbt", [128, 64], F32)
    cmask_f = nc.alloc_sbuf_tensor("cmask_f", [128, 256], F32)
    cmask = nc.alloc_sbuf_tensor("cmask_b", [128, 256], BF16)
    ones = {}
    for nm in (1024, 512, 256, 128, 1):
        ones[nm] = nc.alloc_sbuf_tensor(f"ones{nm}", [128, 128], BF16)
    ARENA_W = 35500
    arena = nc.alloc_sbuf_tensor("arena", [128, ARENA_W], F32)
    PS = nc.alloc_psum_tensor("PS", [128, 8, 512], F32)

    def carve(off_w, n_elem, dt):
        nw = n_elem if dt in (F32, I32) else (n_elem + 1) // 2
        ap = arena[:, off_w:off_w + nw]
        if dt != F32:
            ap = ap.bitcast(dt)
        return ap

    P.dma("sp", out=gains[:], in_=gains_d, sem="c0", w=["gains"])
    P.dma("sp", out=ctab[:], in_=ctab_d, sem="c1", w=["ctab"])
    P.dma("sp", out=mla_g[:], in_=mla_g_d, sem="c2", w=["mla_g"])
    P.dma("sp", out=hg_c[:], in_=hg_c_d, sem="c3", w=["hg_c"])
    P.dma("sp", out=cmask_f[:], in_=cmask_d, sem="c4", w=["cmask_f"])
    P.copy("dve", cmask[:], cmask_f[:], r=["cmask_f"], w=["cmask"])
    lg = hg_c[:, 0:32].rearrange("p (l h) -> p l h", l=4)
    mx = lbt[:, 48:56]
    P.tt("dve", mx, lg[:, 0, :], lg[:, 1, :], ALU.max, r=["hg_c"], w=["lb_mx"])
    P.tt("dve", mx, mx, lg[:, 2, :], ALU.max, r=["hg_c", "lb_mx"], w=["lb_mx"])
    P.tt("dve", mx, mx, lg[:, 3, :], ALU.max, r=["hg_c", "lb_mx"], w=["lb_mx"])
    ex = cmask_f[:, 0:32].rearrange("p (l h) -> p l h", l=4)
    for li_ in range(4):
        P.tt("dve", ex[:, li_, :], lg[:, li_, :], mx, ALU.subtract, r=["hg_c", "lb_mx", "cmask"], w=[("lb_ex", li_), "cmask_f"])
    P.act(cmask_f[:, 0:32], cmask_f[:, 0:32], AF.Exp, r=[("lb_ex", i) for i in range(4)], w=["lb_e"])
    sm = lbt[:, 56:64]
    P.tt("dve", sm, ex[:, 0, :], ex[:, 1, :], ALU.add, r=["lb_e"], w=["lb_sm"])
    P.tt("dve", sm, sm, ex[:, 2, :], ALU.add, r=["lb_e", "lb_sm"], w=["lb_sm"])
    P.tt("dve", sm, sm, ex[:, 3, :], ALU.add, r=["lb_e", "lb_sm"], w=["lb_sm"])
    P.recip(sm, sm, r=["lb_sm"], w=["lb_sm"])
    P.tt("dve", lbt[:, 0:8], ex[:, 1, :], sm, ALU.mult, r=["lb_e", "lb_sm"], w=["lbt0"])
    P.tt("dve", lbt[:, 8:16], ex[:, 2, :], ex[:, 3, :], ALU.add, r=["lb_e"], w=["lbt1"])
    P.tt("dve", lbt[:, 8:16], lbt[:, 8:16], sm, ALU.mult, r=["lbt1", "lb_sm"], w=["lbt1"])
    P.tt("dve", lbt[:, 8:16], lbt[:, 8:16], lbt[:, 0:8], ALU.add, r=["lbt1", "lbt0"], w=["lbt1"])
    P.ts("dve", lbt[:, 16:32], lbt[:, 0:16], -1.0, 1.0, ALU.mult, ALU.add, r=["lbt0", "lbt1"], w=["oml"])
    P.ts("dve", lbt[:, 32:48], lbt[:, 16:32], -1.0, None, ALU.mult, None, r=["oml"], w=["noml"])
    for k in range(KC):
        P.dma("sp", out=hT[:, k, :], in_=xT_d[k * 128:(k + 1) * 128, :], sem="xin", w=[("h", kk, t) for kk in range(KC) for t in range(NT)])
    for nm in ones:
        P.memset("pool", ones_f[:], 1.0 / nm, w=["ones_f"])
        P.copy("pool", ones[nm][:], ones_f[:], r=["ones_f"], w=[("ones", nm)])

    state = {"ps": 0}

    def next_ps(lo=0, hi=6):
        b = lo + state["ps"] % (hi - lo)
        state["ps"] += 1
        return b

    def rstd_from(ss_bank, rstd_ap, rstd_key):
        P.ts("dve", rstd_ap, PS[:, ss_bank, :], EPS, None, ALU.add, None, r=[("ps", ss_bank)], w=[rstd_key])
        P.act(rstd_ap, rstd_ap, AF.Sqrt, r=[rstd_key], w=[rstd_key])
        P.recip(rstd_ap, rstd_ap, r=[rstd_key], w=[rstd_key])

    def norm_pre(tt, gi, dst, dst_key, sq, rstd, ss_bank):
        tok = slice(tt * TT, (tt + 1) * TT)
        for k in range(KC):
            s = sq[k % len(sq)]
            P.act(s[0], hT[:, k, tok], AF.Square, r=[("h", k, tt)], w=[s[1]])
            P.matmul(PS[:, ss_bank, :], ones[1024][:], s[0], k == 0, k == KC - 1, r=[s[1], ("ones", 1024)], w=[("ps", ss_bank)])
        rstd_from(ss_bank, rstd[0], rstd[1])
        for k in range(KC):
            P.stt("dve", dst[:, k, :], hT[:, k, tok], gains[:, gi * 8 + k:gi * 8 + k + 1], rstd[0], ALU.mult, ALU.mult,
                  r=[("h", k, tt), "gains", rstd[1]], w=[(dst_key, k)])

    def mlp_layer(l):
        P.barrier()
        o = 0
        actT = carve(o, 32 * 1024, BF16).rearrange("p (f t) -> p f t", f=32); o += 16384
        mT = carve(o, 2 * 8 * 512, F32).rearrange("p (a k t) -> p a k t", a=2, k=8); o += 8192
        aTh = carve(o - 8192, 8 * 1024, BF16).rearrange("p (k t) -> p k t", k=8)
        W1s = []
        for i in range(3):
            W1s.append(carve(o, 8 * 256, BF16).rearrange("p (k c) -> p k c", k=8)); o += 1024
        W2s = []
        for i in range(2):
            W2s.append(carve(o, 32 * 128, BF16).rearrange("p (f c) -> p f c", f=32)); o += 2048
        tmp = []
        for i in range(3):
            tmp.append((carve(o, 512, F32), ("tmp", i))); o += 512
        sq = []
        for i in range(4):
            sq.append((carve(o, 512, BF16), ("sq", i))); o += 256
        rstd = []
        for i in range(2):
            rstd.append((carve(o, 512, F32), ("rstd", i))); o += 512
        assert o <= ARENA_W, o
        tc = 0

        def load_w1(hf_, g):
            slot = g % 3
            P.dma("pool", out=W1s[slot], in_=w1r_d[l, g].rearrange("p (k c) -> p k c", k=8), sem=f"w1_{slot}", w=[("w1", slot)])

        def load_w2(oc):
            slot = oc % 2
            for q in range(4):
                P.dma("pool", out=W2s[slot][:, q * 8:(q + 1) * 8, :],
                      in_=w2r_d[l, oc][:, q * 1024:(q + 1) * 1024].rearrange("p (f c) -> p f c", f=8),
                      sem=f"w2_{slot}", w=[("w2", slot, qq) for qq in range(4)])

        for g0 in range(3):
            load_w1(0, g0)
        for hf in range(2):
            if hf == 1:
                P.barrier()
            for t2 in range(2):
                norm_pre(hf * 2 + t2, l * 4 + 2, aTh[:, :, t2 * 512:(t2 + 1) * 512], ("aTh", t2), sq, rstd[t2], 6 + t2)
            load_w2(0)
            load_w2(1)
            for g in range(16):
                slot = g % 3
                for fi in range(2):
                    f = g * 2 + fi
                    for t2 in range(2):
                        b = next_ps()
                        for k in range(KC):
                            P.matmul(PS[:, b, :], W1s[slot][:, k, fi * 128:(fi + 1) * 128], aTh[:, k, t2 * 512:(t2 + 1) * 512],
                                     k == 0, k == KC - 1, r=[("w1", slot), (("aTh", t2), k)], w=[("ps", b)])
                        tm = tmp[tc % 3]
                        tc += 1
                        P.act(tm[0], PS[:, b, :], AF.Relu, r=[("ps", b)], w=[tm[1]])
                        P.tt("dve", actT[:, f, t2 * 512:(t2 + 1) * 512], tm[0], tm[0], ALU.mult, r=[tm[1]], w=[("act", f, t2)])
                if g + 3 < 16:
                    load_w1(hf, g + 3)
            P.barrier()
            if hf == 0:
                for g0 in range(3):
                    load_w1(1, g0)
            pend = []
            for oc in range(8):
                slot = oc % 2
                for t2 in range(2):
                    b = next_ps()
                    for f in range(32):
                        P.matmul(PS[:, b, :], W2s[slot][:, f, :], actT[:, f, t2 * 512:(t2 + 1) * 512], f == 0, f == 31,
                                 r=[("w2", slot, f // 8), ("act", f, t2)], w=[("ps", b)])
                    for (pb, pt2, poc, ps_) in pend:
                        P.matmul(PS[:, 6 + pt2, :], ones[1024][:], ps_[0], poc == 0, poc == 7, r=[ps_[1], ("ones", 1024)], w=[("ps", 6 + pt2)])
                    pend = []
                    s = sq[(oc * 2 + t2) % 4]
                    P.act(mT[:, t2, oc, :], PS[:, b, :], AF.Copy, r=[("ps", b)], w=[("mT", t2, oc)])
                    P.act(s[0], PS[:, b, :], AF.Square, r=[("ps", b)], w=[s[1]])
                    pend.append((b, t2, oc, s))
                if oc + 2 < 8:
                    load_w2(oc + 2)
            for (pb, pt2, poc, ps_) in pend:
                P.matmul(PS[:, 6 + pt2, :], ones[1024][:], ps_[0], poc == 0, poc == 7, r=[ps_[1], ("ones", 1024)], w=[("ps", 6 + pt2)])
            gi = l * 4 + 3
            for t2 in range(2):
                tt_ = hf * 2 + t2
                tok = slice(tt_ * TT, (tt_ + 1) * TT)
                rstd_from(6 + t2, rstd[t2][0], rstd[t2][1])
                for oc in range(8):
                    tm = tmp[tc % 3]
                    tc += 1
                    P.stt("dve", tm[0], mT[:, t2, oc, :], gains[:, gi * 8 + oc:gi * 8 + oc + 1], rstd[t2][0], ALU.mult, ALU.mult,
                          r=[("mT", t2, oc), "gains", rstd[t2][1]], w=[tm[1]])
                    P.tt("dve", hT[:, oc, tok], hT[:, oc, tok], tm[0], ALU.add, r=[("h", oc, tt_), tm[1]], w=[("h", oc, tt_)])

    def out_proj_post(l, wo, wkey, oT, okey, mT, sqC, rstdC, tC):
        gi = l * 4 + 1
        tcn = 0
        for tt in range(NT):
            tok = slice(tt * TT, (tt + 1) * TT)
            pend = []
            for oc in range(8):
                b = next_ps()
                for c in range(8):
                    P.matmul(PS[:, b, :], wo[:, c, oc * 128:(oc + 1) * 128], oT[:, c, tok], c == 0, c == 7,
                             r=[wkey, (okey, c, tt)], w=[("ps", b)])
                for (poc, ps_) in pend:
                    P.matmul(PS[:, 7, :], ones[1024][:], ps_[0], poc == 0, poc == 7, r=[ps_[1], ("ones", 1024)], w=[("ps", 7)])
                pend = []
                sq_ = sqC[oc % 4]
                P.act(mT[:, oc, :], PS[:, b, :], AF.Copy, r=[("ps", b)], w=[("mT", oc)])
                P.act(sq_[0], PS[:, b, :], AF.Square, r=[("ps", b)], w=[sq_[1]])
                pend.append((oc, sq_))
            for (poc, ps_) in pend:
                P.matmul(PS[:, 7, :], ones[1024][:], ps_[0], poc == 0, poc == 7, r=[ps_[1], ("ones", 1024)], w=[("ps", 7)])
            rstd_from(7, rstdC[0], rstdC[1])
            for oc in range(8):
                tm = tC[tcn % 3]; tcn += 1
                P.stt("dve", tm[0], mT[:, oc, :], gains[:, gi * 8 + oc:gi * 8 + oc + 1], rstdC[0], ALU.mult, ALU.mult,
                      r=[("mT", oc), "gains", rstdC[1]], w=[tm[1]])
                P.tt("pool", hT[:, oc, tok], hT[:, oc, tok], tm[0], ALU.add, r=[("h", oc, tt), tm[1]], w=[("h", oc, tt)])

    SCALE = float(192 ** -0.5)
    TWO_PI = float(2.0 * np.pi)

    def mla_layer(l):
        s_ = l // 2
        P.barrier()
        o = 0
        cqn = carve(o, 4 * T, BF16).rearrange("p (c t) -> p c t", c=4); o += 4096
        ckv_own = carve(o, 2 * T, BF16).rearrange("p (c t) -> p c t", c=2); o += 2048
        ckv_pre = carve(o, 2 * T, BF16).rearrange("p (c t) -> p c t", c=2); o += 2048
        kr_own = carve(o, T, BF16); o += 1024
        kr_pre = carve(o, T, BF16); o += 1024
        cs = carve(o, T, BF16); o += 1024
        sn = carve(o, T, BF16); o += 1024
        masks = carve(o, 4 * 512, BF16).rearrange("p (j q) -> p j q", j=4); o += 1024
        whs = []
        for i in range(2):
            whs.append(carve(o, 1536, BF16)); o += 768
        visones = carve(o, 128, BF16); o += 64
        oT_off = o
        oT = carve(o, 8 * T, BF16).rearrange("p (c t) -> p c t", c=8); o += 8192
        base = o
        Kh = carve(o, 2 * T, BF16); o += 2048
        Vh = carve(o, 32 * 128, BF16).rearrange("p (b d) -> p b d", b=32); o += 2048
        qn = carve(o, T, BF16); o += 1024
        qr = carve(o, T, BF16); o += 1024
        PT = []
        for i in range(4):
            PT.append((carve(o, 512, BF16), ("PT", i))); o += 256
        rden = []
        for i in range(2):
            rden.append((carve(o, 512, F32), ("rden", i))); o += 512
        tq_ = []
        for i in range(4):
            tq_.append((carve(o, 512, F32), ("tq", i))); o += 512
        assert o <= ARENA_W, o
        a = oT_off
        aT = carve(a, 8 * 512, BF16).rearrange("p (k t) -> p k t", k=8); a += 2048
        cq_raw = carve(a, 4 * 512, F32).rearrange("p (c t) -> p c t", c=4); a += 2048
        ckv_raw = carve(a, 2 * 512, F32).rearrange("p (c t) -> p c t", c=2); a += 1024
        sqA = []
        for i in range(4):
            sqA.append((carve(a, 512, BF16), ("sqA", i))); a += 256
        rstdA = []
        for i in range(3):
            rstdA.append((carve(a, 512, F32), ("rstdA", i))); a += 512
        tA = []
        for i in range(2):
            tA.append((carve(a, 512, F32), ("tA", i))); a += 512
        posi = carve(a, T, I32); a += 2048
        yv = carve(a, T, F32); a += 2048
        kf = carve(a, T, F32); a += 2048
        ki = carve(a, T, I32); a += 2048
        win = carve(a, 8 * 896, BF16).rearrange("p (k c) -> p k c", k=8); a += 3584
        assert a <= ARENA_W, a

        P.dma("pool", out=win, in_=mla_win_d[s_].rearrange("p (k c) -> p k c", k=8), sem="win", w=["win"])
        P.dma("pool", out=masks, in_=masks_d.rearrange("p (j q) -> p j q", j=4), sem="masks", w=["masks"])
        P.dma("sp", out=posi[0:64, :], in_=posb_d, sem="posi", w=["posi"])
        P.ts("dve", visones, ones[1][:], ctab[:, 3:4], None, ALU.mult, None, r=[("ones", 1), "ctab"], w=["visones"])
        posf = kf
        P.copy("dve", posf[0:64, :], posi[0:64, :], r=["posi"], w=["posf"])
        for (tbl, col, key) in ((cs, 1, "cs"), (sn, 2, "sn")):
            P.ts("dve", yv[0:64, :], posf[0:64, :], ctab[0:64, 0:1], ctab[0:64, col:col + 1], ALU.mult, ALU.add,
                 r=["posf", "ctab"], w=["yv"])
            P.copy("dve", ki[0:64, :], yv[0:64, :], r=["yv"], w=["ki"])
            kf2 = carve(oT_off, T, F32)
            P.copy("dve", kf2[0:64, :], ki[0:64, :], r=["ki"], w=["kf2"])
            P.tt("dve", yv[0:64, :], yv[0:64, :], kf2[0:64, :], ALU.subtract, r=["yv", "kf2"], w=["yv"])
            P.ts("dve", kf2[0:64, :], yv[0:64, :], 0.5, None, ALU.is_gt, None, r=["yv"], w=["kf2"])
            P.tt("dve", yv[0:64, :], yv[0:64, :], kf2[0:64, :], ALU.subtract, r=["yv", "kf2"], w=["yv"])
            P.ts("dve", kf2[0:64, :], yv[0:64, :], -0.5, None, ALU.is_lt, None, r=["yv"], w=["kf2"])
            P.tt("dve", yv[0:64, :], yv[0:64, :], kf2[0:64, :], ALU.add, r=["yv", "kf2"], w=["yv"])
            P.act(tbl[0:64, :], yv[0:64, :], AF.Sin, scale=TWO_PI, r=["yv"], w=[key])
        P.barrier()

        gq = 8 * s_
        for tt in range(NT):
            tok = slice(tt * TT, (tt + 1) * TT)
            norm_pre(tt, l * 4 + 0, aT, "aT", sqA, rstdA[0], 6)
            for c in range(4):
                b = next_ps()
                for k in range(KC):
                    P.matmul(PS[:, b, :], win[:, k, c * 128:(c + 1) * 128], aT[:, k, :], k == 0, k == KC - 1,
                             r=["win", ("aT", k)], w=[("ps", b)])
                sq_ = sqA[c % 4]
                P.act(cq_raw[:, c, :], PS[:, b, :], AF.Copy, r=[("ps", b)], w=[("cq_raw", c)])
                P.act(sq_[0], PS[:, b, :], AF.Square, r=[("ps", b)], w=[sq_[1]])
                P.matmul(PS[:, 7, :], ones[512][:], sq_[0], c == 0, c == 3, r=[sq_[1], ("ones", 512)], w=[("ps", 7)])
            rstd_from(7, rstdA[1][0], rstdA[1][1])
            for c in range(4):
                P.stt("dve", cqn[:, c, tok], cq_raw[:, c, :], mla_g[:, gq + c:gq + c + 1], rstdA[1][0], ALU.mult, ALU.mult,
                      r=[("cq_raw", c), "mla_g", rstdA[1][1]], w=[("cqn", c, tt)])
            for c in range(2):
                b = next_ps()
                for k in range(KC):
                    P.matmul(PS[:, b, :], win[:, k, 512 + c * 128:512 + (c + 1) * 128], aT[:, k, :], k == 0, k == KC - 1,
                             r=["win", ("aT", k)], w=[("ps", b)])
                sq_ = sqA[c % 4]
                P.act(ckv_raw[:, c, :], PS[:, b, :], AF.Copy, r=[("ps", b)], w=[("ckv_raw", c)])
                P.act(sq_[0], PS[:, b, :], AF.Square, r=[("ps", b)], w=[sq_[1]])
                P.matmul(PS[:, 6, :], ones[256][:], sq_[0], c == 0, c == 1, r=[sq_[1], ("ones", 256)], w=[("ps", 6)])
            rstd_from(6, rstdA[2][0], rstdA[2][1])
            for c in range(2):
                P.stt("dve", ckv_own[:, c, tok], ckv_raw[:, c, :], mla_g[:, gq + 4 + c:gq + 5 + c], rstdA[2][0], ALU.mult, ALU.mult,
                      r=[("ckv_raw", c), "mla_g", rstdA[2][1]], w=[("ckv_own", c, tt)])
            b1 = next_ps()
            b2 = next_ps()
            for (bb, c0) in ((b1, 768), (b2, 832)):
                for k in range(KC):
                    P.matmul(PS[0:64, bb, :], win[:, k, c0:c0 + 64], aT[:, k, :], k == 0, k == KC - 1,
                             r=["win", ("aT", k)], w=[("ps", bb)])
            P.tt("dve", tA[0][0][0:64, :], PS[0:64, b1, :], cs[0:64, tok], ALU.mult, r=[("ps", b1), "cs"], w=[tA[0][1]])
            P.tt("dve", tA[1][0][0:64, :], PS[0:64, b2, :], sn[0:64, tok], ALU.mult, r=[("ps", b2), "sn"], w=[tA[1][1]])
            P.tt("pool", kr_own[0:64, tok], tA[0][0][0:64, :], tA[1][0][0:64, :], ALU.add, r=[tA[0][1], tA[1][1]], w=[("kr_own", tt)])

        if os.environ.get("KDBG") == "A":
            return
        li = s_
        for c in range(2):
            P.dma("sp", out=xin_d[li][c * 128:(c + 1) * 128, :], in_=ckv_own[:, c, :], sem=f"xi{li}_{c}",
                  r=[("ckv_own", c, t) for t in range(NT)], w=[("xin", li, c)])
        P.dma("sp", out=xin_d[li][256:320, :], in_=kr_own[0:64, :], sem=f"xi{li}_2",
              r=[("kr_own", t) for t in range(NT)], w=[("xin", li, 2)])
        P.collective("AllGather", PAIRS, xin_d[li].opt(), xout_d[li].opt(), f"cc{li}",
                     r=[("xin", li, c) for c in range(3)], w=[("xout", li)])
        for c in range(2):
            P.dma("sp", out=ckv_pre[:, c, :], in_=xout_d[li][c * 128:(c + 1) * 128, :], sem=f"xo{li}_{c}",
                  r=[("xout", li)], w=[("ckv_pre", c)])
        P.dma("sp", out=kr_pre[0:64, :], in_=xout_d[li][256:320, :], sem=f"xo{li}_2", r=[("xout", li)], w=["kr_pre"])
        P.barrier()
        if os.environ.get("KDBG") == "X":
            return

        ev = 0
        for h in range(1 if os.environ.get("KDBG") in ("B1", "B2") else 8):
            wh = whs[h % 2]
            wkey = ("wh", h % 2)
            P.dma("pool", out=wh, in_=mla_wh_d[s_, h], sem=f"wh{h % 2}", w=[wkey])
            uk = wh[:, 0:256].rearrange("p (c d) -> p c d", c=2)
            uv = wh[:, 256:512].rearrange("p (c d) -> p c d", c=2)
            uqn = wh[:, 512:1024].rearrange("p (c d) -> p c d", c=4)
            uqr = wh[:, 1024:1280].rearrange("p (c d) -> p c d", c=4)
            uqrs = wh[:, 1280:1536].rearrange("p (c d) -> p c d", c=4)
            for t8 in range(8):
                src, skey = (ckv_pre, "ckv_pre") if t8 < 4 else (ckv_own, "ckv_own")
                tl = t8 % 4
                b = next_ps(0, 3)
                for c in range(2):
                    rk = (skey, c) if t8 < 4 else (skey, c, tl)
                    P.matmul(PS[:, b, :], uk[:, c, :], src[:, c, tl * 512:(tl + 1) * 512], c == 0, c == 1,
                             r=[wkey, rk], w=[("ps", b)])
                P.copy("dve", Kh[:, t8 * 512:(t8 + 1) * 512], PS[:, b, :], r=[("ps", b)], w=[("Kh", t8)])
            for g in range(8):
                b = next_ps(0, 3)
                for bi in range(4):
                    blk = g * 4 + bi
                    src, skey = (ckv_pre, "ckv_pre") if blk < 16 else (ckv_own, "ckv_own")
                    lb_ = blk % 16
                    for c in range(2):
                        rk = (skey, c) if blk < 16 else (skey, c, lb_ // 4)
                        P.matmul(PS[:, b, bi * 128:(bi + 1) * 128], src[:, c, lb_ * 128:(lb_ + 1) * 128], uv[:, c, :], c == 0, c == 1,
                                 r=[wkey, rk], w=[("ps", b)])
                dst = Vh[:, g * 4:(g + 1) * 4, :]
                srcp = PS[:, b, :].rearrange("p (b d) -> p b d", b=4)
                if g < 4:
                    P.ts("dve", dst, srcp, ctab[:, 3:4], None, ALU.mult, None, r=[("ps", b), "ctab"], w=[("Vh", g)])
                else:
                    P.copy("dve", dst, srcp, r=[("ps", b)], w=[("Vh", g)])
            for tt in range(NT):
                tok = slice(tt * TT, (tt + 1) * TT)
                b = next_ps(0, 3)
                for c in range(4):
                    P.matmul(PS[:, b, :], uqn[:, c, :], cqn[:, c, tok], c == 0, c == 3, r=[wkey, ("cqn", c, tt)], w=[("ps", b)])
                P.copy("dve", qn[:, tok], PS[:, b, :], r=[("ps", b)], w=[("qn", tt)])
                b1 = next_ps(0, 3)
                for c in range(4):
                    P.matmul(PS[0:64, b1, :], uqr[:, c, :], cqn[:, c, tok], c == 0, c == 3, r=[wkey, ("cqn", c, tt)], w=[("ps", b1)])
                t1 = tq_[ev % 4]; ev += 1
                P.tt("dve", t1[0][0:64, :], PS[0:64, b1, :], cs[0:64, tok], ALU.mult, r=[("ps", b1), "cs"], w=[t1[1]])
                b2 = next_ps(0, 3)
                for c in range(4):
                    P.matmul(PS[0:64, b2, :], uqrs[:, c, :], cqn[:, c, tok], c == 0, c == 3, r=[wkey, ("cqn", c, tt)], w=[("ps", b2)])
                t2 = tq_[ev % 4]; ev += 1
                P.tt("dve", t2[0][0:64, :], PS[0:64, b2, :], sn[0:64, tok], ALU.mult, r=[("ps", b2), "sn"], w=[t2[1]])
                P.tt("pool", qr[0:64, tok], t1[0][0:64, :], t2[0][0:64, :], ALU.add, r=[t1[1], t2[1]], w=[("qr", tt)])
            for tq in range(0 if os.environ.get("KDBG") == "B1" else NT):
                qtok = slice(tq * TT, (tq + 1) * TT)
                ab = (h * NT + tq) % 2
                bo, bd = 3 + 2 * ab, 4 + 2 * ab
                units = [(True, kb) for kb in range(16)] + [(False, kb) for kb in range(4 * tq + 4)]
                n = len(units)
                pend = None
                for u in range(n + 1):
                    if u < n:
                        pre, kb = units[u]
                        col = kb * 128 if pre else T + kb * 128
                        bs = next_ps(0, 3)
                        krs, krk = (kr_pre, "kr_pre") if pre else (kr_own, ("kr_own", kb // 4))
                        P.matmul(PS[:, bs, :], Kh[:, col:col + 128], qn[:, qtok], True, False,
                                 r=[("Kh", col // 512), ("qn", tq)], w=[("ps", bs)])
                        P.matmul(PS[:, bs, :], krs[0:64, kb * 128:(kb + 1) * 128], qr[0:64, qtok], False, True,
                                 r=[krk, ("qr", tq)], w=[("ps", bs)])
                        pt = PT[u % 4]
                        P.act(pt[0], PS[:, bs, :], AF.Exp, scale=SCALE, r=[("ps", bs)], w=[pt[1]])
                        if (not pre) and kb >= 4 * tq:
                            P.tt("pool", pt[0], pt[0], masks[:, kb - 4 * tq, :], ALU.mult, r=[pt[1], "masks"], w=[pt[1]])
                    if pend is not None:
                        (ppre, pkb, ppt, pu) = pend
                        vb = pkb if ppre else 16 + pkb
                        P.matmul(PS[:, bo, :], Vh[:, vb, :], ppt[0], pu == 0, pu == n - 1, r=[("Vh", vb // 4), ppt[1]], w=[("ps", bo)])
                        lh = visones if ppre else ones[1][:]
                        P.matmul(PS[:, bd, :], lh, ppt[0], pu == 0, pu == n - 1,
                                 r=["visones" if ppre else ("ones", 1), ppt[1]], w=[("ps", bd)])
                    pend = (pre, kb, pt, u) if u < n else None
                rd = rden[ab]
                P.recip(rd[0], PS[:, bd, :], r=[("ps", bd)], w=[rd[1]])
                P.tt("dve", oT[:, h, qtok], PS[:, bo, :], rd[0], ALU.mult, r=[("ps", bo), rd[1]], w=[("oT", h, tq)])
        P.barrier()
        if os.environ.get("KDBG") in ("B", "B1", "B2"):
            return

        wo = carve(0, 8 * 1024, BF16).rearrange("p (c n) -> p c n", c=8)
        c_ = base
        mT = carve(c_, 8 * 512, F32).rearrange("p (k t) -> p k t", k=8); c_ += 4096
        sqC = []
        for i in range(4):
            sqC.append((carve(c_, 512, BF16), ("sqC", i))); c_ += 256
        rstdC = (carve(c_, 512, F32), "rstdC"); c_ += 512
        tC = []
        for i in range(3):
            tC.append((carve(c_, 512, F32), ("tC", i))); c_ += 512
        assert c_ <= ARENA_W
        P.dma("pool", out=wo, in_=mla_wo_d[s_].rearrange("p (c n) -> p c n", c=8), sem="wo", w=["wo"])
        gi = l * 4 + 1
        tcn = 0
        for tt in range(NT):
            tok = slice(tt * TT, (tt + 1) * TT)
            pend = []
            for oc in range(8):
                b = next_ps()
                for c in range(8):
                    P.matmul(PS[:, b, :], wo[:, c, oc * 128:(oc + 1) * 128], oT[:, c, tok], c == 0, c == 7,
                             r=["wo", ("oT", c, tt)], w=[("ps", b)])
                for (poc, ps_) in pend:
                    P.matmul(PS[:, 7, :], ones[1024][:], ps_[0], poc == 0, poc == 7, r=[ps_[1], ("ones", 1024)], w=[("ps", 7)])
                pend = []
                sq_ = sqC[oc % 4]
                P.act(mT[:, oc, :], PS[:, b, :], AF.Copy, r=[("ps", b)], w=[("mT", oc)])
                P.act(sq_[0], PS[:, b, :], AF.Square, r=[("ps", b)], w=[sq_[1]])
                pend.append((oc, sq_))
            for (poc, ps_) in pend:
                P.matmul(PS[:, 7, :], ones[1024][:], ps_[0], poc == 0, poc == 7, r=[ps_[1], ("ones", 1024)], w=[("ps", 7)])
            rstd_from(7, rstdC[0], rstdC[1])
            for oc in range(8):
                tm = tC[tcn % 3]; tcn += 1
                P.stt("dve", tm[0], mT[:, oc, :], gains[:, gi * 8 + oc:gi * 8 + oc + 1], rstdC[0], ALU.mult, ALU.mult,
                      r=[("mT", oc), "gains", rstdC[1]], w=[tm[1]])
                P.tt("pool", hT[:, oc, tok], hT[:, oc, tok], tm[0], ALU.add, r=[("h", oc, tt), tm[1]], w=[("h", oc, tt)])


    def hgrn_layer(l):
        s_ = l // 2
        P.barrier()
        HW = 1024
        o = 0
        aT = carve(o, 8 * T, BF16).rearrange("p (k t) -> p k t", k=8); o += 8192
        ogT = carve(o, 8 * T, BF16).rearrange("p (k t) -> p k t", k=8); o += 8192
        qf = carve(o, HW, F32); o += 1024
        kf = carve(o, HW, F32); o += 1024
        bb = carve(o, HW, F32); o += 1024
        X = carve(o, HW, F32); o += 1024
        X2 = carve(o, HW, F32); o += 1024
        mscan = carve(o, HW, F32); o += 1024
        qrel_off = o
        qrel = carve(o, HW, BF16); o += 512
        krel = carve(o, HW, BF16); o += 512
        qdec = carve(o, HW, BF16); o += 512
        kdec = carve(o, HW, BF16); o += 512
        mone = carve(o, HW, F32)
        vT = carve(o, HW, BF16); o += 512
        gate = carve(o, HW, BF16); o += 512
        Ws = []
        for i in range(4):
            Ws.append(carve(o, 8 * 128, BF16).rearrange("p (k c) -> p k c", k=8)); o += 512
        kdT = [carve(o + 256 * i, 512, BF16) for i in range(2)]; o += 512
        vch = [carve(o + 256 * i, 512, BF16) for i in range(2)]; o += 512
        ATs = [carve(o + 64 * i, 128, BF16) for i in range(2)]; o += 128
        NSF, NSB = 4, 12
        Sf = [carve(o + 128 * i, 128, F32) for i in range(NSF)]; o += 128 * NSF
        Sb = [carve(o + 64 * i, 128, BF16) for i in range(NSB)]; o += 64 * NSB
        dec = carve(o, 32, F32); o += 32
        dhalf = carve(o, 2, F32); o += 2
        o += 2
        S1f = carve(o, 8 * 128, F32).rearrange("p (h e) -> p h e", h=8); o += 1024
        S0 = carve(o, 8 * 128, F32).rearrange("p (h e) -> p h e", h=8); o += 1024
        osb = [(carve(o + 512 * i, 512, F32), ("osb", i)) for i in range(2)]; o += 1024
        sqH = [(carve(o + 256 * i, 512, BF16), ("sqH", i)) for i in range(4)]; o += 1024
        rstdH = [(carve(o + 512 * i, 512, F32), ("rstdH", i)) for i in range(2)]; o += 1024
        kfT = carve(qrel_off, 8 * 128, BF16).rearrange("p (b d) -> p b d", b=8)
        vtok = carve(qrel_off + 512, 8 * 128, BF16).rearrange("p (b e) -> p b e", b=8)
        assert o <= ARENA_W, o
        ident = cmask[:, 128:256]
        m32 = cmask[0:32, 0:128]
        gcol = 2 * s_
        lb_ap = lambda h: lbt[:, 8 * s_ + h:8 * s_ + h + 1]
        oml_ap = lambda h: lbt[:, 16 + 8 * s_ + h:16 + 8 * s_ + h + 1]
        noml_ap = lambda h: lbt[:, 32 + 8 * s_ + h:32 + 8 * s_ + h + 1]
        onorm_ap = hg_c[:, 32 + s_:33 + s_]

        P.memset("pool", mone, 1.0, w=["mone"])
        P.memset("pool", mscan, 1.0, w=["mscan"])
        P.memset("pool", mscan.rearrange("p (c t) -> p c t", t=32)[:, :, 0:1], 0.0, w=["mscan"])
        for tt in range(NT):
            norm_pre(tt, l * 4 + 0, aT[:, :, tt * TT:(tt + 1) * TT], ("aTg", tt), sqH, rstdH[tt % 2], 6 + tt % 2)

        wcnt = {"n": 0}

        def load_w(which, h):
            P.dma("pool", out=Ws[which], in_=hg_win_d[s_, which * 8 + h].rearrange("p (k c) -> p k c", k=8),
                  sem=f"hw{which}", w=[("hw", which)])

        def proj(which, hf, evac):
            for t2 in range(2):
                tt = hf * 2 + t2
                b = next_ps(0, 4)
                for k in range(KC):
                    P.matmul(PS[:, b, :], Ws[which][:, k, :], aT[:, k, tt * TT:(tt + 1) * TT], k == 0, k == KC - 1,
                             r=[("hw", which), (("aTg", tt), k)], w=[("ps", b)])
                evac(b, t2)

        def f_branch(h, hf):
            def ev(b, t2):
                P.act(X[:, t2 * 512:(t2 + 1) * 512], PS[:, b, :], AF.Sigmoid, r=[("ps", b)], w=[("X", t2)])
            proj(1, hf, ev)
            P.ts("dve", kf, X, noml_ap(h), oml_ap(h), ALU.mult, ALU.add, r=[("X", 0), ("X", 1), "noml", "oml"], w=["kf"])
            P.ts("dve", X, kf, -1.0, 1.0, ALU.mult, ALU.add, r=["kf"], w=[("X", 0), ("X", 1)])
            P.act(bb, X, AF.Ln, r=[("X", 0), ("X", 1)], w=["bb"])

        for h in range(8):
            load_w(1, h)
            load_w(2, h)
            for hf in range(2):
                f_branch(h, hf)
                P.scan("dve", X2, mone, bb, 0.0, ALU.mult, ALU.add, r=["mone", "bb"], w=["X2"])
                P.tt("dve", X, X2[:, HW - 1:HW].broadcast_to([128, HW]), X2, ALU.subtract, r=["X2"], w=[("X", 0), ("X", 1)])
                P.act(bb, X, AF.Exp, r=[("X", 0), ("X", 1)], w=["bb"])
                P.tt("pool", kdec, kf, bb, ALU.mult, r=["kf", "bb"], w=["kdec"])
                P.act(dhalf[:, 0:1], X2[:, HW - 1:HW], AF.Exp, r=["X2"], w=["dhalf"])
                for g in range(2):
                    b = next_ps(0, 4)
                    for bi in range(4):
                        blk = g * 4 + bi
                        t0 = hf * HW + blk * 128
                        for k in range(KC):
                            P.matmul(PS[:, b, bi * 128:(bi + 1) * 128], aT[:, k, t0:t0 + 128], Ws[2][:, k, :], k == 0, k == KC - 1,
                                     r=[("hw", 2), (("aTg", t0 // TT), k)], w=[("ps", b)])
                    P.copy("dve", vtok[:, g * 4:(g + 1) * 4, :], PS[:, b, :].rearrange("p (b e) -> p b e", b=4), r=[("ps", b)], w=[("vtok", g)])
                    b2 = next_ps(0, 4)
                    pb = PS[:, b2, :].bitcast(BF16)
                    for bi in range(4):
                        blk = g * 4 + bi
                        P.transpose(pb[:, bi * 128:(bi + 1) * 128], kdec[:, blk * 128:(blk + 1) * 128], ident, r=["kdec", "cmask"], w=[("ps", b2)])
                    P.copy("dve", kfT[:, g * 4:(g + 1) * 4, :], pb[:, 0:512].rearrange("p (b d) -> p b d", b=4), r=[("ps", b2)], w=[("kfT", g)])
                bS = next_ps(4, 6)
                for blk in range(8):
                    P.matmul(PS[:, bS, 0:128], kfT[:, blk, :], vtok[:, blk, :], blk == 0, blk == 7,
                             r=[("kfT", blk // 4), ("vtok", blk // 4)], w=[("ps", bS)])
                if hf == 0:
                    P.copy("dve", S1f[:, h, :], PS[:, bS, 0:128], r=[("ps", bS)], w=[("S1f", h)])
                else:
                    P.stt("dve", S1f[:, h, :], S1f[:, h, :], dhalf[:, 0:1], PS[:, bS, 0:128], ALU.mult, ALU.add,
                          r=[("S1f", h), "dhalf", ("ps", bS)], w=[("S1f", h)])
        li = s_
        P.dma("sp", out=sin_d[li], in_=S1f.rearrange("p h e -> p (h e)"), sem=f"si{li}", r=[("S1f", h) for h in range(8)], w=[("sin", li)])
        P.collective("AllGather", PAIRS, sin_d[li].opt(), sout_d[li].opt(), f"hcc{li}", r=[("sin", li)], w=[("sout", li)])
        P.dma("sp", out=S0.rearrange("p h e -> p (h e)"), in_=sout_d[li][0:128, :], sem=f"so{li}", r=[("sout", li)], w=["S0"])

        P.barrier()
        oc_ = 0
        sidx = 0
        for h in range(8):
            for w_ in range(4):
                load_w(w_, h)
            sidx += 1
            P.ts("dve", Sf[sidx % NSF], S0[:, h, :], ctab[:, 3:4], None, ALU.mult, None, r=["S0", "ctab"], w=[("Sf", sidx % NSF)])
            P.copy("act", Sb[sidx % NSB], Sf[sidx % NSF], r=[("Sf", sidx % NSF)], w=[("Sb", sidx % NSB)])
            for hf in range(2):
                def evq(b, t2):
                    P.act(qf[:, t2 * 512:(t2 + 1) * 512], PS[:, b, :], AF.Silu, r=[("ps", b)], w=[("qf", t2)])
                proj(0, hf, evq)
                def evg(b, t2):
                    P.act(gate[:, t2 * 512:(t2 + 1) * 512], PS[:, b, :], AF.Silu, r=[("ps", b)], w=[("gate", t2)])
                proj(3, hf, evg)
                def evv(b, t2):
                    P.copy("dve", vT[:, t2 * 512:(t2 + 1) * 512], PS[:, b, :], r=[("ps", b)], w=[("vT", t2)])
                proj(2, hf, evv)
                f_branch(h, hf)
                QF = [("qf", 0), ("qf", 1)]
                XK = [("X", 0), ("X", 1)]
                P.scan("dve", X2, mscan, bb, 0.0, ALU.mult, ALU.add, r=["mscan", "bb"], w=["X2"])
                b3 = X2.rearrange("p (c t) -> p c t", t=32)
                X3 = X.rearrange("p (c t) -> p c t", t=32)
                P.tt("dve", X3, b3, b3[:, :, 16:17].broadcast_to([128, 32, 32]), ALU.subtract, r=["X2"], w=XK)
                P.act(bb, X, AF.Exp, r=XK, w=["bb"])
                P.tt("pool", qrel, qf, bb, ALU.mult, r=QF + ["bb"], w=["qrel"])
                P.act(bb, X, AF.Exp, scale=-1.0, r=XK, w=["bb"])
                P.tt("dve", krel, kf, bb, ALU.mult, r=["kf", "bb"], w=["krel"])
                P.act(bb, X2, AF.Exp, r=["X2"], w=["bb"])
                P.tt("pool", qdec, qf, bb, ALU.mult, r=QF + ["bb"], w=["qdec"])
                P.tt("dve", X3, b3[:, :, 31:32].broadcast_to([128, 32, 32]), b3, ALU.subtract, r=["X2"], w=XK)
                P.act(bb, X, AF.Exp, r=XK, w=["bb"])
                P.tt("dve", kdec, kf, bb, ALU.mult, r=["kf", "bb"], w=["kdec"])
                P.act(dec.rearrange("p (c o) -> p c o", o=1), b3[:, :, 31:32], AF.Exp, r=["X2"], w=["dec"])
                s_base = sidx

                def emit_tau(g):
                    nonlocal sidx
                    sl = g % 2
                    bT = next_ps(0, 4)
                    pbk = PS[:, bT, :].bitcast(BF16)
                    for n in range(4):
                        c0 = (g * 4 + n) * 32
                        P.transpose(pbk[0:32, n * 128:(n + 1) * 128], kdec[:, c0:c0 + 32], ident, r=["kdec", "cmask"], w=[("ps", bT)])
                    P.copy("dve", kdT[sl][0:32, :], pbk[0:32, 0:512], r=[("ps", bT)], w=[("kdT", sl)])
                    bV = next_ps(0, 4)
                    pbv = PS[:, bV, :].bitcast(BF16)
                    for n in range(4):
                        c0 = (g * 4 + n) * 32
                        P.transpose(pbv[0:32, n * 128:(n + 1) * 128], vT[:, c0:c0 + 32], ident, r=[("vT", c0 // 512), "cmask"], w=[("ps", bV)])
                    P.copy("act", vch[sl][0:32, :], pbv[0:32, 0:512], r=[("ps", bV)], w=[("vch", sl)])
                    bA = next_ps(0, 4)
                    for n in range(4):
                        c0 = (g * 4 + n) * 32
                        P.matmul(PS[0:32, bA, n * 32:(n + 1) * 32], krel[:, c0:c0 + 32], qrel[:, c0:c0 + 32], True, True,
                                 r=["krel", "qrel"], w=[("ps", bA)])
                    P.tt("dve", ATs[sl][0:32, :], PS[0:32, bA, 0:128], m32, ALU.mult, r=[("ps", bA), "cmask"], w=[("AT", sl)])
                    bU = 6 + g % 2
                    for n in range(4):
                        P.matmul(PS[:, bU, n * 128:(n + 1) * 128], kdT[sl][0:32, n * 128:(n + 1) * 128], vch[sl][0:32, n * 128:(n + 1) * 128], True, True,
                                 r=[("kdT", sl), ("vch", sl)], w=[("ps", bU)])
                    for n in range(4):
                        ci = g * 4 + n
                        cur, nxt = sidx, sidx + 1
                        P.stt("dve", Sf[nxt % NSF], Sf[cur % NSF], dec[:, ci:ci + 1], PS[:, bU, n * 128:(n + 1) * 128], ALU.mult, ALU.add,
                              r=[("Sf", cur % NSF), "dec", ("ps", bU)], w=[("Sf", nxt % NSF)])
                        P.copy("act", Sb[nxt % NSB], Sf[nxt % NSF], r=[("Sf", nxt % NSF)], w=[("Sb", nxt % NSB)])
                        sidx = nxt

                def emit_o(g):
                    nonlocal oc_
                    sl = g % 2
                    bO = 4 + ((hf * 8 + g) // 4) % 2
                    for n in range(4):
                        c0 = (g * 4 + n) * 32
                        oc0 = ((g % 4) * 4 + n) * 32
                        sb = (s_base + g * 4 + n) % NSB
                        P.matmul(PS[:, bO, oc0:oc0 + 32], vch[sl][0:32, n * 128:(n + 1) * 128], ATs[sl][0:32, n * 32:(n + 1) * 32], True, False,
                                 r=[("vch", sl), ("AT", sl)], w=[("ps", bO)])
                        P.matmul(PS[:, bO, oc0:oc0 + 32], Sb[sb], qdec[:, c0:c0 + 32], False, True,
                                 r=[("Sb", sb), "qdec"], w=[("ps", bO)])
                    if g % 4 == 3:
                        t2 = g // 4
                        tt = hf * 2 + t2
                        ob = osb[oc_ % 2]; sq_ = sqH[oc_ % 4]; rs = rstdH[oc_ % 2]; oc_ += 1
                        P.act(ob[0], PS[:, bO, :], AF.Copy, r=[("ps", bO)], w=[ob[1]])
                        P.act(sq_[0], PS[:, bO, :], AF.Square, r=[("ps", bO)], w=[sq_[1]])
                        bN = next_ps(0, 4)
                        P.matmul(PS[:, bN, :], ones[128][:], sq_[0], True, True, r=[sq_[1], ("ones", 128)], w=[("ps", bN)])
                        rstd_from(bN, rs[0], rs[1])
                        P.stt("dve", ob[0], ob[0], onorm_ap, rs[0], ALU.mult, ALU.mult, r=[ob[1], "hg_c", rs[1]], w=[ob[1]])
                        P.tt("pool", ogT[:, h, tt * TT:(tt + 1) * TT], ob[0], gate[:, t2 * 512:(t2 + 1) * 512], ALU.mult,
                             r=[ob[1], ("gate", t2)], w=[("ogT", h, tt)])

                emit_tau(0)
                for g in range(8):
                    if g + 1 < 8:
                        emit_tau(g + 1)
                    emit_o(g)
        P.barrier()
        wo = carve(0, 8 * 1024, BF16).rearrange("p (c n) -> p c n", c=8)
        c_ = 16384
        mT = carve(c_, 8 * 512, F32).rearrange("p (k t) -> p k t", k=8); c_ += 4096
        sqC = []
        for i in range(4):
            sqC.append((carve(c_, 512, BF16), ("sqC", i))); c_ += 256
        rstdC = (carve(c_, 512, F32), "rstdC"); c_ += 512
        tC = []
        for i in range(3):
            tC.append((carve(c_, 512, F32), ("tC", i))); c_ += 512
        P.dma("pool", out=wo, in_=hg_wo_d[s_].rearrange("p (c n) -> p c n", c=8), sem="hwo", w=["hwo"])
        out_proj_post(l, wo, "hwo", ogT, "ogT", mT, sqC, rstdC, tC)

    for (kind, l) in plan:
        if kind == "mlp":
            mlp_layer(l)
        elif kind == "mla":
            mla_layer(l)
        elif kind == "hgrn":
            hgrn_layer(l)
        else:
            raise NotImplementedError(kind)

    P.barrier()
    for k in range(KC):
        P.dma("sp", out=outT_d[k * 128:(k + 1) * 128, :], in_=hT[:, k, :], sem="xout", r=[("h", k, t) for t in range(NT)])
    P.wait_all_dma("sp")
    P.emit()
    return nc


FULL_PLAN = []
for _l in range(DEPTH):
    FULL_PLAN.append(("mla" if _l % 2 == 0 else "hgrn", _l))
    FULL_PLAN.append(("mlp", _l))


def prep_weights(inp):
    f = np.float32
    out = {}
    ng = np.asarray(inp["norm_gains"], f)
    out["gains"] = np.ascontiguousarray(ng.reshape(16, 8, 128).transpose(2, 0, 1).reshape(128, 128))
    w1 = np.asarray(inp["mlp_w1"], f)
    out["w1r"] = np.ascontiguousarray(w1.reshape(4, 8, 128, 16, 256).transpose(0, 3, 2, 1, 4).reshape(4, 16, 128, 2048))
    w2 = np.asarray(inp["mlp_w2"], f)
    out["w2r"] = np.ascontiguousarray(w2.reshape(4, 32, 128, 8, 128).transpose(0, 3, 2, 1, 4).reshape(4, 8, 128, 4096))
    w_in = np.asarray(inp["mla_w_in"], f)
    kr = w_in[:, :, 768:832]
    kr_sw = np.concatenate([kr[:, :, 32:64], kr[:, :, 0:32]], axis=2)
    w_in_ext = np.concatenate([w_in, kr_sw], axis=2)
    out["mla_win"] = np.ascontiguousarray(w_in_ext.reshape(2, 8, 128, 896).transpose(0, 2, 1, 3).reshape(2, 128, 8 * 896))
    g = np.zeros((128, 16), f)
    qn_ = np.asarray(inp["mla_q_norm"], f)
    kvn_ = np.asarray(inp["mla_kv_norm"], f)
    for s_ in range(2):
        g[:, 8 * s_:8 * s_ + 4] = qn_[s_].reshape(4, 128).T
        g[:, 8 * s_ + 4:8 * s_ + 6] = kvn_[s_].reshape(2, 128).T
    out["mla_g"] = g
    w_uq = np.asarray(inp["mla_w_uq"], f).reshape(2, 4, 128, 8, 192)
    w_ukv = np.asarray(inp["mla_w_ukv"], f).reshape(2, 2, 128, 8, 256)
    uk = w_ukv[..., 0:128].transpose(0, 3, 2, 1, 4).reshape(2, 8, 128, 256)
    uv = w_ukv[..., 128:256].transpose(0, 3, 2, 1, 4).reshape(2, 8, 128, 256)
    uqn = w_uq[..., 0:128].transpose(0, 3, 2, 1, 4).reshape(2, 8, 128, 512)
    qr_ = w_uq[..., 128:192]
    uqr = qr_.transpose(0, 3, 2, 1, 4).reshape(2, 8, 128, 256)
    qrs_ = np.concatenate([qr_[..., 32:64], qr_[..., 0:32]], axis=-1)
    uqrs = qrs_.transpose(0, 3, 2, 1, 4).reshape(2, 8, 128, 256)
    out["mla_wh"] = np.ascontiguousarray(np.concatenate([uk, uv, uqn, uqr, uqrs], axis=3))
    w_o = np.asarray(inp["mla_w_o"], f)
    out["mla_wo"] = np.ascontiguousarray(w_o.reshape(2, 8, 128, 1024).transpose(0, 2, 1, 3).reshape(2, 128, 8192))
    hw = np.asarray(inp["hgrn_w_in"], f)
    out["hg_win"] = np.ascontiguousarray(hw.reshape(2, 8, 128, 32, 128).transpose(0, 3, 2, 1, 4).reshape(2, 32, 128, 1024))
    hwo = np.asarray(inp["hgrn_w_o"], f)
    out["hg_wo"] = np.ascontiguousarray(hwo.reshape(2, 8, 128, 1024).transpose(0, 2, 1, 3).reshape(2, 128, 8192))
    hc = np.zeros((128, 64), f)
    lbl = np.asarray(inp["hgrn_lb_logits"], f)
    hc[:, 0:32] = lbl.reshape(4, 8, 128).transpose(2, 0, 1).reshape(128, 32)
    hc[:, 32:34] = np.asarray(inp["hgrn_o_norm"], f).T
    out["hg_c"] = hc
    cm = np.zeros((128, 256), f)
    ss_ = np.arange(32)[:, None]
    cc_ = np.arange(32)[None, :]
    cm[0:32, 0:128] = np.tile((cc_ >= ss_).astype(f), (1, 4))
    cm[:, 128:256] = np.eye(128, dtype=f)
    out["cmask"] = cm
    kk = np.arange(128)[:, None, None]
    jj = np.arange(4)[None, :, None]
    qq = np.arange(512)[None, None, :]
    out["masks"] = np.ascontiguousarray((qq >= kk + 128 * jj).astype(f).reshape(128, 2048))
    return out


def make_ctab(j):
    ct = np.zeros((128, 16), np.float32)
    i = np.arange(128) % 32
    ct[:, 0] = ((10000.0 ** (-(2.0 * i) / 64.0)) / (2.0 * np.pi)).astype(np.float32)
    ct[:, 1] = 0.25
    ct[:, 2] = np.where((np.arange(128) % 64) < 32, 0.5, 0.0)
    ct[:, 3] = float(j)
    return ct


def run_plan(plan, x, inputs, wts=None):
    if wts is None:
        wts = prep_weights(inputs)
    nc = build_program(plan)
    in_maps = []
    for c in range(8):
        b, j = c // 2, c % 2
        m = dict(wts)
        m["xT"] = np.ascontiguousarray(np.asarray(x[b, j * T:(j + 1) * T, :], np.float32).T)
        m["ctab"] = make_ctab(j)
        m["posb"] = np.ascontiguousarray(np.broadcast_to(np.asarray(inputs["positions"])[b, j * T:(j + 1) * T].astype(np.int32)[None, :], (64, T)))
        in_maps.append(m)
    res = run_bass_kernel_spmd(nc, in_maps, core_ids=list(range(8)))
    out = np.empty((4, 4096, D), np.float32)
    for c in range(8):
        b, j = c // 2, c % 2
        out[b, j * T:(j + 1) * T, :] = res.results[c]["outT"].T
    return out


def kernel(**inputs):
    x = np.asarray(inputs["x"], np.float32)
    return run_plan(FULL_PLAN, x, inputs)
```
